# Optimizing a Trainium2 kernel written in Bass

```python
import math
import jax, jax.numpy as jnp
from jax import lax
import numpy as np

D_MODEL = 1024
BATCH = 4
SEQ = 4096
DEPTH = 2

HEAD_DIM = 64
N_MIXERS = 4
N_HEADS_TOTAL = D_MODEL // HEAD_DIM
HPM = N_HEADS_TOTAL // N_MIXERS
GW = HPM * HEAD_DIM
MIX_WIDTH = GW * N_MIXERS
DIFF_QK_DIM = HEAD_DIM // 2
D_FF = 2816
CONV_WIDTH = 3
NUM_BUCKETS = 32
MAX_EXACT = NUM_BUCKETS // 2
MAX_DISTANCE = 128
N_BIAS_HEADS = 2 * HPM
Q_BLOCK = 128
CMP_LEN = 32
CMP_STRIDE = 16
SEL_LEN = 64
N_SEL = 16
WINDOW = 512
FORCED_SCORE = 1.0e4
NEG = -1.0e30
EPS = 1e-6

SPLIT_SIZES = (
    GW, GW, GW,
    GW, GW, GW, HPM,
    GW, GW, GW,
    GW, HEAD_DIM, HEAD_DIM, HEAD_DIM, HEAD_DIM, HEAD_DIM, HEAD_DIM, 3 * HPM,
)
P_IN = sum(SPLIT_SIZES)

kernel_name = "hymba_style_hybrid_diff_fox_stickbreak_nsa"


def _split_points(sizes):
    pts, acc = [], 0
    for s in sizes[:-1]:
        acc += s
        pts.append(acc)
    return pts


def rmsnorm(x, g):
    xf = x.astype(jnp.float32)
    y = xf * lax.rsqrt(jnp.mean(xf * xf, axis=-1, keepdims=True) + EPS)
    return (y * g).astype(x.dtype)


def _heads(t, n_heads):
    b, s, _ = t.shape
    return t.reshape(b, s, n_heads, -1).transpose(0, 2, 1, 3)


def _merge_blocks(o):
    nb, b, h, qb, dv = o.shape
    return o.transpose(1, 0, 3, 2, 4).reshape(b, nb * qb, h * dv)


def _t5_bucket(dist):
    n = jnp.maximum(dist, 0)
    large = MAX_EXACT + (jnp.log(jnp.maximum(n, 1).astype(jnp.float32) / MAX_EXACT)
                         / math.log(MAX_DISTANCE / MAX_EXACT) * (NUM_BUCKETS - MAX_EXACT)).astype(jnp.int32)
    return jnp.where(n < MAX_EXACT, n, jnp.minimum(large, NUM_BUCKETS - 1))


def _t5_bias(table, dist):
    return jnp.moveaxis(table[_t5_bucket(dist)], -1, -3).astype(jnp.float32)


def diff_attention(q, k, v, g_q, g_k, lam_vec, sub_g, table, lam_init):
    q = rmsnorm(q, g_q)
    k = rmsnorm(k, g_k)
    lam_f = lam_vec.astype(jnp.float32)
    lam = jnp.exp(jnp.sum(lam_f[0] * lam_f[1])) - jnp.exp(jnp.sum(lam_f[2] * lam_f[3])) + lam_init
    scale = DIFF_QK_DIM ** -0.5
    s_len = q.shape[2]
    k_pos = jnp.arange(s_len)

    def block(i):
        qb = lax.dynamic_slice_in_dim(q, i * Q_BLOCK, Q_BLOCK, axis=2)
        q_pos = i * Q_BLOCK + jnp.arange(Q_BLOCK)
        dist = q_pos[:, None] - k_pos[None, :]
        s = jnp.einsum('bhqmd,bhkmd->bhmqk', qb, k).astype(jnp.float32) * scale
        s = s + _t5_bias(table, dist)[:, None]
        s = jnp.where(dist >= 0, s, NEG)
        p = jax.nn.softmax(s, axis=-1)
        w = p[:, :, 0] - lam * p[:, :, 1]
        return jnp.einsum('bhqk,bhkd->bhqd', w.astype(v.dtype), v)

    o = lax.map(block, jnp.arange(s_len // Q_BLOCK))
    o = rmsnorm(o, sub_g) * (1.0 - lam_init)
    return _merge_blocks(o)


def forgetting_attention(q, k, v, f_logit, g_q, g_k):
    q = rmsnorm(q, g_q)
    k = rmsnorm(k, g_k)
    log_f = jax.nn.log_sigmoid(f_logit.astype(jnp.float32)).transpose(0, 2, 1)
    cum = jnp.cumsum(log_f, axis=-1)
    scale = HEAD_DIM ** -0.5
    s_len = q.shape[2]
    k_pos = jnp.arange(s_len)

    def block(i):
        qb = lax.dynamic_slice_in_dim(q, i * Q_BLOCK, Q_BLOCK, axis=2)
        cq = lax.dynamic_slice_in_dim(cum, i * Q_BLOCK, Q_BLOCK, axis=2)
        q_pos = i * Q_BLOCK + jnp.arange(Q_BLOCK)
        dist = q_pos[:, None] - k_pos[None, :]
        s = jnp.einsum('bhqd,bhkd->bhqk', qb, k).astype(jnp.float32) * scale
        s = s + (cq[..., :, None] - cum[..., None, :])
        s = jnp.where(dist >= 0, s, NEG)
        p = jax.nn.softmax(s, axis=-1)
        return jnp.einsum('bhqk,bhkd->bhqd', p.astype(v.dtype), v)

    return _merge_blocks(lax.map(block, jnp.arange(s_len // Q_BLOCK)))


def stick_breaking_attention(q, k, v):
    scale = HEAD_DIM ** -0.5
    s_len = q.shape[2]
    k_pos = jnp.arange(s_len)

    def block(i):
        qb = lax.dynamic_slice_in_dim(q, i * Q_BLOCK, Q_BLOCK, axis=2)
        q_pos = i * Q_BLOCK + jnp.arange(Q_BLOCK)
        past = (q_pos[:, None] - k_pos[None, :]) > 0
        z = jnp.einsum('bhqd,bhkd->bhqk', qb, k).astype(jnp.float32) * scale
        log_beta = jax.nn.log_sigmoid(z)
        log_keep = jnp.where(past, jax.nn.log_sigmoid(-z), 0.0)
        tail = lax.cumsum(log_keep, axis=3, reverse=True) - log_keep
        a = jnp.where(past, jnp.exp(log_beta + tail), 0.0)
        return jnp.einsum('bhqk,bhkd->bhqd', a.astype(v.dtype), v)

    return _merge_blocks(lax.map(block, jnp.arange(s_len // Q_BLOCK)))


def native_sparse_attention(q, k_cmp, v_cmp, k_slc, v_slc, k_win, v_win, gate_logit,
                            g_q, g_k, pe, phi_w, table):
    b, h, s_len, d = q.shape
    scale = d ** -0.5
    q = rmsnorm(q, g_q)
    n_cmp = (s_len - CMP_LEN) // CMP_STRIDE + 1
    cmp_idx = np.arange(n_cmp)[:, None] * CMP_STRIDE + np.arange(CMP_LEN)[None, :]
    kc = (k_cmp[:, cmp_idx] + pe[0]).reshape(b, n_cmp, CMP_LEN * d) @ phi_w[0]
    vc = (v_cmp[:, cmp_idx] + pe[1]).reshape(b, n_cmp, CMP_LEN * d) @ phi_w[1]
    kc = rmsnorm(kc, g_k[0])
    ks = rmsnorm(k_slc, g_k[1])
    kw = rmsnorm(k_win, g_k[2])
    cmp_end = jnp.asarray(cmp_idx[:, -1])
    n_sel_blocks = s_len // SEL_LEN
    n_sel = min(N_SEL, n_sel_blocks)
    c_start = np.arange(n_cmp) * CMP_STRIDE
    s_start = np.arange(n_sel_blocks) * SEL_LEN
    overlap = jnp.asarray(((c_start[:, None] < s_start[None, :] + SEL_LEN)
                           & (s_start[None, :] < c_start[:, None] + CMP_LEN)).astype(np.float32))
    sel_start = jnp.asarray(s_start)
    blk_ids = jnp.arange(n_sel_blocks)
    kw_pad = jnp.pad(kw, ((0, 0), (WINDOW, 0), (0, 0)))
    vw_pad = jnp.pad(v_win, ((0, 0), (WINDOW, 0), (0, 0)))
    gates = jax.nn.sigmoid(gate_logit.astype(jnp.float32)).reshape(b, s_len, 3, h)
    gather = jax.vmap(lambda arr, idx: arr[idx])

    def block(i):
        q_pos = i * Q_BLOCK + jnp.arange(Q_BLOCK)
        qb = lax.dynamic_slice_in_dim(q, i * Q_BLOCK, Q_BLOCK, axis=2)
        dist_c = q_pos[:, None] - cmp_end[None, :]
        valid_c = dist_c >= 0
        s_c = jnp.einsum('bhqd,bnd->bhqn', qb, kc).astype(jnp.float32) * scale + _t5_bias(table, dist_c)
        p_c = jnp.where(valid_c, jax.nn.softmax(jnp.where(valid_c, s_c, NEG), axis=-1), 0.0)
        o_c = jnp.einsum('bhqn,bnd->bhqd', p_c.astype(vc.dtype), vc)
        imp = jnp.einsum('bhqn,nj->bqj', p_c, overlap)
        forced = (blk_ids[None, :] == (q_pos // SEL_LEN)[:, None]) | (blk_ids[None, :] == 0)
        imp = jnp.where(forced, FORCED_SCORE, imp)
        imp = jnp.where(sel_start[None, :] <= q_pos[:, None], imp, -1.0)
        _, sel = lax.top_k(imp, n_sel)
        tok = (sel[..., None] * SEL_LEN + jnp.arange(SEL_LEN)).reshape(b, Q_BLOCK, n_sel * SEL_LEN)
        ks_g = gather(ks, tok.reshape(b, -1)).reshape(b, Q_BLOCK, n_sel * SEL_LEN, d)
        vs_g = gather(v_slc, tok.reshape(b, -1)).reshape(b, Q_BLOCK, n_sel * SEL_LEN, d)
        dist_s = q_pos[None, :, None] - tok
        s_s = jnp.einsum('bhqd,bqkd->bhqk', qb, ks_g).astype(jnp.float32) * scale + _t5_bias(table, dist_s)
        s_s = jnp.where((dist_s >= 0)[:, None], s_s, NEG)
        p_s = jax.nn.softmax(s_s, axis=-1)
        o_s = jnp.einsum('bhqk,bqkd->bhqd', p_s.astype(vs_g.dtype), vs_g)
        kwb = lax.dynamic_slice_in_dim(kw_pad, i * Q_BLOCK, Q_BLOCK + WINDOW, axis=1)
        vwb = lax.dynamic_slice_in_dim(vw_pad, i * Q_BLOCK, Q_BLOCK + WINDOW, axis=1)
        k_pos_w = i * Q_BLOCK - WINDOW + jnp.arange(Q_BLOCK + WINDOW)
        dist_w = q_pos[:, None] - k_pos_w[None, :]
        valid_w = (dist_w >= 0) & (dist_w < WINDOW) & (k_pos_w >= 0)[None, :]
        s_w = jnp.einsum('bhqd,bkd->bhqk', qb, kwb).astype(jnp.float32) * scale + _t5_bias(table, dist_w)
        p_w = jax.nn.softmax(jnp.where(valid_w, s_w, NEG), axis=-1)
        o_w = jnp.einsum('bhqk,bkd->bhqd', p_w.astype(vwb.dtype), vwb)
        g = lax.dynamic_slice_in_dim(gates, i * Q_BLOCK, Q_BLOCK, axis=1).transpose(0, 2, 3, 1)
        o = g[:, 0, :, :, None] * o_c + g[:, 1, :, :, None] * o_s + g[:, 2, :, :, None] * o_w
        return o.astype(q.dtype)

    return _merge_blocks(lax.map(block, jnp.arange(s_len // Q_BLOCK)))


def conv_ffn(h, w_up, conv_w, conv_b, w_down):
    u = h @ w_up
    gate, val = jnp.split(u, 2, axis=-1)
    gate = lax.conv_general_dilated(gate, conv_w, window_strides=(1,), padding=[(CONV_WIDTH - 1, 0)],
                                    dimension_numbers=('NWC', 'WIO', 'NWC'),
                                    feature_group_count=D_FF) + conv_b
    return (jax.nn.silu(gate) * val) @ w_down


def setup_inputs(seed: int = 0) -> dict:
    key = jax.random.key(seed)
    ks = jax.random.split(key, 26)

    def nrm(i, shape, s):
        return jax.random.normal(ks[i], shape, jnp.float32) * s

    return {
        "x": nrm(0, (BATCH, SEQ, D_MODEL), 1.0),
        "c": nrm(1, (BATCH, D_MODEL), 1.0),
        "rel_bias": nrm(2, (NUM_BUCKETS, N_BIAS_HEADS), 0.5),
        "ada_w": nrm(3, (DEPTH, D_MODEL, 6 * D_MODEL), 0.5 * D_MODEL ** -0.5),
        "ada_b": nrm(4, (DEPTH, 6 * D_MODEL), 0.02),
        "norm_mix_g": 1.0 + nrm(5, (DEPTH, D_MODEL), 0.02),
        "norm_ffn_g": 1.0 + nrm(6, (DEPTH, D_MODEL), 0.02),
        "w_in": nrm(7, (DEPTH, D_MODEL, P_IN), D_MODEL ** -0.5),
        "w_out": nrm(8, (DEPTH, MIX_WIDTH, D_MODEL), MIX_WIDTH ** -0.5),
        "diff_qnorm_g": 1.0 + nrm(9, (DEPTH, DIFF_QK_DIM), 0.02),
        "diff_knorm_g": 1.0 + nrm(10, (DEPTH, DIFF_QK_DIM), 0.02),
        "diff_lambda": nrm(11, (DEPTH, 4, DIFF_QK_DIM), 0.1),
        "diff_subln_g": 1.0 + nrm(12, (DEPTH, 2 * DIFF_QK_DIM), 0.02),
        "fox_qnorm_g": 1.0 + nrm(13, (DEPTH, HEAD_DIM), 0.02),
        "fox_knorm_g": 1.0 + nrm(14, (DEPTH, HEAD_DIM), 0.02),
        "fox_b_f": 3.0 + nrm(15, (DEPTH, HPM), 0.1),
        "nsa_qnorm_g": 1.0 + nrm(16, (DEPTH, HEAD_DIM), 0.02),
        "nsa_knorm_g": 1.0 + nrm(17, (DEPTH, 3, HEAD_DIM), 0.02),
        "nsa_pe": nrm(18, (DEPTH, 2, CMP_LEN, HEAD_DIM), 0.5),
        "nsa_phi_w": nrm(19, (DEPTH, 2, CMP_LEN * HEAD_DIM, HEAD_DIM), (CMP_LEN * HEAD_DIM) ** -0.5),
        "ffn_w_up": nrm(20, (DEPTH, D_MODEL, 2 * D_FF), D_MODEL ** -0.5),
        "ffn_conv_w": nrm(21, (DEPTH, CONV_WIDTH, 1, D_FF), CONV_WIDTH ** -0.5),
        "ffn_conv_b": nrm(22, (DEPTH, D_FF), 0.02),
        "ffn_w_down": nrm(23, (DEPTH, D_FF, D_MODEL), D_FF ** -0.5),
    }


def reference(x, c, rel_bias, ada_w, ada_b, norm_mix_g, norm_ffn_g, w_in, w_out,
              diff_qnorm_g, diff_knorm_g, diff_lambda, diff_subln_g,
              fox_qnorm_g, fox_knorm_g, fox_b_f,
              nsa_qnorm_g, nsa_knorm_g, nsa_pe, nsa_phi_w,
              ffn_w_up, ffn_conv_w, ffn_conv_b, ffn_w_down):
    b, s_len, _ = x.shape
    table_diff = rel_bias[:, :HPM]
    table_nsa = rel_bias[:, HPM:]
    split_pts = _split_points(SPLIT_SIZES)
    c_act = jax.nn.silu(c)
    for l in range(DEPTH):
        mod = c_act @ ada_w[l] + ada_b[l]
        sh_a, sc_a, gt_a, sh_m, sc_m, gt_m = [m[:, None, :] for m in jnp.split(mod, 6, axis=-1)]
        h = rmsnorm(x, norm_mix_g[l]) * (1.0 + sc_a) + sh_a
        proj = h @ w_in[l]
        (a_q, a_k, a_v, b_q, b_k, b_v, b_f, c_q, c_k, c_v,
         d_q, d_kc, d_vc, d_ks, d_vs, d_kw, d_vw, d_g) = jnp.split(proj, split_pts, axis=-1)
        lam_init = 0.8 - 0.6 * math.exp(-0.3 * l)
        o_a = diff_attention(_heads(a_q, HPM).reshape(b, HPM, s_len, 2, DIFF_QK_DIM),
                             _heads(a_k, HPM).reshape(b, HPM, s_len, 2, DIFF_QK_DIM),
                             _heads(a_v, HPM), diff_qnorm_g[l], diff_knorm_g[l], diff_lambda[l],
                             diff_subln_g[l], table_diff, lam_init)
        o_b = forgetting_attention(_heads(b_q, HPM), _heads(b_k, HPM), _heads(b_v, HPM),
                                   b_f + fox_b_f[l], fox_qnorm_g[l], fox_knorm_g[l])
        o_c = stick_breaking_attention(_heads(c_q, HPM), _heads(c_k, HPM), _heads(c_v, HPM))
        o_d = native_sparse_attention(_heads(d_q, HPM), d_kc, d_vc, d_ks, d_vs, d_kw, d_vw, d_g,
                                      nsa_qnorm_g[l], nsa_knorm_g[l], nsa_pe[l], nsa_phi_w[l], table_nsa)
        mixed = jnp.concatenate([o_a, o_b, o_c, o_d], axis=-1) @ w_out[l]
        x = x + gt_a * mixed
        h = rmsnorm(x, norm_ffn_g[l]) * (1.0 + sc_m) + sh_m
        x = x + gt_m * conv_ffn(h, ffn_w_up[l], ffn_conv_w[l], ffn_conv_b[l], ffn_w_down[l])
    return x
```

```python
import math
import numpy as np
import concourse.bass as bass
import concourse.mybir as mybir
from concourse.bass_utils import run_bass_kernel_spmd

F32 = mybir.dt.float32
BF16 = mybir.dt.bfloat16
AF = mybir.ActivationFunctionType
ALU = mybir.AluOpType
AX = mybir.AxisListType

S = 4096
D = 1024
NT = 32
DFF = 2816
NFC = 22
PIN = 2960
BIG = 32768.0
EPS = 1e-6
NQ = 8


class Dep:
    __slots__ = ("w", "r")

    def __init__(self):
        self.w = {}
        self.r = {}


class TT:
    def __init__(self, t):
        self.t = t
        self.dep = Dep()

    def __getitem__(self, idx):
        return self.t[idx]


class Kern:
    def __init__(self, nc):
        self.nc = nc
        self.engs = {"pe": nc.tensor, "act": nc.scalar, "dve": nc.vector, "pool": nc.gpsimd, "sp": nc.sync}
        self.sem = {e: nc.semaphore("c_" + e).__enter__() for e in ["pe", "act", "dve", "pool"]}
        self.cnt = {e: 0 for e in self.sem}
        self.waited = {}
        self.dq = {}
        for q in ["sp", "pool", "act"]:
            self.dq[q] = {"sems": [nc.semaphore("d_%s%d" % (q, i)).__enter__() for i in range(NQ)],
                          "cnt": [0] * NQ, "i": 0}
        self.own = {e: id(self.sem[e]) for e in self.sem}
        self.pools = []

    def sb(self, name, shape, dt):
        return TT(self.nc.sbuf_tensor(name, shape, dt).__enter__())

    def ps(self, name, shape, dt):
        return TT(self.nc.psum_tensor(name, shape, dt).__enter__())

    def _wait(self, eng, sem, val):
        key = (eng, id(sem))
        if self.waited.get(key, 0) >= val:
            return
        self.waited[key] = val
        self.engs[eng].wait_ge(sem, val)

    def _deps(self, eng, reads, writes):
        own = self.own.get(eng)
        for d in reads:
            for sid, (sem, val) in d.dep.w.items():
                if sid == own and eng == "pe":
                    continue
                self._wait(eng, sem, val)
        for d in writes:
            for sid, (sem, val) in d.dep.w.items():
                if sid == own and eng == "pe":
                    continue
                self._wait(eng, sem, val)
            for sid, (sem, val) in d.dep.r.items():
                if sid == own and eng == "pe":
                    continue
                self._wait(eng, sem, val)

    def _reg(self, sem, val, reads, writes):
        sid = id(sem)
        for d in reads:
            d.dep.r[sid] = (sem, val)
        for d in writes:
            d.dep.w[sid] = (sem, val)

    def op(self, eng, fn, reads=(), writes=()):
        self._deps(eng, reads, writes)
        ins = fn(self.engs[eng])
        self.cnt[eng] += 1
        ins.then_inc(self.sem[eng], 1)
        self._reg(self.sem[eng], self.cnt[eng], reads, writes)

    def dma(self, q, out_ap, in_ap, reads=(), writes=()):
        self._deps(q, reads, writes)
        Q = self.dq[q]
        j = Q["i"] % NQ
        Q["i"] += 1
        sem = Q["sems"][j]
        if Q["cnt"][j] > 0:
            self._wait(q, sem, 16 * Q["cnt"][j])
        Q["cnt"][j] += 1
        self.engs[q].dma_start(out=out_ap, in_=in_ap).then_inc(sem, 16)
        self._reg(sem, 16 * Q["cnt"][j], reads, writes)

    def finish(self, outs):
        for d in outs:
            for sid, (sem, val) in d.dep.w.items():
                self._wait("sp", sem, val)

    def mm(self, out_tt, out_ap, lhsT, rhs, reads, start=True, stop=True):
        self.op("pe", lambda e: e.matmul(out_ap, lhsT, rhs, start=start, stop=stop, skip_group_check=True),
                reads=reads, writes=[out_tt])

    def act(self, out_tt, out_ap, in_ap, func, reads, bias=None, scale=None, accum=None, extra_w=()):
        kw = {}
        if bias is not None:
            kw["bias"] = bias
        if scale is not None:
            kw["scale"] = scale
        if accum is not None:
            kw["accum_out"] = accum
        self.op("act", lambda e: e.activation(out_ap, in_ap, func, **kw), reads=reads,
                writes=[out_tt] + list(extra_w))

    def ts(self, eng, out_tt, out_ap, in_ap, s1, s2, op0, op1, reads):
        if op1 is None:
            self.op(eng, lambda e: e.tensor_scalar(out=out_ap, in0=in_ap, scalar1=s1, scalar2=None, op0=op0),
                    reads=reads, writes=[out_tt])
        else:
            self.op(eng, lambda e: e.tensor_scalar(out=out_ap, in0=in_ap, scalar1=s1, scalar2=s2, op0=op0, op1=op1),
                    reads=reads, writes=[out_tt])

    def tt(self, eng, out_tt, out_ap, a, b, op, reads):
        self.op(eng, lambda e: e.tensor_tensor(out=out_ap, in0=a, in1=b, op=op), reads=reads, writes=[out_tt])

    def cp(self, eng, out_tt, out_ap, in_ap, reads):
        if eng == "act":
            self.op("act", lambda e: e.copy(out_ap, in_ap), reads=reads, writes=[out_tt])
        else:
            self.op(eng, lambda e: e.tensor_copy(out=out_ap, in_=in_ap), reads=reads, writes=[out_tt])


def bmid(a, m):
    return bass.AP(tensor=a.tensor, offset=a.offset, ap=[list(a.ap[0]), [0, m]] + [list(x) for x in a.ap[1:]])


def blast(a, n):
    return bass.AP(tensor=a.tensor, offset=a.offset, ap=[list(x) for x in a.ap] + [[0, n]])


def bc_ap(ap2d, nparts):
    return bass.AP(tensor=ap2d.tensor, offset=ap2d.offset, ap=[[0, nparts]] + [list(x) for x in ap2d.ap[1:]])


def _bucket(dist):
    n = np.maximum(dist, 0)
    nf = np.maximum(n, 1).astype(np.float32)
    large = 16 + (np.log(nf / np.float32(16)) / np.float32(math.log(128 / 16)) * np.float32(16)).astype(np.int32)
    return np.where(n < 16, n, np.minimum(large, 31))


CF = {}


def _mk_consts():
    p = np.arange(128)[:, None]
    j = np.arange(128)[None, :]
    cols = []

    def add(name, arr):
        CF[name] = (sum(a.shape[1] for a in cols), arr.shape[1])
        cols.append(arr.astype(np.float32))

    add("ident", (p == j))
    add("antiI", (p + j == 127))
    add("U", (p <= j))
    add("ones", np.ones((128, 128)))
    add("TRI", np.where(j >= p, 0.0, -BIG))
    add("TW", np.where(j < p, 0.0, -BIG))
    add("M01", (p + j > 127))
    add("NMC", np.where(p + j > 127, 0.0, -BIG))
    j2 = np.arange(256)[None, :]
    d = j2 - p
    r = np.arange(512)[None, :]
    dc = p - 16 * (r - 255) - 31
    global _CF2_ARR
    _CF2_ARR = np.ascontiguousarray(np.concatenate([np.where(d >= 0, _bucket(d), -1),
                                                    np.where(dc >= 0, _bucket(dc), -1)], axis=1).astype(np.float32))
    jj = np.arange(128)[None, :]
    fb = (jj == 64 + p // 64)
    vb = (jj <= 64 + p // 64)
    add("M1", (vb & ~fb))
    add("CC", 1.0e4 * fb - 1.0 * (~vb))
    cf = np.concatenate(cols, axis=1)
    ex = np.zeros((64, 32, 128), np.float32)
    for kb in range(32):
        for pp in range(128):
            ex[2 * kb + pp // 64, kb, pp] = 1.0
    return np.ascontiguousarray(cf), np.ascontiguousarray(ex.reshape(64, 4096))


_CF_ARR, _EX_ARR = _mk_consts()
NCF = _CF_ARR.shape[1]

_SEGS = [("aq", 0, 256), ("ak", 256, 256), ("bq", 768, 256), ("bk", 1024, 256),
         ("dq", 2308, 256), ("ks", 2692, 64), ("kw", 2820, 64),
         ("cq", 1540, 256), ("ck", 1796, 256),
         ("av", 512, 256), ("bv", 1280, 256),
         ("cv", 2052, 256), ("vs", 2756, 64), ("vw", 2884, 64), ("bf", 1536, 4), ("dg", 2948, 12),
         ("kc", 2564, 64), ("vc", 2628, 64)]
WOFF = {}
_o = 0
for _n, _s, _w in _SEGS:
    WOFF[_n] = _o
    _o += _w
assert _o == PIN


class Prog:
    def __init__(self, dbg=False, nlayers=2, stop_after=None, mixers="ABCD"):
        self.dbg = dbg
        self.mixers = mixers
        self.stop_after = stop_after
        nc = bass.Bass("TRN2", target_bir_lowering=False)
        self.nc = nc
        k = Kern(nc)
        self.k = k
        self.stack = []

        def din(name, shape):
            return nc.dram_tensor(name, list(shape), F32, kind="ExternalInput").ap()

        self.x_in = din("x", [S, D])
        self.c_in = din("c", [8, 128])
        self.I = {}
        for name, shape in [("rel_bias", [1, 256]), ("ada_w", [2, D, 6 * D]), ("ada_b", [2, 48, 128]),
                            ("norm_mix_g", [2, 8, 128]), ("norm_ffn_g", [2, 8, 128]), ("w_in", [2, D, PIN]),
                            ("w_out", [2, D, D]), ("diff_qnorm_g", [2, 1, 32]), ("diff_knorm_g", [2, 1, 32]),
                            ("diff_lambda", [2, 1, 128]), ("diff_subln_g", [2, 1, 64]), ("fox_qnorm_g", [2, 1, 64]),
                            ("fox_knorm_g", [2, 1, 64]), ("fox_b_f", [2, 1, 4]), ("nsa_qnorm_g", [2, 1, 64]),
                            ("nsa_knorm_g", [2, 1, 192]), ("nsa_pe", [2, 2, 32, 64]), ("nsa_phi_w", [2, 2, 2048, 64]),
                            ("ffn_w_up", [2, D, 2 * DFF]), ("ffn_conv_w", [2, 66, 128]), ("ffn_conv_b", [2, 22, 128]),
                            ("ffn_w_down", [2, DFF, D]), ("cf", [128, NCF]), ("cf2", [128, 768]), ("cex", [64, 4096])]:
            self.I[name] = din(name, shape)
        kind = "ExternalOutput" if dbg else "Internal"

        def dscr(name, shape, dt):
            return nc.dram_tensor(name, list(shape), dt, kind=kind).ap()

        self.y = nc.dram_tensor("y", [S, D], F32, kind="ExternalOutput").ap()
        self.y_dep = TT(None)
        self.xs = dscr("xs", [S, D], F32)
        self.xmid = dscr("xmid", [S, D], F32)
        self.qkT = dscr("qkT", [19, 128, S], BF16)
        self.vtm = dscr("vtm", [S, 650], BF16)
        self.cvr = dscr("cvr", [S, 256], BF16)
        self.baug = dscr("baug", [4, 128, S], BF16)
        self.o_d = dscr("o_d", [S, D], BF16)
        self.tts = dscr("tts", [128, 2 * 8 * 256], BF16)
        self.brs = dscr("brs", [128, 4 * 512], F32)
        self.dd = {n: TT(None) for n in ["xs", "xmid", "qkT", "vtm", "cvr", "baug", "o_d", "tts", "brs"]}
        self.ps = [k.ps("ps%d" % i, [128, 512], F32) for i in range(8)]
        self.setup()
        for l in range(nlayers if stop_after != ("setup", 0) else 0):
            xsrc, xsd = (self.x_in, None) if l == 0 else (self.xs, self.dd["xs"])
            self.phase_a(l, xsrc, xsd)
            if stop_after == ("a", l):
                break
            self.phase_b(l)
            if stop_after == ("b", l):
                break
            last = (l == nlayers - 1)
            self.phase_c(l, xsrc, xsd, self.y if last else self.xs, self.y_dep if last else self.dd["xs"])
        outs = [self.y_dep] + (list(self.dd.values()) if dbg else [])
        self.barrier()
        k.finish(outs)

    def push(self):
        self.stack.append([])

    def sb(self, name, shape, dt):
        cm = self.nc.sbuf_tensor(name + "_%d" % len(self.k.pools), shape, dt)
        self.k.pools.append(0)
        t = TT(cm.__enter__())
        self.stack[-1].append(cm)
        return t

    def pop(self):
        self.barrier()
        for cm in reversed(self.stack.pop()):
            cm.__exit__(None, None, None)

    def barrier(self):
        k = self.k
        for e in ["pe", "act", "dve", "pool", "sp"]:
            for e2 in k.sem:
                if k.cnt[e2] > 0 and e2 != e:
                    k._wait(e, k.sem[e2], k.cnt[e2])
            for q in k.dq.values():
                for j in range(NQ):
                    if q["cnt"][j] > 0:
                        k._wait(e, q["sems"][j], 16 * q["cnt"][j])

    def cfs(self, name, a=0, b=None):
        o, w = CF[name]
        if b is None:
            b = w
        return self.cf[:, o + a:o + b]

    def tr(self, ps_tt, out_ap, in_ap, ident_ap, reads):
        self.k.mm(ps_tt, out_ap, in_ap, ident_ap, reads)

    def setup(self):
        k, nc, I = self.k, self.nc, self.I
        self.stack.append([])
        self.cf = self.sb("cf", [128, NCF], F32)
        k.dma("sp", self.cf[:, :], I["cf"][:, :], writes=[self.cf])
        cf = self.cf
        self.identb = self.sb("identb", [128, 128], BF16)
        self.antib = self.sb("antib", [128, 128], BF16)
        self.trib = self.sb("trib", [128, 128], BF16)
        self.twb = self.sb("twb", [128, 128], BF16)
        for t, n in [(self.identb, "ident"), (self.antib, "antiI"), (self.trib, "TRI"), (self.twb, "TW")]:
            k.cp("dve", t, t[:, :], self.cfs(n), [cf])
        self.zerob = self.sb("zerob", [128, 512], BF16)
        k.op("dve", lambda e: e.memset(self.zerob[:, :], 0.0), writes=[self.zerob])
        self.tab = self.sb("tab", [128, 256], F32)
        k.dma("sp", self.tab[:, :], bc_ap(I["rel_bias"], 128), writes=[self.tab])
        self.modT = self.sb("modT", [128, 96], F32)
        self.colv = self.sb("colv", [128, 128], F32)
        self.gates = self.sb("gates", [128, 32, 12], F32)
        self.kcT = self.sb("kcT", [128, 256], BF16)
        self.vcs = self.sb("vcs", [128, 2, 64], BF16)
        self.lam = self.sb("lam", [128, 2], F32)
        self.subg = self.sb("subg", [128, 64], F32)
        self.epsc = self.sb("epsc", [128, 4], F32)
        k.op("dve", lambda e: e.memset(self.epsc[:, 0:1], EPS), writes=[self.epsc])
        k.op("dve", lambda e: e.memset(self.epsc[:, 1:2], 1.0), writes=[self.epsc])
        k.op("dve", lambda e: e.memset(self.epsc[:, 2:3], 1e-30), writes=[self.epsc])
        self.push()
        tabd = self.sb("tabd", [128, 256], F32)
        k.tt("dve", tabd, tabd[:, :].rearrange("p (b h) -> p b h", h=8),
             self.tab[:, :].rearrange("p (b h) -> p b h", h=8), bmid(self.tab[:, 248:256], 32),
             ALU.subtract, [self.tab])
        acc = self.sb("ttacc", [128, 8, 256], F32)
        accb = self.sb("bracc", [128, 4, 512], F32)
        mb = self.sb("mb", [128, 512], F32)
        cf2 = self.sb("cf2", [128, 768], F32)
        k.dma("sp", cf2[:, :], I["cf2"][:, :], writes=[cf2])
        BKa = cf2[:, 0:256]
        BKC = cf2[:, 256:768]
        for h in range(8):
            k.ts("dve", acc, acc[:, h, :], BKa, -1.0, -BIG, ALU.is_equal, ALU.mult, [cf2])
        for h in range(4):
            k.ts("dve", accb, accb[:, h, :], BKC, -1.0, -BIG, ALU.is_equal, ALU.mult, [cf2])
        for b in range(32):
            k.ts("dve", mb, mb[:, 0:256], BKa, float(b), None, ALU.is_equal, None, [cf2])
            for h in range(8):
                k.op("dve", lambda e, h=h, b=b: e.scalar_tensor_tensor(
                    out=acc[:, h, :], in0=mb[:, 0:256], scalar=tabd[:, b * 8 + h:b * 8 + h + 1], in1=acc[:, h, :],
                    op0=ALU.mult, op1=ALU.add), reads=[mb, tabd], writes=[acc])
            k.ts("dve", mb, mb[:, :], BKC, float(b), None, ALU.is_equal, None, [cf2])
            for h in range(4):
                k.op("dve", lambda e, h=h, b=b: e.scalar_tensor_tensor(
                    out=accb[:, h, :], in0=mb[:, :], scalar=self.tab[:, b * 8 + 4 + h:b * 8 + 5 + h], in1=accb[:, h, :],
                    op0=ALU.mult, op1=ALU.add), reads=[mb, self.tab], writes=[accb])
        tthi = self.sb("tthi", [128, 8 * 256], BF16)
        ttlo = self.sb("ttlo", [128, 8 * 256], BF16)
        accf = acc[:, :, :].rearrange("p h j -> p (h j)")
        k.cp("dve", tthi, tthi[:, :], accf, [acc])
        k.tt("dve", acc, accf, accf, tthi[:, :], ALU.subtract, [tthi])
        k.cp("dve", ttlo, ttlo[:, :], accf, [acc])
        k.dma("pool", self.tts[:, 0:2048], tthi[:, :], reads=[tthi], writes=[self.dd["tts"]])
        k.dma("pool", self.tts[:, 2048:4096], ttlo[:, :], reads=[ttlo], writes=[self.dd["tts"]])
        k.dma("pool", self.brs[:, :], accb[:, :, :].rearrange("p h j -> p (h j)"), reads=[accb], writes=[self.dd["brs"]])
        st = self.sb("colst", [128, 128], F32)
        k.dma("sp", st[0:48, :], I["ada_b"][0], writes=[st])
        k.dma("sp", st[48:96, :], I["ada_b"][1], writes=[st])
        k.dma("sp", st[96:104, :], I["norm_mix_g"][0], writes=[st])
        k.dma("sp", st[104:112, :], I["norm_mix_g"][1], writes=[st])
        k.dma("sp", st[112:120, :], I["norm_ffn_g"][0], writes=[st])
        k.dma("sp", st[120:128, :], I["norm_ffn_g"][1], writes=[st])
        ps0 = self.ps[0]
        self.tr(ps0, ps0[:, 0:128], st[:, :], self.cfs("ident"), [st, cf])
        k.cp("dve", self.colv, self.colv[:, :], ps0[:, 0:128], [ps0])
        c8 = self.sb("c8", [8, 128], F32)
        k.dma("sp", c8[:, :], self.c_in[:, :], writes=[c8])
        ps1 = self.ps[1]
        self.tr(ps1, ps1[:, 0:8], c8[:, :], self.cfs("ident")[0:8, 0:8], [c8, cf])
        cact = self.sb("cact", [128, 8], F32)
        k.act(cact, cact[:, :], ps1[:, 0:8], AF.Silu, [ps1])
        wb = [self.sb("adaw%d" % i, [128, 8, 512], F32) for i in range(3)]
        ps2 = self.ps[2]
        for l in range(2):
            for blk in range(12):
                w = wb[(l * 12 + blk) % 3]
                src = I["ada_w"][l].rearrange("(kc p) n -> p kc n", p=128)
                for kc in range(8):
                    k.dma(["sp", "act", "pool"][kc % 3], w[:, kc, :], src[:, kc, blk * 512:(blk + 1) * 512], writes=[w])
                for jj in range(4):
                    j = l * 48 + blk * 4 + jj
                    for kc in range(8):
                        k.mm(ps2, ps2[:, j:j + 1], w[:, kc, jj * 128:(jj + 1) * 128], cact[:, kc:kc + 1], [w, cact],
                             start=(kc == 0), stop=(kc == 7))
        k.tt("dve", self.modT, self.modT[:, :], ps2[:, 0:96], self.colv[:, 0:96], ALU.add, [ps2, self.colv])
        self.pop()

    def modcol(self, l, which):
        o = l * 48 + which * 8
        return self.modT[:, o:o + 8]

    def phase_a(self, l, xsrc, xsd):
        k, I, cf = self.k, self.I, self.cf
        ps = self.ps
        ident = self.cfs("ident")
        self.push()
        gsm = self.sb("gsm", [128, 644], F32)
        go = {}
        o = 0
        for name, n in [("diff_qnorm_g", 32), ("diff_knorm_g", 32), ("fox_qnorm_g", 64), ("fox_knorm_g", 64),
                        ("nsa_qnorm_g", 64), ("nsa_knorm_g", 192), ("fox_b_f", 4), ("diff_subln_g", 64),
                        ("diff_lambda", 128)]:
            k.dma("sp", gsm[:, o:o + n], bc_ap(I[name][l], 128), writes=[gsm])
            go[name] = o
            o += n

        def gs_(name, a, b):
            return gsm[:, go[name] + a:go[name] + b]
        gA = self.sb("gA", [128, 512], F32)
        gB = self.sb("gB", [128, 512], F32)
        gD = self.sb("gD", [128, 384], F32)
        k.ts("dve", gA, gA[:, 0:256].rearrange("p (g e) -> p g e", e=32), bmid(gs_("diff_qnorm_g", 0, 32), 8),
             32 ** -0.5, None, ALU.mult, None, [gsm])
        k.cp("dve", gA, gA[:, 256:512].rearrange("p (g e) -> p g e", e=32), bmid(gs_("diff_knorm_g", 0, 32), 8), [gsm])
        k.ts("dve", gB, gB[:, 0:256].rearrange("p (g e) -> p g e", e=64), bmid(gs_("fox_qnorm_g", 0, 64), 4),
             0.125, None, ALU.mult, None, [gsm])
        k.cp("dve", gB, gB[:, 256:512].rearrange("p (g e) -> p g e", e=64), bmid(gs_("fox_knorm_g", 0, 64), 4), [gsm])
        k.ts("dve", gD, gD[:, 0:256].rearrange("p (g e) -> p g e", e=64), bmid(gs_("nsa_qnorm_g", 0, 64), 4),
             0.125, None, ALU.mult, None, [gsm])
        k.cp("dve", gD, gD[:, 256:384], gs_("nsa_knorm_g", 64, 192), [gsm])
        lam_init = 0.8 - 0.6 * math.exp(-0.3 * l)
        self.lam_init = lam_init
        subg = self.sb("subgl", [128, 64], F32)
        k.ts("dve", subg, subg[:, :], gs_("diff_subln_g", 0, 64), 1.0 - lam_init, None, ALU.mult, None, [gsm])
        lt = self.sb("lt", [128, 64], F32)
        ls = self.sb("ls", [128, 4], F32)
        lv = gs_("diff_lambda", 0, 128).rearrange("p (a b e) -> p a b e", a=2, b=2, e=32)
        k.tt("dve", lt, lt[:, :].rearrange("p (a e) -> p a e", e=32), lv[:, :, 0, :], lv[:, :, 1, :], ALU.mult, [gsm])
        k.op("dve", lambda e: e.tensor_reduce(out=ls[:, 0:2], in_=lt[:, :].rearrange("p (a e) -> p a e", e=32),
                                              axis=AX.X, op=ALU.add), reads=[lt], writes=[ls])
        k.act(ls, ls[:, 2:4], ls[:, 0:2], AF.Exp, [ls])
        k.tt("dve", ls, ls[:, 0:1], ls[:, 2:3], ls[:, 3:4], ALU.subtract, [ls])
        k.ts("dve", self.lam, self.lam[:, l:l + 1], ls[:, 0:1], lam_init, None, ALU.add, None, [ls])
        ab = self.sb("ab", [128, 16], F32)
        k.ts("dve", ab, ab[:, 0:8], self.modcol(l, 1), 1.0, None, ALU.add, None, [self.modT])
        k.tt("dve", ab, ab[:, 0:8], ab[:, 0:8], self.colv[:, 96 + 8 * l:104 + 8 * l], ALU.mult, [self.colv])
        k.cp("dve", ab, ab[:, 8:16], self.modcol(l, 0), [self.modT])
        sq = self.sb("sq", [128, D], F32)
        qn = self.sb("qn", [128, 512], F32)
        ssg = self.sb("ssg", [128, 48], F32)
        rawT = self.sb("rawT", [128, S + 16], F32)
        fl = self.sb("fl", [128, 32, 4], F32)
        gl = self.sb("gl", [128, 32, 12], F32)
        self.push()
        w = self.sb("win", [128, 8, PIN], BF16)
        src = I["w_in"][l].rearrange("(kc p) n -> p kc n", p=128)
        for name, s0, wd in _SEGS:
            o = WOFF[name]
            k.dma("pool", w[:, :, o:o + wd], src[:, :, s0:s0 + wd], writes=[w])
        xts = [self.sb("xt%d" % i, [128, D], F32) for i in range(2)]
        xn = self.sb("xn", [128, D], F32)
        sss = [self.sb("ss%d" % i, [128, 4], F32) for i in range(2)]
        hTs = [self.sb("hT%d" % i, [128, 8, 128], BF16) for i in range(2)]
        stgs = [self.sb("stg%d" % i, [128, 18 * 128], BF16) for i in range(2)]
        qsts = [self.sb("qst%d" % i, [128, 18, 512], BF16) for i in range(2)]
        vsts = [self.sb("vst%d" % i, [128, 650], BF16) for i in range(2)]
        cvb = self.sb("cvb", [128, 256], BF16)
        cvst = [self.sb("cvst%d" % i, [128, 256], BF16) for i in range(2)]
        for s_ in stgs:
            k.op("pool", lambda e, s_=s_: e.memset(s_[:, :], 0.0), writes=[s_])
        for v_ in vsts:
            k.op("pool", lambda e, v_=v_: e.memset(v_[:, :], 1.0), writes=[v_])
        k.op("pool", lambda e: e.memset(rawT[:, S:S + 16], 0.0), writes=[rawT])
        qkTv = self.qkT.rearrange("c p s -> p c s")
        qd = self.dd["qkT"]

        def normev(bank, n, gs):
            ng = n // gs
            k.act(sq, sq[:, 0:n], bank[:, 0:n], AF.Square, [bank])
            k.op("dve", lambda e: e.tensor_reduce(out=ssg[:, 0:ng], in_=sq[:, 0:n].rearrange("p (g e) -> p g e", e=gs),
                                                  axis=AX.X, op=ALU.add), reads=[sq], writes=[ssg])
            k.act(ssg, ssg[:, 16:16 + ng], ssg[:, 0:ng], AF.Ln, [ssg], bias=self.epsc[:, 0:1], scale=1.0 / gs)
            k.act(ssg, ssg[:, 32:32 + ng], ssg[:, 16:16 + ng], AF.Exp, [ssg], scale=-0.5)
            k.tt("dve", qn, qn[:, 0:n].rearrange("p (g e) -> p g e", e=gs),
                 bank[:, 0:n].rearrange("p (g e) -> p g e", e=gs), blast(ssg[:, 32:32 + ng], gs), ALU.mult, [bank, ssg])

        def front(t):
            xt = xts[t % 2]
            ss = sss[t % 2]
            hT = hTs[t % 2]
            stg = stgs[t % 2]
            vst = vsts[t % 2]
            qst = qsts[(t // 4) % 2]
            k.dma("sp", xt[:, :], xsrc[t * 128:(t + 1) * 128, :], reads=[xsd] if xsd else [], writes=[xt])
            k.act(sq, sq[:, :], xt[:, :], AF.Square, [xt], accum=ss[:, 0:1], extra_w=[ss])
            k.act(ss, ss[:, 1:2], ss[:, 0:1], AF.Ln, [ss], bias=self.epsc[:, 0:1], scale=1.0 / D)
            k.act(ss, ss[:, 2:3], ss[:, 1:2], AF.Exp, [ss], scale=-0.5)
            k.ts("dve", xn, xn[:, :], xt[:, :], ss[:, 2:3], None, ALU.mult, None, [xt, ss])
            for c in range(8):
                b = ps[c // 4]
                self.tr(b, b[:, (c % 4) * 128:(c % 4 + 1) * 128], xn[:, c * 128:(c + 1) * 128], ident, [xn, cf])
            for c in range(8):
                b = ps[c // 4]
                pin = b[:, (c % 4) * 128:(c % 4 + 1) * 128]
                if c % 2 == 0:
                    k.ts("dve", hT, hT[:, c, :], pin, ab[:, c:c + 1], ab[:, 8 + c:9 + c], ALU.mult, ALU.add, [b, ab])
                else:
                    k.act(hT, hT[:, c, :], pin, AF.Identity, [b, ab], bias=ab[:, 8 + c:9 + c], scale=ab[:, c:c + 1])

            def proj(bank, a, n):
                for c in range(8):
                    k.mm(bank, bank[:, 0:n], hT[:, c, :], w[:, c, a:a + n], [hT, w], start=(c == 0), stop=(c == 7))
            proj(ps[2], 0, 512)
            normev(ps[2], 512, 32)
            for m in range(2):
                srcq = qn[:, 0:256].rearrange("p (hh hl m e) -> p hh hl m e", hh=2, hl=2, m=2)[:, :, :, m, :]
                gq = gA[:, 0:256].rearrange("p (hh hl m e) -> p hh hl m e", hh=2, hl=2, m=2)[:, :, :, m, :]
                dst = stg[:, 0:512].rearrange("p (hh m hl f) -> p hh m hl f", hh=2, m=2, hl=2)[:, :, m, :, 32 * m:32 * m + 32]
                k.tt("pool", stg, dst, srcq, gq, ALU.mult, [qn, gA])
            k.tt("pool", stg, stg[:, 512:768], qn[:, 256:512], gA[:, 256:512], ALU.mult, [qn, gA])
            proj(ps[3], 512, 512)
            normev(ps[3], 512, 64)
            k.tt("pool", stg, stg[:, 768:1280], qn[:, 0:512], gB[:, :], ALU.mult, [qn, gB])
            proj(ps[4], 1024, 384)
            normev(ps[4], 384, 64)
            k.tt("pool", stg, stg[:, 1792:2048], qn[:, 0:256], gD[:, 0:256], ALU.mult, [qn, gD])
            for dup in range(2):
                k.tt("pool", stg, stg[:, 2048 + dup * 64:2112 + dup * 64], qn[:, 256:320], gD[:, 256:320], ALU.mult, [qn, gD])
                k.tt("pool", stg, stg[:, 2176 + dup * 64:2240 + dup * 64], qn[:, 320:384], gD[:, 320:384], ALU.mult, [qn, gD])
            proj(ps[5], 1408, 512)
            k.act(stg, stg[:, 1280:1536], ps[5][:, 0:256], AF.Copy, [ps[5]], scale=0.125)
            k.cp("act", stg, stg[:, 1536:1792], ps[5][:, 256:512], [ps[5]])
            proj(ps[2], 1920, 512)
            k.cp("act", vst, vst[:, 0:520].rearrange("p (h e) -> p h e", e=65)[:, :, 0:64],
                 ps[2][:, 0:512].rearrange("p (h e) -> p h e", e=64), [ps[2]])
            proj(ps[3], 2432, 400)
            k.cp("dve", cvb, cvb[:, :], ps[3][:, 0:256], [ps[3]])
            k.cp("dve", vst, vst[:, 520:584], ps[3][:, 256:320], [ps[3]])
            k.cp("dve", vst, vst[:, 585:649], ps[3][:, 320:384], [ps[3]])
            k.cp("dve", fl, fl[:, t, :], ps[3][:, 384:388], [ps[3]])
            k.cp("dve", gl, gl[:, t, :], ps[3][:, 388:400], [ps[3]])
            k.dma("pool", self.vtm[t * 128:(t + 1) * 128, :], vst[:, :], reads=[vst], writes=[self.dd["vtm"]])
            k.mm(ps[4], ps[4][:, 0:256], self.antib[:, :], cvb[:, :], [cvb, self.antib])
            cvs = cvst[t % 2]
            k.cp("act", cvs, cvs[:, :], ps[4][:, 0:256], [ps[4]])
            k.dma("pool", self.cvr[(31 - t) * 128:(32 - t) * 128, :], cvs[:, :], reads=[cvs], writes=[self.dd["cvr"]])
            for c in range(8):
                k.mm(ps[5], ps[5][:, 0:128], w[:, c, 2832:2960], hT[:, c, :], [hT, w], start=(c == 0), stop=(c == 7))
            k.cp("act", rawT, rawT[:, t * 128:(t + 1) * 128], ps[5][:, 0:128], [ps[5]])
        def back(t):
            stg = stgs[t % 2]
            qst = qsts[(t // 4) % 2]
            for g4 in range(5):
                chs = list(range(g4 * 4, min(g4 * 4 + 4, 18)))
                bank = [ps[6], ps[7]][(t * 5 + g4) % 2]
                for ci, ch in enumerate(chs):
                    idn = self.antib if ch in (12, 13) else self.identb
                    k.mm(bank, bank[:, ci * 128:(ci + 1) * 128], stg[:, ch * 128:(ch + 1) * 128], idn[:, :], [stg, idn])
                n = len(chs)
                for ci, ch in enumerate(chs):
                    slot = (3 - t % 4) if ch in (12, 13) else (t % 4)
                    eng = "act" if ci % 2 == 0 else "dve"
                    k.cp(eng, qst, qst[:, ch, slot * 128:(slot + 1) * 128], bank[:, ci * 128:(ci + 1) * 128], [bank])
            if t % 4 == 3:
                g = t // 4
                k.dma("pool", qkTv[:, 0:12, g * 512:(g + 1) * 512], qst[:, 0:12, :], reads=[qst], writes=[qd])
                k.dma("pool", qkTv[:, 12:14, (7 - g) * 512:(8 - g) * 512], qst[:, 12:14, :], reads=[qst], writes=[qd])
                k.dma("pool", qkTv[:, 14:18, g * 512:(g + 1) * 512], qst[:, 14:18, :], reads=[qst], writes=[qd])

        for t in range(NT + 1):
            if t < NT:
                front(t)
            if t >= 1:
                back(t - 1)
        self.pop()
        glf = gl[:, :, :].rearrange("p t g -> p (t g)")
        gaf = self.gates[:, :, :].rearrange("p t g -> p (t g)")
        k.act(self.gates, gaf, glf, AF.Exp, [gl], scale=-1.0)
        k.ts("dve", self.gates, gaf, gaf, 1.0, None, ALU.add, None, [self.gates])
        k.op("dve", lambda e: e.reciprocal(out=gaf, in_=gaf), reads=[self.gates], writes=[self.gates])
        sp = self.sb("fsp", [128, 128], F32)
        spv = sp[:, :].rearrange("p (t h) -> p t h", h=4)
        k.tt("dve", sp, spv, fl[:, :, :], bmid(gs_("fox_b_f", 0, 4), 32), ALU.add, [fl, gsm])
        k.act(sp, sp[:, :], sp[:, :], AF.Exp, [sp], scale=-1.0)
        k.act(sp, sp[:, :], sp[:, :], AF.Ln, [sp], bias=self.epsc[:, 1:2])
        k.mm(ps[0], ps[0][:, 0:128], self.cfs("U"), sp[:, :], [sp, cf])
        k.mm(ps[1], ps[1][:, 0:128], self.cfs("ones"), sp[:, :], [sp, cf])
        tot = self.sb("ftot", [128, 128], F32)
        inc = self.sb("finc", [128, 128], F32)
        cn = self.sb("fcn", [128, 128], F32)
        k.cp("dve", tot, tot[:, :], ps[1][:, 0:128], [ps[1]])
        totv = tot[:, :].rearrange("p (t h) -> p t h", h=4)
        incv = inc[:, :].rearrange("p (t h) -> p t h", h=4)
        for h in range(4):
            k.op("dve", lambda e, h=h: e.tensor_tensor_scan(out=incv[:, :, h], data0=totv[:, :, h],
                                                            data1=self.zerob[:, 0:32], initial=0.0,
                                                            op0=ALU.add, op1=ALU.add), reads=[tot, cf], writes=[inc])
        k.tt("dve", inc, inc[:, :], inc[:, :], tot[:, :], ALU.subtract, [tot])
        k.tt("dve", cn, cn[:, :], ps[0][:, 0:128], inc[:, :], ALU.add, [ps[0], inc])
        parts = [self.sb("fpart%d" % i, [128, 128], BF16) for i in range(3)]
        k.cp("dve", parts[0], parts[0][:, :], cn[:, :], [cn])
        k.tt("dve", cn, cn[:, :], cn[:, :], parts[0][:, :], ALU.subtract, [parts[0]])
        k.cp("dve", parts[1], parts[1][:, :], cn[:, :], [cn])
        k.tt("dve", cn, cn[:, :], cn[:, :], parts[1][:, :], ALU.subtract, [parts[1]])
        k.cp("dve", parts[2], parts[2][:, :], cn[:, :], [cn])
        aug = [self.sb("aug%d" % i, [128, 32, 128], BF16) for i in range(4)]
        for a_ in aug:
            k.op("pool", lambda e, a_=a_: e.memset(a_[:, :, :], 0.0), writes=[a_])
        for pair in range(2):
            aq, ak = aug[pair], aug[2 + pair]
            for hl in range(2):
                h = 2 * pair + hl
                for r in range(3):
                    pv = parts[r][:, :].rearrange("p (t h) -> p t h", h=4)[:, :, h]
                    k.ts("dve", aq, aq[:, :, hl * 64 + r], pv, -1.0, None, ALU.mult, None, [parts[r]])
                    k.cp("dve", ak, ak[:, :, hl * 64 + 3 + r], pv, [parts[r]])
                k.op("pool", lambda e, aq=aq, hl=hl: e.memset(aq[:, :, hl * 64 + 3:hl * 64 + 6], 1.0), writes=[aq])
                k.op("pool", lambda e, ak=ak, hl=hl: e.memset(ak[:, :, hl * 64:hl * 64 + 3], 1.0), writes=[ak])
        ast = [self.sb("augst%d" % i, [128, 512], BF16) for i in range(2)]
        n_ = 0
        for ai in range(4):
            for g in range(8):
                bank = ps[2 + n_ % 2]
                st_ = ast[n_ % 2]
                n_ += 1
                for tt_ in range(4):
                    k.mm(bank, bank[:, tt_ * 128:(tt_ + 1) * 128], aug[ai][:, g * 4 + tt_, :], self.identb[:, :],
                         [aug[ai], self.identb])
                k.cp("act", st_, st_[:, :], bank[:, :], [bank])
                k.dma("pool", self.baug[ai, :, g * 512:(g + 1) * 512], st_[:, :], reads=[st_], writes=[self.dd["baug"]])
        phibd = self.sb("phibd", [128, 32, 128], BF16)
        k.op("pool", lambda e: e.memset(phibd[:, :, :], 0.0), writes=[phibd])
        for wi in range(2):
            k.dma("pool", phibd[wi * 64:(wi + 1) * 64, :, wi * 64:(wi + 1) * 64],
                  I["nsa_phi_w"][l, wi].rearrange("(r d) e -> d r e", d=64), writes=[phibd])
        pst = self.sb("pest", [32, 128], F32)
        for wi in range(2):
            k.dma("sp", pst[:, wi * 64:(wi + 1) * 64], I["nsa_pe"][l, wi], writes=[pst])
        self.tr(ps[4], ps[4][:, 0:32], pst[:, :], self.cfs("ident")[0:32, 0:32], [pst, cf])
        peT = self.sb("peT", [128, 32], F32)
        k.cp("dve", peT, peT[:, :], ps[4][:, 0:32], [ps[4]])
        tmps = [self.sb("ctmp%d" % i, [128, 256], BF16) for i in range(4)]
        for t_ in tmps:
            k.op("pool", lambda e, t_=t_: e.memset(t_[:, :], 0.0), writes=[t_])
        r0 = rawT[:, 0:1]
        for r in range(32):
            tm = tmps[r % 4]
            src_ = bass.AP(tensor=r0.tensor, offset=r0.offset + r, ap=[list(r0.ap[0]), [16, 255]])
            k.ts("dve", tm, tm[:, 0:255], src_, peT[:, r:r + 1], None, ALU.add, None, [rawT, peT])
            for nb in range(2):
                k.mm(ps[5 + nb], ps[5 + nb][:, 0:128], tm[:, nb * 128:(nb + 1) * 128], phibd[:, r, :], [tm, phibd],
                     start=(r == 0), stop=(r == 31))
        kcn = self.sb("kcn", [128, 2, 128], BF16)
        for nb in range(2):
            bank = ps[5 + nb]
            k.act(sq, sq[:, 0:64], bank[:, 0:64], AF.Square, [bank], accum=ssg[:, 0:1], extra_w=[ssg])
            k.act(ssg, ssg[:, 1:2], ssg[:, 0:1], AF.Ln, [ssg], bias=self.epsc[:, 0:1], scale=1.0 / 64)
            k.act(ssg, ssg[:, 2:3], ssg[:, 1:2], AF.Exp, [ssg], scale=-0.5)
            k.ts("dve", qn, qn[:, 0:64], bank[:, 0:64], ssg[:, 2:3], None, ALU.mult, None, [bank, ssg])
            for dup in range(2):
                k.tt("dve", kcn, kcn[:, nb, dup * 64:(dup + 1) * 64], qn[:, 0:64], gs_("nsa_knorm_g", 0, 64), ALU.mult, [qn, gsm])
            k.cp("dve", self.vcs, self.vcs[:, nb, :], bank[:, 64:128], [bank])
        for nb in range(2):
            k.mm(ps[7], ps[7][:, nb * 128:(nb + 1) * 128], kcn[:, nb, :], self.identb[:, :], [kcn, self.identb])
        k.cp("dve", self.kcT, self.kcT[:, :], ps[7][:, 0:256], [ps[7]])
        k.cp("dve", self.subg, self.subg[:, :], subg[:, :], [subg])
        if self.dbg:
            k.dma("pool", self.qkT[18, :, 0:256], self.kcT[:, :], reads=[self.kcT], writes=[qd])
        self.pop()


def make_inputs(inp, b):
    f = np.float32
    m = {
        "x": np.ascontiguousarray(inp["x"][b], dtype=f),
        "c": np.ascontiguousarray(inp["c"][b].reshape(8, 128), dtype=f),
        "rel_bias": np.ascontiguousarray(inp["rel_bias"].reshape(1, 256), dtype=f),
        "ada_w": np.ascontiguousarray(inp["ada_w"], dtype=f),
        "ada_b": np.ascontiguousarray(inp["ada_b"].reshape(2, 48, 128), dtype=f),
        "norm_mix_g": np.ascontiguousarray(inp["norm_mix_g"].reshape(2, 8, 128), dtype=f),
        "norm_ffn_g": np.ascontiguousarray(inp["norm_ffn_g"].reshape(2, 8, 128), dtype=f),
        "w_in": np.ascontiguousarray(inp["w_in"], dtype=f),
        "w_out": np.ascontiguousarray(inp["w_out"], dtype=f),
        "diff_qnorm_g": inp["diff_qnorm_g"].reshape(2, 1, 32), "diff_knorm_g": inp["diff_knorm_g"].reshape(2, 1, 32),
        "diff_lambda": inp["diff_lambda"].reshape(2, 1, 128), "diff_subln_g": inp["diff_subln_g"].reshape(2, 1, 64),
        "fox_qnorm_g": inp["fox_qnorm_g"].reshape(2, 1, 64), "fox_knorm_g": inp["fox_knorm_g"].reshape(2, 1, 64),
        "fox_b_f": inp["fox_b_f"].reshape(2, 1, 4), "nsa_qnorm_g": inp["nsa_qnorm_g"].reshape(2, 1, 64),
        "nsa_knorm_g": inp["nsa_knorm_g"].reshape(2, 1, 192), "nsa_pe": inp["nsa_pe"], "nsa_phi_w": inp["nsa_phi_w"],
        "ffn_w_up": inp["ffn_w_up"], "ffn_conv_w": inp["ffn_conv_w"].reshape(2, 66, 128),
        "ffn_conv_b": inp["ffn_conv_b"].reshape(2, 22, 128), "ffn_w_down": inp["ffn_w_down"],
        "cf": _CF_ARR, "cf2": _CF2_ARR, "cex": _EX_ARR,
    }
    return {k_: np.ascontiguousarray(v, dtype=f) for k_, v in m.items()}


def _phase_b(self, l):
    k, I, cf, ps = self.k, self.I, self.cf, self.ps
    self.push()
    tthi = self.sb("tthi", [128, 8, 256], BF16)
    ttlo = self.sb("ttlo", [128, 8, 256], BF16)
    brel = self.sb("brel", [128, 4, 512], F32)
    k.dma("sp", tthi[:, :, :].rearrange("p h j -> p (h j)"), self.tts[:, 0:2048], reads=[self.dd["tts"]], writes=[tthi])
    k.dma("sp", ttlo[:, :, :].rearrange("p h j -> p (h j)"), self.tts[:, 2048:4096], reads=[self.dd["tts"]], writes=[ttlo])
    k.dma("sp", brel[:, :, :].rearrange("p h j -> p (h j)"), self.brs[:, :], reads=[self.dd["brs"]], writes=[brel])
    pTs = [self.sb("pT%d" % i, [128, 512], BF16) for i in range(6)]
    oTs = [self.sb("oTs%d" % i, [128, 512], F32) for i in range(2)]
    osts = [self.sb("ost%d" % i, [128, 4, 64], BF16) for i in range(4)]
    rzs = [self.sb("rz%d" % i, [128, 8], F32) for i in range(2)]
    ons = [self.sb("on%d" % i, [128, 4, 64], F32) for i in range(2)]
    sqss = [self.sb("sqs%d" % i, [128, 4, 64], F32) for i in range(2)]
    qkTv = self.qkT
    qd = self.dd["qkT"]
    ident = self.cfs("ident")
    cnt = {"s": 0, "c": 0, "o": 0}

    def load_chunk(name, ch, src=None, dep=None):
        t = self.sb(name, [128, S], BF16)
        s_ = qkTv[ch] if src is None else src
        for hh in range(2):
            k.dma("sp" if hh == 0 else "act", t[:, hh * 2048:(hh + 1) * 2048], s_[:, hh * 2048:(hh + 1) * 2048],
                  reads=[qd if dep is None else dep], writes=[t])
        return t

    def load_rows(t, r0, src_rows, dep=None, q="sp"):
        n_ = src_rows.shape[0]
        for hh in range(2):
            k.dma(q if hh == 0 else "act", t[r0:r0 + n_, hh * 2048:(hh + 1) * 2048], src_rows[:, hh * 2048:(hh + 1) * 2048],
                  reads=[qd if dep is None else dep], writes=[t])

    def padded(name, parts):
        t = self.sb(name, [128, S], BF16)
        k.op("pool", lambda e: e.memset(t[:, :], 0.0), writes=[t])
        for (r0, rows, dep) in parts:
            load_rows(t, r0, rows, dep)
        return t
    self.padded = padded
    self.load_rows = load_rows

    def load_vaug(name, c0, nh):
        t = self.sb(name, [128, 32, nh * 65], BF16)
        k.dma("sp", t[:, :, :], self.vtm[:, c0:c0 + nh * 65].rearrange("(t p) c -> p t c", p=128),
              reads=[self.dd["vtm"]], writes=[t])
        return t

    NS = 4
    NPT = 6
    sbanks = [ps[0], ps[1], ps[2], ps[4]]
    jobs2 = [[], []]

    def softmax_head(qT, qdeps, kT, kdeps, vaug, vdeps, bias_h, act_bias, window, extras, out_cb, slot=0):
        jobs = jobs2[slot]
        tcount = 0
        for c in range(8):
            kbs = list(range(max(0, 4 * c - 4) if window else 0, 4 * c + 4))
            touched = set()
            if window:
                touched = {0, 1, 2, 3}
            acc = [[ps[6], ps[3]], [ps[7], ps[5]]][slot][c % 2]
            for kb in kbs:
                d0 = c * 512 - kb * 128
                segs = [s for s in range(4) if d0 + 128 * s >= 0 and (not window or d0 + 128 * s <= 512)]
                groups = []
                for s in segs:
                    ft = s not in touched
                    touched.add(s)
                    if groups and groups[-1][2] == ft:
                        groups[-1][1] = s
                    else:
                        groups.append([s, s, ft])
                jlo, jhi = segs[0] * 128, (segs[-1] + 1) * 128
                Sb = sbanks[(2 * tcount + slot) % NS]
                pT = pTs[(2 * tcount + slot) % NPT]
                tcount += 1
                first, last = (kb == kbs[0]), (kb == kbs[-1])

                def s0(c=c, kb=kb, d0=d0, segs=segs, Sb=Sb, jlo=jlo, jhi=jhi):
                    mms = [(Sb[:, jlo:jhi], kT[:, kb * 128:(kb + 1) * 128], qT[:, c * 512 + jlo:c * 512 + jhi], qdeps + kdeps)]
                    for (lf, rf, deps) in extras:
                        mms.append((Sb[:, jlo:jhi], lf(kb), rf(c * 512 + jlo, c * 512 + jhi), deps))
                    for s in segs:
                        dl = d0 + 128 * s
                        seg = Sb[:, s * 128:(s + 1) * 128]
                        if bias_h is not None and dl in (0, 128):
                            mms.append((seg, self.identb[:, :], tthi[:, bias_h, dl:dl + 128], [self.identb, tthi]))
                            mms.append((seg, self.identb[:, :], ttlo[:, bias_h, dl:dl + 128], [self.identb, ttlo]))
                        elif bias_h is None and dl == 0:
                            mms.append((seg, self.identb[:, :], self.trib[:, :], [self.identb, self.trib]))
                        if window and dl == 512:
                            mms.append((seg, self.identb[:, :], self.twb[:, :], [self.identb, self.twb]))
                    for i, (o_, a_, b_, deps) in enumerate(mms):
                        k.mm(Sb, o_, a_, b_, deps, start=(i == 0), stop=(i == len(mms) - 1))

                def s1(Sb=Sb, pT=pT, jlo=jlo, jhi=jhi):
                    if act_bias is None:
                        k.act(pT, pT[:, jlo:jhi], Sb[:, jlo:jhi], AF.Exp, [Sb])
                    else:
                        k.act(pT, pT[:, jlo:jhi], Sb[:, jlo:jhi], AF.Exp, [Sb, self.tab], bias=act_bias)

                def s2(kb=kb, segs=segs, pT=pT, acc=acc, first=first, last=last):
                    if first:
                        k.mm(acc, acc[:, 0:260], self.zerob[:, 0:128], self.zerob[:, 0:260], [self.zerob], start=True, stop=False)
                    for s in segs:
                        k.mm(acc, acc[:, s * 65:(s + 1) * 65], pT[:, s * 128:(s + 1) * 128], vaug(kb), [pT] + vdeps,
                             start=False, stop=last)
                stages = [s0, s1, s2]
                if last:
                    def s3(c=c, acc=acc):
                        out_cb(c, acc, acc[:, 0:260].rearrange("p (s e) -> p s e", e=65))
                    stages += [s3]
                jobs.append(stages)

    def run_jobs():
        ja, jb = jobs2
        merged = []
        for i in range(max(len(ja), len(jb))):
            sa = ja[i] if i < len(ja) else []
            sb_ = jb[i] if i < len(jb) else []
            st = []
            for si in range(max(len(sa), len(sb_))):
                fa = sa[si] if si < len(sa) else None
                fb = sb_[si] if si < len(sb_) else None
                st.append(lambda fa=fa, fb=fb: ((fa() if fa else None), (fb() if fb else None)))
            merged.append(st)
        run_staged(merged)
        del ja[:]
        del jb[:]
    self.run_jobs = run_jobs

    def normalize(bank, v3, dst_tt, dst_ap, slot=0):
        rz = rzs[slot]
        k.op("dve", lambda e: e.reciprocal(out=rz[:, 0:4], in_=v3[:, :, 64]), reads=[bank], writes=[rz])
        k.tt("dve", dst_tt, dst_ap, v3[:, :, 0:64], blast(rz[:, 0:4], 64), ALU.mult, [bank, rz])

    def store_o(c, col, src_tt, src_ap):
        ost = osts[cnt["o"] % 4]
        cnt["o"] += 1
        k.cp("act", ost, ost[:, :, :], src_ap, [src_tt])
        k.dma("pool", self.o_d[c * 512:(c + 1) * 512, col:col + 64].rearrange("(s p) e -> p s e", p=128), ost[:, :, :],
              reads=[ost], writes=[self.dd["o_d"]])

    if "A" in self.mixers:
        self.push()
        kA = [load_chunk("kA%d" % i, 4 + i) for i in range(2)]
        vA = load_vaug("vA", 0, 4)
        o1s = [self.sb("o1all%d" % i, [128, 32, 64], F32) for i in range(2)]
        nlam = self.sb("nlam", [128, 1], F32)
        k.ts("dve", nlam, nlam[:, :], self.lam[:, l:l + 1], -1.0, None, ALU.mult, None, [self.lam])
        qpad = {}
        for hp in range(2):
            for m in range(2):
                for slot in range(2):
                    pb = 64 * slot
                    qpad[(hp, m, slot)] = padded("qAp%d%d%d" % (hp, m, slot), [(pb, qkTv[2 * hp + m][pb:pb + 64, :], None)])
        for hp in range(2):
            for m in range(2):
                for slot in range(2):
                    h = 2 * hp + slot
                    pb = 64 * slot
                    qt = qpad[(hp, m, slot)]
                    kt = kA[hp]

                    def cb(c, bank, v3, h=h, m=m, slot=slot):
                        o1, on, sqs, rz = o1s[slot], ons[slot], sqss[slot], rzs[slot]
                        if m == 0:
                            normalize(bank, v3, o1, o1[:, 4 * c:4 * c + 4, :], slot)
                        else:
                            normalize(bank, v3, on, on[:, :, :], slot)
                            k.op("dve", lambda e: e.scalar_tensor_tensor(
                                out=on[:, :, :].rearrange("p s e -> p (s e)"), in0=on[:, :, :].rearrange("p s e -> p (s e)"),
                                scalar=nlam[:, 0:1], in1=o1[:, 4 * c:4 * c + 4, :].rearrange("p s e -> p (s e)"),
                                op0=ALU.mult, op1=ALU.add), reads=[on, o1, nlam], writes=[on])
                            k.act(sqs, sqs[:, :, :], on[:, :, :], AF.Square, [on])
                            k.op("dve", lambda e: e.tensor_reduce(out=rz[:, 4:8], in_=sqs[:, :, :], axis=AX.X, op=ALU.add),
                                 reads=[sqs], writes=[rz])
                            k.act(rz, rz[:, 4:8], rz[:, 4:8], AF.Ln, [rz], bias=self.epsc[:, 0:1], scale=1.0 / 64)
                            k.act(rz, rz[:, 4:8], rz[:, 4:8], AF.Exp, [rz], scale=-0.5)
                            k.tt("dve", on, on[:, :, :], on[:, :, :], blast(rz[:, 4:8], 64), ALU.mult, [rz])
                            k.tt("dve", on, on[:, :, :], on[:, :, :], bmid(self.subg[:, :], 4), ALU.mult, [self.subg])
                            store_o(c, h * 64, on, on[:, :, :])
                    softmax_head(qt[:, :], [qt], kt[:, :], [kt],
                                 lambda kb, h=h: vA[:, kb, h * 65:(h + 1) * 65], [vA], h, None,
                                 False, [], cb, slot)
        run_jobs()
        self.pop()

    if "B" in self.mixers:
        self.push()
        bd = self.dd["baug"]
        qBp, kBp = [], []
        for h in range(4):
            pr, pb = h // 2, 64 * (h % 2)
            qBp.append(padded("qBp%d" % h, [(0, qkTv[6 + pr][pb:pb + 64, :], None), (64, self.baug[pr][pb:pb + 6, :], bd)]))
            kBp.append(padded("kBp%d" % h, [(0, qkTv[8 + pr][pb:pb + 64, :], None), (64, self.baug[2 + pr][pb:pb + 6, :], bd)]))
        vB = load_vaug("vB", 260, 4)
        for pair in range(2):
            for slot in range(2):
                h = 2 * pair + slot
                pb = 64 * slot

                def cb(c, bank, v3, h=h, slot=slot):
                    on = ons[slot]
                    normalize(bank, v3, on, on[:, :, :], slot)
                    store_o(c, 256 + h * 64, on, on[:, :, :])
                softmax_head(qBp[h][:, :], [qBp[h]], kBp[h][:, :], [kBp[h]],
                             lambda kb, h=h: vB[:, kb, h * 65:(h + 1) * 65], [vB], None, None, False, [], cb, slot)
            run_jobs()
        self.pop()

    if "C" in self.mixers:
        self.mixer_c(l, load_chunk)

    if "D" in self.mixers:
        self.mixer_d(l, load_chunk, load_vaug, softmax_head, normalize, tthi, ttlo, brel)
    self.pop()


Prog.phase_b = _phase_b


def run_staged(jobs):
    if not jobs:
        return
    ns = max(len(j) for j in jobs)
    for step in range(len(jobs) + ns):
        for si in range(ns - 1, -1, -1):
            j = step - si
            if 0 <= j < len(jobs) and si < len(jobs[j]):
                jobs[j][si]()


def _mixer_c(self, l, load_chunk):
    k, ps, cf = self.k, self.ps, self.cf
    self.push()
    qC = [self.padded("qCp%d" % h, [(64 * (h % 2), self.qkT[10 + h // 2][64 * (h % 2):64 * (h % 2) + 64, :], None)])
          for h in range(4)]
    kC = [load_chunk("kC%d" % i, 12 + i) for i in range(2)]
    vC = self.sb("vC", [128, 32, 256], BF16)
    k.dma("sp", vC[:, :, :], self.cvr.rearrange("(t p) c -> p t c", p=128), reads=[self.dd["cvr"]], writes=[vC])
    NB = 9
    SPb = [self.sb("cSP%d" % i, [128, 512], F32) for i in range(NB)]
    PRb = [self.sb("cPR%d" % i, [128, 512], F32) for i in range(NB)]
    Eb = [self.sb("cE%d" % i, [128, 512], F32) for i in range(NB)]
    ab = [self.sb("ca%d" % i, [128, 512], BF16) for i in range(NB)]
    aTb = [self.sb("caT%d" % i, [128, 512], BF16) for i in range(NB)]
    oc = [self.sb("coc%d" % i, [128, 256], BF16) for i in range(2)]
    zeros = self.zerob
    zbanks = [ps[0], ps[1], ps[2]]
    tbanks = [ps[3], ps[4], ps[5]]
    jobs = []
    n = 0
    for i in range(32):
        ost = oc[i % 2]
        accb = ps[6 + i % 2]
        k0 = 128 * (31 - i)
        nblk = i + 1
        chunks = [(k0 + 512 * j, min(512, 128 * nblk - 512 * j)) for j in range((nblk + 3) // 4)]
        carries = [None] * 4
        nch = len(chunks)
        for ci, (ks_, w_) in enumerate(chunks):
            for h in range(4):
                pb = 64 * (h % 2)
                qt, kt = qC[h], kC[h // 2]
                zb, tb = zbanks[n % 3], tbanks[n % 3]
                SP, PR, E, a, at = SPb[n % NB], PRb[n % NB], Eb[n % NB], ab[n % NB], aTb[n % NB]
                n += 1
                car = carries[h]
                carries[h] = (PR, PR[:, w_ - 1:w_])
                nb_ = w_ // 128

                def s0(i=i, pb=pb, qt=qt, kt=kt, zb=zb, ks_=ks_, w_=w_):
                    k.mm(zb, zb[:, 0:w_], qt[pb:pb + 64, i * 128:(i + 1) * 128], kt[pb:pb + 64, ks_:ks_ + w_], [qt, kt])

                def s1(zb=zb, E=E, w_=w_, ci=ci):
                    k.act(E, E[:, 0:w_], zb[:, 0:w_], AF.Exp, [zb])
                    if ci == 0:
                        k.tt("pool", E, E[:, 0:128], E[:, 0:128], self.cfs("M01"), ALU.mult, [cf])

                def s2(E=E, SP=SP, w_=w_):
                    k.act(SP, SP[:, 0:w_], E[:, 0:w_], AF.Ln, [E, self.epsc], bias=self.epsc[:, 1:2])

                def s3(SP=SP, PR=PR, w_=w_, car=car):
                    if car is None:
                        k.op("dve", lambda e: e.tensor_tensor_scan(
                            out=PR[:, 0:w_], data0=SP[:, 0:w_], data1=zeros[:, 0:w_], initial=0.0, op0=ALU.add, op1=ALU.add),
                            reads=[SP, zeros], writes=[PR])
                    else:
                        ctt, cap = car
                        k.op("dve", lambda e: e.tensor_tensor_scan(
                            out=PR[:, 0:w_], data0=SP[:, 0:w_], data1=zeros[:, 0:w_], initial=cap, op0=ALU.add, op1=ALU.add),
                            reads=[SP, zeros, ctt], writes=[PR])

                def s4(SP=SP, PR=PR, w_=w_):
                    k.act(SP, SP[:, 0:w_], PR[:, 0:w_], AF.Exp, [PR], scale=-1.0)

                def s5(SP=SP, E=E, a=a, w_=w_):
                    k.tt("pool", a, a[:, 0:w_], E[:, 0:w_], SP[:, 0:w_], ALU.mult, [E, SP])

                def s6(a=a, tb=tb, nb_=nb_):
                    for b in range(nb_):
                        k.mm(tb, tb[:, b * 128:(b + 1) * 128], a[:, b * 128:(b + 1) * 128], self.identb[:, :], [a, self.identb])

                def s7(at=at, tb=tb, w_=w_):
                    k.cp("dve", at, at[:, 0:w_], tb[:, 0:w_], [tb])

                def s8(at=at, accb=accb, h=h, ci=ci, ks_=ks_, nb_=nb_, nch=nch):
                    if ci == 0 and h == 0:
                        k.mm(accb, accb[:, 0:256], self.zerob[:, 0:128], self.zerob[:, 0:256], [self.zerob], start=True, stop=False)
                    for b in range(nb_):
                        kblk = ks_ // 128 + b
                        k.mm(accb, accb[:, h * 64:(h + 1) * 64], at[:, b * 128:(b + 1) * 128], vC[:, kblk, h * 64:(h + 1) * 64],
                             [at, vC], start=False, stop=(ci == nch - 1 and b == nb_ - 1 and h == 3))

                stages = [s0, s1, s2, s3, s4, s5, s6, s7, s8]
                if ci == nch - 1 and h == 3:
                    def s9(accb=accb, ost=ost, i=i):
                        k.cp("act", ost, ost[:, :], accb[:, 0:256], [accb])

                    def s10(ost=ost, i=i):
                        k.dma("pool", self.o_d[i * 128:(i + 1) * 128, 512:768], ost[:, :], reads=[ost], writes=[self.dd["o_d"]])
                    stages += [s9, s10]
                jobs.append(stages)
    run_staged(jobs)
    self.pop()


def _mixer_d(self, l, load_chunk, load_vaug, softmax_head, normalize, tthi, ttlo, brel):
    k, ps, cf, I = self.k, self.ps, self.cf, self.I
    self.push()
    qD = [load_chunk("qD%d" % i, 14 + i) for i in range(2)]
    oD = self.sb("oDacc", [128, 32, 256], F32)
    nmT = self.sb("nmT", [128, S], BF16)
    kcT, vcs = self.kcT, self.vcs
    NBc = 6
    sC = [self.sb("dsC%d" % i, [128, 256], F32) for i in range(NBc)]
    eC = [self.sb("deC%d" % i, [128, 256], F32) for i in range(NBc)]
    pbf = [self.sb("dpb%d" % i, [128, 256], BF16) for i in range(NBc)]
    pTc = [self.sb("dpT%d" % i, [128, 256], BF16) for i in range(NBc)]
    psumCs = [self.sb("dpsum%d" % i, [128, 256], F32) for i in range(3)]
    zcs = [self.sb("dzc%d" % i, [128, 8], F32) for i in range(3)]
    imps = [self.sb("dimp%d" % i, [128, 64], F32) for i in range(3)]
    imp2s = [self.sb("dimp2%d" % i, [128, 64], F32) for i in range(3)]
    nmfs = [self.sb("dnmf%d" % i, [128, 64], F32) for i in range(3)]
    m8s = [self.sb("dm8%d" % i, [128, 16], F32) for i in range(3)]
    nms = [self.sb("dnm%d" % i, [128, 128], BF16) for i in range(3)]
    dons = [self.sb("don%d" % i, [128, 4, 64], F32) for i in range(2)]
    import os
    dstage = int(os.environ.get("DSTAGE", "9"))
    jobs = []
    n = 0
    zbanks = [ps[0], ps[1], ps[2]]
    for i in range(32):
        off = 255 - 8 * i
        nnb = 1 if i < 16 else 2
        psumC, zc, imp, imp2, nmf, m8, nm = (psumCs[i % 3], zcs[i % 3], imps[i % 3], imp2s[i % 3], nmfs[i % 3],
                                             m8s[i % 3], nms[i % 3])
        ocb = [ps[4], ps[6]][i % 2]
        nmb = [ps[5], ps[7]][i % 2]
        for h in range(4):
            pb = 64 * (h % 2)
            qt = qD[h // 2]
            zb = zbanks[n % 3]
            tb = ps[3]
            s_, e_, pb_, pt_ = sC[n % NBc], eC[n % NBc], pbf[n % NBc], pTc[n % NBc]
            n += 1

            def s0(i=i, pb=pb, qt=qt, zb=zb):
                k.mm(zb, zb[:, 0:256], qt[pb:pb + 64, i * 128:(i + 1) * 128], kcT[pb:pb + 64, :], [qt, kcT])

            def s1(zb=zb, s_=s_, h=h, off=off):
                k.tt("dve", s_, s_[:, :], zb[:, 0:256], brel[:, h, off:off + 256], ALU.add, [zb, brel])

            def s2(s_=s_, e_=e_, zc=zc, h=h):
                k.act(e_, e_[:, :], s_[:, :], AF.Exp, [s_], accum=zc[:, h:h + 1], extra_w=[zc])

            def s3(e_=e_, pb_=pb_, zc=zc, psumC=psumC, h=h):
                k.ts("dve", zc, zc[:, 4 + h:5 + h], zc[:, h:h + 1], 1e-30, None, ALU.add, None, [zc])
                k.op("dve", lambda e: e.reciprocal(out=zc[:, 4 + h:5 + h], in_=zc[:, 4 + h:5 + h]), reads=[zc], writes=[zc])
                k.ts("dve", pb_, pb_[:, :], e_[:, :], zc[:, 4 + h:5 + h], None, ALU.mult, None, [e_, zc])
                if h == 0:
                    k.ts("dve", psumC, psumC[:, :], e_[:, :], zc[:, 4:5], None, ALU.mult, None, [e_, zc])
                else:
                    k.op("dve", lambda e: e.scalar_tensor_tensor(
                        out=psumC[:, :], in0=e_[:, :], scalar=zc[:, 4 + h:5 + h], in1=psumC[:, :], op0=ALU.mult, op1=ALU.add),
                        reads=[e_, zc, psumC], writes=[psumC])

            def s4(pb_=pb_, tb=tb, nnb=nnb):
                for nb in range(nnb):
                    k.mm(tb, tb[:, nb * 128:(nb + 1) * 128], pb_[:, nb * 128:(nb + 1) * 128], self.identb[:, :], [pb_, self.identb])

            def s5(pt_=pt_, tb=tb, nnb=nnb):
                k.cp("act", pt_, pt_[:, 0:nnb * 128], tb[:, 0:nnb * 128], [tb])

            def s6(pt_=pt_, ocb=ocb, h=h, nnb=nnb):
                for nb in range(nnb):
                    k.mm(ocb, ocb[:, h * 64:(h + 1) * 64], pt_[:, nb * 128:(nb + 1) * 128], vcs[:, nb, :], [pt_, vcs],
                         start=(nb == 0), stop=(nb == nnb - 1))
            stages = [s0, s1, s2, s3, s4, s5, s6]
            if h == 3:
                def s7(i=i, ocb=ocb, psumC=psumC, imp=imp):
                    k.tt("dve", oD, oD[:, i, :].rearrange("p (h e) -> p h e", e=64),
                         ocb[:, 0:256].rearrange("p (h e) -> p h e", e=64), blast(self.gates[:, i, 0:4], 64), ALU.mult,
                         [ocb, self.gates])
                    pv = psumC[:, :].rearrange("p (j r) -> p j r", r=4)
                    k.op("dve", lambda e: e.tensor_reduce(out=imp[:, :], in_=pv, axis=AX.X, op=ALU.add), reads=[psumC], writes=[imp])
                    k.tt("dve", imp, imp[:, 1:64], imp[:, 1:64], pv[:, 0:63, 3], ALU.add, [psumC])

                def s8(i=i, imp=imp, imp2=imp2, m8=m8):
                    st = 64 - 2 * i
                    k.tt("dve", imp2, imp2[:, :], imp[:, :], self.cfs("M1", st, st + 64), ALU.mult, [imp, cf])
                    k.tt("dve", imp2, imp2[:, :], imp2[:, :], self.cfs("CC", st, st + 64), ALU.add, [cf])
                    k.op("dve", lambda e: e.memset(imp2[:, 0:1], 1.0e4), writes=[imp2])
                    k.op("dve", lambda e: e.max(out=m8[:, 0:8], in_=imp2[:, :]), reads=[imp2], writes=[m8])

                def s9(imp=imp, imp2=imp2, m8=m8, nmf=nmf, nm=nm):
                    k.op("dve", lambda e: e.match_replace(out=imp[:, :], in_to_replace=m8[:, 0:8], in_values=imp2[:, :],
                                                          imm_value=-2.0), reads=[m8, imp2], writes=[imp])
                    k.op("dve", lambda e: e.max(out=m8[:, 8:16], in_=imp[:, :]), reads=[imp], writes=[m8])
                    k.ts("dve", nmf, nmf[:, :], imp2[:, :], m8[:, 15:16], BIG, ALU.is_ge, ALU.mult, [imp2, m8])
                    k.ts("dve", nm, nm[:, 0:64], nmf[:, :], -BIG, None, ALU.add, None, [nmf])
                    k.ts("dve", nm, nm[:, 64:128], nmf[:, :], -BIG, None, ALU.add, None, [nmf])

                def s10(nm=nm, nmb=nmb):
                    k.mm(nmb, nmb[:, 0:128], nm[:, :], self.identb[:, :], [nm, self.identb])

                def s11(i=i, nmb=nmb):
                    k.cp("act", nmT, nmT[:, i * 128:(i + 1) * 128], nmb[:, 0:128], [nmb])
                stages += [s7, s8, s9, s10, s11]
            jobs.append(stages)
    run_staged(jobs)
    k.dma("pool", self.baug[0, 0:64, :], nmT[0:64, :], reads=[nmT], writes=[self.dd["baug"]])
    for br, (ch, c0, gcol) in enumerate([(16, 520, 4), (17, 585, 8)]):
        if dstage < 4 + br:
            continue
        self.push()
        if br == 0:
            kT = self.sb("ksx", [128, S], BF16)
            self.load_rows(kT, 0, self.qkT[16][0:64, :])
            k.dma("pool", kT[64:128, :], I["cex"][:, :], writes=[kT])
        else:
            kT = load_chunk("dk%d" % br, ch)
        vv = load_vaug("dv%d" % br, c0, 1)
        qq4 = []
        for h in range(4):
            pair, pb = h // 2, 64 * (h % 2)
            if br == 0:
                qq4.append(self.padded("qsel%d" % h, [(0, self.qkT[14 + pair][pb:pb + 64, :], None),
                                                     (64, self.baug[0][0:64, :], self.dd["baug"])]))
            else:
                qq4.append(self.padded("qwin%d" % h, [(pb, self.qkT[14 + pair][pb:pb + 64, :], None)]))
        for pair in range(2):
            for slot in range(2):
                h = 2 * pair + slot

                def cb(c, bank, v3, h=h, gcol=gcol, slot=slot):
                    on = dons[slot]
                    normalize(bank, v3, on, on[:, :, :], slot)
                    k.tt("dve", on, on[:, :, :], on[:, :, :], blast(self.gates[:, 4 * c:4 * c + 4, gcol + h], 64), ALU.mult,
                         [self.gates])
                    dst = oD[:, 4 * c:4 * c + 4, h * 64:(h + 1) * 64]
                    k.tt("dve", oD, dst, dst, on[:, :, :], ALU.add, [on])
                qq = qq4[h]
                softmax_head(qq[:, :], [qq], kT[:, :], [kT],
                             lambda kb, vv=vv: vv[:, kb, 0:65], [vv], 4 + h, None, br == 1, [], cb, slot)
        self.run_jobs()
        self.pop()
    ob = [self.sb("dob%d" % i, [128, 256], BF16) for i in range(2)]
    for t in range(32):
        o_ = ob[t % 2]
        k.cp("act", o_, o_[:, :], oD[:, t, :], [oD])
        k.dma("pool", self.o_d[t * 128:(t + 1) * 128, 768:1024], o_[:, :], reads=[o_], writes=[self.dd["o_d"]])
    self.pop()


Prog.mixer_c = _mixer_c
Prog.mixer_d = _mixer_d


def _phase_c(self, l, xsrc, xsd, xdst, xdd):
    k, I, cf, ps = self.k, self.I, self.cf, self.ps
    ident = self.cfs("ident")
    GT = 256
    NG = S // GT
    self.push()
    wout = self.sb("wout", [128, 8, D], BF16)
    wup = self.sb("wup", [128, 8, 2 * DFF], BF16)
    wdn = self.sb("wdn", [128, NFC, D], BF16)
    src_up = I["ffn_w_up"][l].rearrange("(c p) n -> p c n", p=128)
    for c in range(8):
        for hh in range(2):
            k.dma("pool", wup[:, c, hh * DFF:(hh + 1) * DFF], src_up[:, c, hh * DFF:(hh + 1) * DFF], writes=[wup])
    cst = self.sb("cst", [88, 128], F32)
    k.dma("sp", cst[0:66, :], I["ffn_conv_w"][l], writes=[cst])
    k.dma("sp", cst[66:88, :], I["ffn_conv_b"][l], writes=[cst])
    self.tr(ps[0], ps[0][:, 0:88], cst[:, :], ident[0:88, 0:88], [cst, cf])
    cw = self.sb("cw", [128, 88], F32)
    k.cp("dve", cw, cw[:, :], ps[0][:, 0:88], [ps[0]])
    ab = self.sb("abf", [128, 16], F32)
    k.ts("dve", ab, ab[:, 0:8], self.modcol(l, 4), 1.0, None, ALU.add, None, [self.modT])
    k.tt("dve", ab, ab[:, 0:8], ab[:, 0:8], self.colv[:, 112 + 8 * l:120 + 8 * l], ALU.mult, [self.colv])
    k.cp("dve", ab, ab[:, 8:16], self.modcol(l, 3), [self.modT])
    self.push()
    gbc = [self.sb("gbc%d" % i, [128, D], F32) for i in range(2)]
    dg = self.sb("dgt", [128, 128], F32)
    for gi, which in enumerate([2, 5]):
        for c in range(8):
            k.ts("dve", dg, dg[:, :], ident, self.modcol(l, which)[:, c:c + 1], None, ALU.mult, None, [cf, self.modT])
            b = ps[1 + (c // 4)]
            k.mm(b, b[:, (c % 4) * 128:(c % 4 + 1) * 128], self.cfs("ones"), dg[:, :], [dg, cf])
        for hh in range(2):
            k.cp("dve", gbc[gi], gbc[gi][:, hh * 512:(hh + 1) * 512], ps[1 + hh][:, :], [ps[1 + hh]])
    stg = [self.sb("wstg%d" % i, [128, D], F32) for i in range(4)]
    src_o = I["w_out"][l].rearrange("(c p) n -> p c n", p=128)
    src_d = I["ffn_w_down"][l].rearrange("(c p) n -> p c n", p=128)
    n = 0
    for c in range(8):
        st_ = stg[n % 4]
        k.dma("sp" if n % 2 == 0 else "act", st_[:, :], src_o[:, c, :], writes=[st_])
        k.tt("dve" if n % 2 == 0 else "pool", wout, wout[:, c, :], st_[:, :], gbc[0][:, :], ALU.mult, [st_, gbc[0]])
        n += 1
    for c in range(NFC):
        st_ = stg[n % 4]
        k.dma("sp" if n % 2 == 0 else "act", st_[:, :], src_d[:, c, :], writes=[st_])
        k.tt("dve" if n % 2 == 0 else "pool", wdn, wdn[:, c, :], st_[:, :], gbc[1][:, :], ALU.mult, [st_, gbc[1]])
        n += 1
    self.pop()
    ots = [self.sb("ot%d" % i, [128, D], BF16) for i in range(2)]
    oT = self.sb("oTc", [128, 8, 128], BF16)
    xms = [self.sb("xm%d" % i, [128, D], F32) for i in range(3)]
    xn = self.sb("xnc", [128, D], F32)
    sss = [self.sb("ssc%d" % i, [128, 4], F32) for i in range(2)]
    h2T = self.sb("h2T", [128, 8, GT], BF16)
    gsb = [self.sb("gsb%d" % i, [128, GT + 2], F32) for i in range(2)]
    accs = [self.sb("cacc%d" % i, [128, GT], F32) for i in range(2)]
    sgs = [self.sb("csg%d" % i, [128, GT], F32) for i in range(2)]
    aT = self.sb("aTc", [128, NFC, GT], BF16)
    halo = self.sb("halo", [128, NFC, 2], F32)
    k.op("dve", lambda e: e.memset(halo[:, :, :], 0.0), writes=[halo])
    nt = 0
    for g in range(NG):
        for tt_ in range(GT // 128):
            t = g * (GT // 128) + tt_
            ot = ots[t % 2]
            xm = xms[t % 3]
            ss = sss[t % 2]
            k.dma("sp", ot[:, :], self.o_d[t * 128:(t + 1) * 128, :], reads=[self.dd["o_d"]], writes=[ot])
            k.dma("act", xm[:, :], xsrc[t * 128:(t + 1) * 128, :], reads=[xsd] if xsd else [], writes=[xm])
            for c in range(8):
                b = ps[c // 4]
                k.mm(b, b[:, (c % 4) * 128:(c % 4 + 1) * 128], ot[:, c * 128:(c + 1) * 128], self.identb[:, :], [ot, self.identb])
            for hh in range(2):
                k.cp("act" if hh == 0 else "dve", oT, oT[:, hh * 4:(hh + 1) * 4, :].rearrange("p c t -> p (c t)"), ps[hh][:, :], [ps[hh]])
            for hh in range(2):
                b = ps[2 + hh]
                for c in range(8):
                    k.mm(b, b[:, :], oT[:, c, :], wout[:, c, hh * 512:(hh + 1) * 512], [oT, wout], start=(c == 0), stop=(c == 7))
                k.tt("dve", xm, xm[:, hh * 512:(hh + 1) * 512], b[:, :], xm[:, hh * 512:(hh + 1) * 512], ALU.add, [b])
            if self.dbg:
                k.dma("pool", self.xmid[t * 128:(t + 1) * 128, :], xm[:, :], reads=[xm], writes=[self.dd["xmid"]])
            k.act(xn, xn[:, :], xm[:, :], AF.Square, [xm], accum=ss[:, 0:1], extra_w=[ss])
            k.act(ss, ss[:, 1:2], ss[:, 0:1], AF.Ln, [ss], bias=self.epsc[:, 0:1], scale=1.0 / D)
            k.act(ss, ss[:, 2:3], ss[:, 1:2], AF.Exp, [ss], scale=-0.5)
            k.ts("dve", xn, xn[:, :], xm[:, :], ss[:, 2:3], None, ALU.mult, None, [xm, ss])
            for c in range(8):
                b = ps[4 + c // 4]
                self.tr(b, b[:, (c % 4) * 128:(c % 4 + 1) * 128], xn[:, c * 128:(c + 1) * 128], ident, [xn, cf])
            for c in range(8):
                b = ps[4 + c // 4]
                pin = b[:, (c % 4) * 128:(c % 4 + 1) * 128]
                dst = h2T[:, c, tt_ * 128:(tt_ + 1) * 128]
                if c % 2 == 0:
                    k.ts("dve", h2T, dst, pin, ab[:, c:c + 1], ab[:, 8 + c:9 + c], ALU.mult, ALU.add, [b, ab])
                else:
                    k.act(h2T, dst, pin, AF.Identity, [b, ab], bias=ab[:, 8 + c:9 + c], scale=ab[:, c:c + 1])
        for fc in range(NFC):
            bg = ps[(2 * fc) % 4]
            bv = ps[(2 * fc + 1) % 4]
            for c in range(8):
                k.mm(bg, bg[:, 0:GT], wup[:, c, fc * 128:(fc + 1) * 128], h2T[:, c, :], [wup, h2T], start=(c == 0), stop=(c == 7))
            for c in range(8):
                k.mm(bv, bv[:, 0:GT], wup[:, c, DFF + fc * 128:DFF + (fc + 1) * 128], h2T[:, c, :], [wup, h2T],
                     start=(c == 0), stop=(c == 7))
            gs_, ac_, sg_ = gsb[fc % 2], accs[fc % 2], sgs[fc % 2]
            k.cp("act", gs_, gs_[:, 0:2], halo[:, fc, :], [halo])
            k.cp("act", gs_, gs_[:, 2:GT + 2], bg[:, 0:GT], [bg])
            k.cp("act", halo, halo[:, fc, :], gs_[:, GT:GT + 2], [gs_])
            k.ts("dve", ac_, ac_[:, :], gs_[:, 2:GT + 2], cw[:, 44 + fc:45 + fc], cw[:, 66 + fc:67 + fc], ALU.mult, ALU.add,
                 [gs_, cw])
            k.op("dve", lambda e, ac_=ac_, gs_=gs_, fc=fc: e.scalar_tensor_tensor(
                out=ac_[:, :], in0=gs_[:, 1:GT + 1], scalar=cw[:, 22 + fc:23 + fc], in1=ac_[:, :], op0=ALU.mult, op1=ALU.add),
                reads=[gs_, cw, ac_], writes=[ac_])
            k.op("dve", lambda e, ac_=ac_, gs_=gs_, fc=fc: e.scalar_tensor_tensor(
                out=ac_[:, :], in0=gs_[:, 0:GT], scalar=cw[:, fc:fc + 1], in1=ac_[:, :], op0=ALU.mult, op1=ALU.add),
                reads=[gs_, cw, ac_], writes=[ac_])
            k.act(sg_, sg_[:, :], ac_[:, :], AF.Silu, [ac_])
            k.tt("dve", aT, aT[:, fc, :], sg_[:, :], bv[:, 0:GT], ALU.mult, [sg_, bv])
        for tt_ in range(GT // 128):
            t = g * (GT // 128) + tt_
            xm = xms[t % 3]
            for hh in range(2):
                b = ps[4 + hh]
                for fc in range(NFC):
                    k.mm(b, b[:, :], aT[:, fc, tt_ * 128:(tt_ + 1) * 128], wdn[:, fc, hh * 512:(hh + 1) * 512], [aT, wdn],
                         start=(fc == 0), stop=(fc == NFC - 1))
                k.tt("dve", xm, xm[:, hh * 512:(hh + 1) * 512], b[:, :], xm[:, hh * 512:(hh + 1) * 512], ALU.add, [b])
            k.dma("pool", xdst[t * 128:(t + 1) * 128, :], xm[:, :], reads=[xm], writes=[xdd])
    self.pop()


Prog.phase_c = _phase_c


_PROG = None


def kernel(**inputs):
    global _PROG
    if _PROG is None:
        _PROG = Prog()
    in_maps = [make_inputs(inputs, r % 4) for r in range(4)]
    in_maps = in_maps + in_maps
    res = run_bass_kernel_spmd(_PROG.nc, in_maps, core_ids=list(range(8)))
    out = np.stack([np.asarray(res.results[b]["y"], dtype=np.float32) for b in range(4)], axis=0)
    return out
```

```python
import math
import numpy as np
import concourse.bass as bass
import concourse.mybir as mybir
from concourse.bass_utils import run_bass_kernel_spmd

F32 = mybir.dt.float32
BF16 = mybir.dt.bfloat16
AF = mybir.ActivationFunctionType
ALU = mybir.AluOpType
AX = mybir.AxisListType

S = 4096
D = 1024
NT = 32
DFF = 2816
NFC = 22
PIN = 2960
BIG = 32768.0
EPS = 1e-6
NQ = 8


class Dep:
    __slots__ = ("w", "r")

    def __init__(self):
        self.w = {}
        self.r = {}


class TT:
    def __init__(self, t):
        self.t = t
        self.dep = Dep()

    def __getitem__(self, idx):
        return self.t[idx]


class Kern:
    def __init__(self, nc):
        self.nc = nc
        self.engs = {"pe": nc.tensor, "act": nc.scalar, "dve": nc.vector, "pool": nc.gpsimd, "sp": nc.sync}
        self.sem = {e: nc.semaphore("c_" + e).__enter__() for e in ["pe", "act", "dve", "pool"]}
        self.cnt = {e: 0 for e in self.sem}
        self.waited = {}
        self.dq = {}
        for q in ["sp", "pool", "act"]:
            self.dq[q] = {"sems": [nc.semaphore("d_%s%d" % (q, i)).__enter__() for i in range(NQ)],
                          "cnt": [0] * NQ, "i": 0}
        self.own = {e: id(self.sem[e]) for e in self.sem}
        self.pools = []

    def sb(self, name, shape, dt):
        return TT(self.nc.sbuf_tensor(name, shape, dt).__enter__())

    def ps(self, name, shape, dt):
        return TT(self.nc.psum_tensor(name, shape, dt).__enter__())

    def _wait(self, eng, sem, val):
        key = (eng, id(sem))
        if self.waited.get(key, 0) >= val:
            return
        self.waited[key] = val
        self.engs[eng].wait_ge(sem, val)

    def _deps(self, eng, reads, writes):
        own = self.own.get(eng)
        for d in reads:
            for sid, (sem, val) in d.dep.w.items():
                if sid == own and eng == "pe":
                    continue
                self._wait(eng, sem, val)
        for d in writes:
            for sid, (sem, val) in d.dep.w.items():
                if sid == own and eng == "pe":
                    continue
                self._wait(eng, sem, val)
            for sid, (sem, val) in d.dep.r.items():
                if sid == own and eng == "pe":
                    continue
                self._wait(eng, sem, val)

    def _reg(self, sem, val, reads, writes):
        sid = id(sem)
        for d in reads:
            d.dep.r[sid] = (sem, val)
        for d in writes:
            d.dep.w[sid] = (sem, val)

    def op(self, eng, fn, reads=(), writes=()):
        self._deps(eng, reads, writes)
        ins = fn(self.engs[eng])
        self.cnt[eng] += 1
        ins.then_inc(self.sem[eng], 1)
        self._reg(self.sem[eng], self.cnt[eng], reads, writes)

    def dma(self, q, out_ap, in_ap, reads=(), writes=()):
        self._deps(q, reads, writes)
        Q = self.dq[q]
        j = Q["i"] % NQ
        Q["i"] += 1
        sem = Q["sems"][j]
        if Q["cnt"][j] > 0:
            self._wait(q, sem, 16 * Q["cnt"][j])
        Q["cnt"][j] += 1
        self.engs[q].dma_start(out=out_ap, in_=in_ap).then_inc(sem, 16)
        self._reg(sem, 16 * Q["cnt"][j], reads, writes)

    def finish(self, outs):
        for d in outs:
            for sid, (sem, val) in d.dep.w.items():
                self._wait("sp", sem, val)

    def mm(self, out_tt, out_ap, lhsT, rhs, reads, start=True, stop=True):
        self.op("pe", lambda e: e.matmul(out_ap, lhsT, rhs, start=start, stop=stop, skip_group_check=True),
                reads=reads, writes=[out_tt])

    def act(self, out_tt, out_ap, in_ap, func, reads, bias=None, scale=None, accum=None, extra_w=()):
        kw = {}
        if bias is not None:
            kw["bias"] = bias
        if scale is not None:
            kw["scale"] = scale
        if accum is not None:
            kw["accum_out"] = accum
        self.op("act", lambda e: e.activation(out_ap, in_ap, func, **kw), reads=reads,
                writes=[out_tt] + list(extra_w))

    def ts(self, eng, out_tt, out_ap, in_ap, s1, s2, op0, op1, reads):
        if op1 is None:
            self.op(eng, lambda e: e.tensor_scalar(out=out_ap, in0=in_ap, scalar1=s1, scalar2=None, op0=op0),
                    reads=reads, writes=[out_tt])
        else:
            self.op(eng, lambda e: e.tensor_scalar(out=out_ap, in0=in_ap, scalar1=s1, scalar2=s2, op0=op0, op1=op1),
                    reads=reads, writes=[out_tt])

    def tt(self, eng, out_tt, out_ap, a, b, op, reads):
        self.op(eng, lambda e: e.tensor_tensor(out=out_ap, in0=a, in1=b, op=op), reads=reads, writes=[out_tt])

    def cp(self, eng, out_tt, out_ap, in_ap, reads):
        if eng == "act":
            self.op("act", lambda e: e.copy(out_ap, in_ap), reads=reads, writes=[out_tt])
        else:
            self.op(eng, lambda e: e.tensor_copy(out=out_ap, in_=in_ap), reads=reads, writes=[out_tt])


def bmid(a, m):
    return bass.AP(tensor=a.tensor, offset=a.offset, ap=[list(a.ap[0]), [0, m]] + [list(x) for x in a.ap[1:]])


def blast(a, n):
    return bass.AP(tensor=a.tensor, offset=a.offset, ap=[list(x) for x in a.ap] + [[0, n]])


def bc_ap(ap2d, nparts):
    return bass.AP(tensor=ap2d.tensor, offset=ap2d.offset, ap=[[0, nparts]] + [list(x) for x in ap2d.ap[1:]])


def _bucket(dist):
    n = np.maximum(dist, 0)
    nf = np.maximum(n, 1).astype(np.float32)
    large = 16 + (np.log(nf / np.float32(16)) / np.float32(math.log(128 / 16)) * np.float32(16)).astype(np.int32)
    return np.where(n < 16, n, np.minimum(large, 31))


CF = {}


def _mk_consts():
    p = np.arange(128)[:, None]
    j = np.arange(128)[None, :]
    cols = []

    def add(name, arr):
        CF[name] = (sum(a.shape[1] for a in cols), arr.shape[1])
        cols.append(arr.astype(np.float32))

    add("ident", (p == j))
    add("antiI", (p + j == 127))
    add("U", (p <= j))
    add("ones", np.ones((128, 128)))
    add("TRI", np.where(j >= p, 0.0, -BIG))
    add("TW", np.where(j < p, 0.0, -BIG))
    add("M01", (p + j > 127))
    add("NMC", np.where(p + j > 127, 0.0, -BIG))
    j2 = np.arange(256)[None, :]
    d = j2 - p
    r = np.arange(512)[None, :]
    dc = p - 16 * (r - 255) - 31
    global _CF2_ARR
    _CF2_ARR = np.ascontiguousarray(np.concatenate([np.where(d >= 0, _bucket(d), -1),
                                                    np.where(dc >= 0, _bucket(dc), -1)], axis=1).astype(np.float32))
    jj = np.arange(128)[None, :]
    fb = (jj == 64 + p // 64)
    vb = (jj <= 64 + p // 64)
    add("M1", (vb & ~fb))
    add("CC", 1.0e4 * fb - 1.0 * (~vb))
    cf = np.concatenate(cols, axis=1)
    ex = np.zeros((64, 32, 128), np.float32)
    for kb in range(32):
        for pp in range(128):
            ex[2 * kb + pp // 64, kb, pp] = 1.0
    return np.ascontiguousarray(cf), np.ascontiguousarray(ex.reshape(64, 4096))


_CF_ARR, _EX_ARR = _mk_consts()
NCF = _CF_ARR.shape[1]

_SEGS = [("aq", 0, 256), ("ak", 256, 256), ("bq", 768, 256), ("bk", 1024, 256),
         ("dq", 2308, 256), ("ks", 2692, 64), ("kw", 2820, 64),
         ("cq", 1540, 256), ("ck", 1796, 256),
         ("av", 512, 256), ("bv", 1280, 256),
         ("cv", 2052, 256), ("vs", 2756, 64), ("vw", 2884, 64), ("bf", 1536, 4), ("dg", 2948, 12),
         ("kc", 2564, 64), ("vc", 2628, 64)]
WOFF = {}
_o = 0
for _n, _s, _w in _SEGS:
    WOFF[_n] = _o
    _o += _w
assert _o == PIN


class Prog:
    def __init__(self, dbg=False, nlayers=2, stop_after=None, mixers="ABCD"):
        self.dbg = dbg
        self.mixers = mixers
        self.stop_after = stop_after
        nc = bass.Bass("TRN2", target_bir_lowering=False)
        self.nc = nc
        k = Kern(nc)
        self.k = k
        self.stack = []

        def din(name, shape):
            return nc.dram_tensor(name, list(shape), F32, kind="ExternalInput").ap()

        self.x_in = din("x", [S, D])
        self.c_in = din("c", [8, 128])
        self.I = {}
        for name, shape in [("rel_bias", [1, 256]), ("ada_w", [2, D, 6 * D]), ("ada_b", [2, 48, 128]),
                            ("norm_mix_g", [2, 8, 128]), ("norm_ffn_g", [2, 8, 128]), ("w_in", [2, D, PIN]),
                            ("w_out", [2, D, D]), ("diff_qnorm_g", [2, 1, 32]), ("diff_knorm_g", [2, 1, 32]),
                            ("diff_lambda", [2, 1, 128]), ("diff_subln_g", [2, 1, 64]), ("fox_qnorm_g", [2, 1, 64]),
                            ("fox_knorm_g", [2, 1, 64]), ("fox_b_f", [2, 1, 4]), ("nsa_qnorm_g", [2, 1, 64]),
                            ("nsa_knorm_g", [2, 1, 192]), ("nsa_pe", [2, 2, 32, 64]), ("nsa_phi_w", [2, 2, 2048, 64]),
                            ("ffn_w_up", [2, D, 2 * DFF]), ("ffn_conv_w", [2, 66, 128]), ("ffn_conv_b", [2, 22, 128]),
                            ("ffn_w_down", [2, DFF, D]), ("cf", [128, NCF]), ("cf2", [128, 768]), ("cex", [64, 4096])]:
            self.I[name] = din(name, shape)
        kind = "ExternalOutput" if dbg else "Internal"

        def dscr(name, shape, dt):
            return nc.dram_tensor(name, list(shape), dt, kind=kind).ap()

        self.y = nc.dram_tensor("y", [S, D], F32, kind="ExternalOutput").ap()
        self.y_dep = TT(None)
        self.xs = dscr("xs", [S, D], F32)
        self.xmid = dscr("xmid", [S, D], F32)
        self.qkT = dscr("qkT", [19, 128, S], BF16)
        self.vtm = dscr("vtm", [S, 650], BF16)
        self.cvr = dscr("cvr", [S, 256], BF16)
        self.baug = dscr("baug", [4, 128, S], BF16)
        self.o_d = dscr("o_d", [S, D], BF16)
        self.tts = dscr("tts", [128, 2 * 8 * 256], BF16)
        self.brs = dscr("brs", [128, 4 * 512], F32)
        self.dd = {n: TT(None) for n in ["xs", "xmid", "qkT", "vtm", "cvr", "baug", "o_d", "tts", "brs"]}
        self.ps = [k.ps("ps%d" % i, [128, 512], F32) for i in range(8)]
        self.setup()
        for l in range(nlayers if stop_after != ("setup", 0) else 0):
            xsrc, xsd = (self.x_in, None) if l == 0 else (self.xs, self.dd["xs"])
            self.phase_a(l, xsrc, xsd)
            if stop_after == ("a", l):
                break
            self.phase_b(l)
            if stop_after == ("b", l):
                break
            last = (l == nlayers - 1)
            self.phase_c(l, xsrc, xsd, self.y if last else self.xs, self.y_dep if last else self.dd["xs"])
        outs = [self.y_dep] + (list(self.dd.values()) if dbg else [])
        self.barrier()
        k.finish(outs)

    def push(self):
        self.stack.append([])

    def sb(self, name, shape, dt):
        cm = self.nc.sbuf_tensor(name + "_%d" % len(self.k.pools), shape, dt)
        self.k.pools.append(0)
        t = TT(cm.__enter__())
        self.stack[-1].append(cm)
        return t

    def pop(self):
        self.barrier()
        for cm in reversed(self.stack.pop()):
            cm.__exit__(None, None, None)

    def barrier(self):
        k = self.k
        for e in ["pe", "act", "dve", "pool", "sp"]:
            for e2 in k.sem:
                if k.cnt[e2] > 0 and e2 != e:
                    k._wait(e, k.sem[e2], k.cnt[e2])
            for q in k.dq.values():
                for j in range(NQ):
                    if q["cnt"][j] > 0:
                        k._wait(e, q["sems"][j], 16 * q["cnt"][j])

    def cfs(self, name, a=0, b=None):
        o, w = CF[name]
        if b is None:
            b = w
        return self.cf[:, o + a:o + b]

    def tr(self, ps_tt, out_ap, in_ap, ident_ap, reads):
        self.k.mm(ps_tt, out_ap, in_ap, ident_ap, reads)

    def setup(self):
        k, nc, I = self.k, self.nc, self.I
        self.stack.append([])
        self.cf = self.sb("cf", [128, NCF], F32)
        k.dma("sp", self.cf[:, :], I["cf"][:, :], writes=[self.cf])
        cf = self.cf
        self.identb = self.sb("identb", [128, 128], BF16)
        self.antib = self.sb("antib", [128, 128], BF16)
        self.trib = self.sb("trib", [128, 128], BF16)
        self.twb = self.sb("twb", [128, 128], BF16)
        for t, n in [(self.identb, "ident"), (self.antib, "antiI"), (self.trib, "TRI"), (self.twb, "TW")]:
            k.cp("dve", t, t[:, :], self.cfs(n), [cf])
        self.zerob = self.sb("zerob", [128, 512], BF16)
        k.op("dve", lambda e: e.memset(self.zerob[:, :], 0.0), writes=[self.zerob])
        self.tab = self.sb("tab", [128, 256], F32)
        k.dma("sp", self.tab[:, :], bc_ap(I["rel_bias"], 128), writes=[self.tab])
        self.modT = self.sb("modT", [128, 96], F32)
        self.colv = self.sb("colv", [128, 128], F32)
        self.gates = self.sb("gates", [128, 32, 12], F32)
        self.kcT = self.sb("kcT", [128, 256], BF16)
        self.vcs = self.sb("vcs", [128, 2, 64], BF16)
        self.lam = self.sb("lam", [128, 2], F32)
        self.subg = self.sb("subg", [128, 64], F32)
        self.epsc = self.sb("epsc", [128, 4], F32)
        k.op("dve", lambda e: e.memset(self.epsc[:, 0:1], EPS), writes=[self.epsc])
        k.op("dve", lambda e: e.memset(self.epsc[:, 1:2], 1.0), writes=[self.epsc])
        k.op("dve", lambda e: e.memset(self.epsc[:, 2:3], 1e-30), writes=[self.epsc])
        self.push()
        tabd = self.sb("tabd", [128, 256], F32)
        k.tt("dve", tabd, tabd[:, :].rearrange("p (b h) -> p b h", h=8),
             self.tab[:, :].rearrange("p (b h) -> p b h", h=8), bmid(self.tab[:, 248:256], 32),
             ALU.subtract, [self.tab])
        acc = self.sb("ttacc", [128, 8, 256], F32)
        accb = self.sb("bracc", [128, 4, 512], F32)
        mb = self.sb("mb", [128, 512], F32)
        cf2 = self.sb("cf2", [128, 768], F32)
        k.dma("sp", cf2[:, :], I["cf2"][:, :], writes=[cf2])
        BKa = cf2[:, 0:256]
        BKC = cf2[:, 256:768]
        for h in range(8):
            k.ts("dve", acc, acc[:, h, :], BKa, -1.0, -BIG, ALU.is_equal, ALU.mult, [cf2])
        for h in range(4):
            k.ts("dve", accb, accb[:, h, :], BKC, -1.0, -BIG, ALU.is_equal, ALU.mult, [cf2])
        for b in range(32):
            k.ts("dve", mb, mb[:, 0:256], BKa, float(b), None, ALU.is_equal, None, [cf2])
            for h in range(8):
                k.op("dve", lambda e, h=h, b=b: e.scalar_tensor_tensor(
                    out=acc[:, h, :], in0=mb[:, 0:256], scalar=tabd[:, b * 8 + h:b * 8 + h + 1], in1=acc[:, h, :],
                    op0=ALU.mult, op1=ALU.add), reads=[mb, tabd], writes=[acc])
            k.ts("dve", mb, mb[:, :], BKC, float(b), None, ALU.is_equal, None, [cf2])
            for h in range(4):
                k.op("dve", lambda e, h=h, b=b: e.scalar_tensor_tensor(
                    out=accb[:, h, :], in0=mb[:, :], scalar=self.tab[:, b * 8 + 4 + h:b * 8 + 5 + h], in1=accb[:, h, :],
                    op0=ALU.mult, op1=ALU.add), reads=[mb, self.tab], writes=[accb])
        tthi = self.sb("tthi", [128, 8 * 256], BF16)
        ttlo = self.sb("ttlo", [128, 8 * 256], BF16)
        accf = acc[:, :, :].rearrange("p h j -> p (h j)")
        k.cp("dve", tthi, tthi[:, :], accf, [acc])
        k.tt("dve", acc, accf, accf, tthi[:, :], ALU.subtract, [tthi])
        k.cp("dve", ttlo, ttlo[:, :], accf, [acc])
        k.dma("pool", self.tts[:, 0:2048], tthi[:, :], reads=[tthi], writes=[self.dd["tts"]])
        k.dma("pool", self.tts[:, 2048:4096], ttlo[:, :], reads=[ttlo], writes=[self.dd["tts"]])
        k.dma("pool", self.brs[:, :], accb[:, :, :].rearrange("p h j -> p (h j)"), reads=[accb], writes=[self.dd["brs"]])
        st = self.sb("colst", [128, 128], F32)
        k.dma("sp", st[0:48, :], I["ada_b"][0], writes=[st])
        k.dma("sp", st[48:96, :], I["ada_b"][1], writes=[st])
        k.dma("sp", st[96:104, :], I["norm_mix_g"][0], writes=[st])
        k.dma("sp", st[104:112, :], I["norm_mix_g"][1], writes=[st])
        k.dma("sp", st[112:120, :], I["norm_ffn_g"][0], writes=[st])
        k.dma("sp", st[120:128, :], I["norm_ffn_g"][1], writes=[st])
        ps0 = self.ps[0]
        self.tr(ps0, ps0[:, 0:128], st[:, :], self.cfs("ident"), [st, cf])
        k.cp("dve", self.colv, self.colv[:, :], ps0[:, 0:128], [ps0])
        c8 = self.sb("c8", [8, 128], F32)
        k.dma("sp", c8[:, :], self.c_in[:, :], writes=[c8])
        ps1 = self.ps[1]
        self.tr(ps1, ps1[:, 0:8], c8[:, :], self.cfs("ident")[0:8, 0:8], [c8, cf])
        cact = self.sb("cact", [128, 8], F32)
        k.act(cact, cact[:, :], ps1[:, 0:8], AF.Silu, [ps1])
        wb = [self.sb("adaw%d" % i, [128, 8, 512], F32) for i in range(2)]
        ps2 = self.ps[2]
        for l in range(2):
            for blk in range(12):
                w = wb[blk % 2]
                src = I["ada_w"][l].rearrange("(kc p) n -> p kc n", p=128)
                for kc in range(8):
                    k.dma("sp" if kc % 2 == 0 else "act", w[:, kc, :], src[:, kc, blk * 512:(blk + 1) * 512], writes=[w])
                for jj in range(4):
                    j = l * 48 + blk * 4 + jj
                    for kc in range(8):
                        k.mm(ps2, ps2[:, j:j + 1], w[:, kc, jj * 128:(jj + 1) * 128], cact[:, kc:kc + 1], [w, cact],
                             start=(kc == 0), stop=(kc == 7))
        k.tt("dve", self.modT, self.modT[:, :], ps2[:, 0:96], self.colv[:, 0:96], ALU.add, [ps2, self.colv])
        self.pop()

    def modcol(self, l, which):
        o = l * 48 + which * 8
        return self.modT[:, o:o + 8]

    def phase_a(self, l, xsrc, xsd):
        k, I, cf = self.k, self.I, self.cf
        ps = self.ps
        ident = self.cfs("ident")
        self.push()
        gsm = self.sb("gsm", [128, 644], F32)
        go = {}
        o = 0
        for name, n in [("diff_qnorm_g", 32), ("diff_knorm_g", 32), ("fox_qnorm_g", 64), ("fox_knorm_g", 64),
                        ("nsa_qnorm_g", 64), ("nsa_knorm_g", 192), ("fox_b_f", 4), ("diff_subln_g", 64),
                        ("diff_lambda", 128)]:
            k.dma("sp", gsm[:, o:o + n], bc_ap(I[name][l], 128), writes=[gsm])
            go[name] = o
            o += n

        def gs_(name, a, b):
            return gsm[:, go[name] + a:go[name] + b]
        gA = self.sb("gA", [128, 512], F32)
        gB = self.sb("gB", [128, 512], F32)
        gD = self.sb("gD", [128, 384], F32)
        k.ts("dve", gA, gA[:, 0:256].rearrange("p (g e) -> p g e", e=32), bmid(gs_("diff_qnorm_g", 0, 32), 8),
             32 ** -0.5, None, ALU.mult, None, [gsm])
        k.cp("dve", gA, gA[:, 256:512].rearrange("p (g e) -> p g e", e=32), bmid(gs_("diff_knorm_g", 0, 32), 8), [gsm])
        k.ts("dve", gB, gB[:, 0:256].rearrange("p (g e) -> p g e", e=64), bmid(gs_("fox_qnorm_g", 0, 64), 4),
             0.125, None, ALU.mult, None, [gsm])
        k.cp("dve", gB, gB[:, 256:512].rearrange("p (g e) -> p g e", e=64), bmid(gs_("fox_knorm_g", 0, 64), 4), [gsm])
        k.ts("dve", gD, gD[:, 0:256].rearrange("p (g e) -> p g e", e=64), bmid(gs_("nsa_qnorm_g", 0, 64), 4),
             0.125, None, ALU.mult, None, [gsm])
        k.cp("dve", gD, gD[:, 256:384], gs_("nsa_knorm_g", 64, 192), [gsm])
        lam_init = 0.8 - 0.6 * math.exp(-0.3 * l)
        self.lam_init = lam_init
        subg = self.sb("subgl", [128, 64], F32)
        k.ts("dve", subg, subg[:, :], gs_("diff_subln_g", 0, 64), 1.0 - lam_init, None, ALU.mult, None, [gsm])
        lt = self.sb("lt", [128, 64], F32)
        ls = self.sb("ls", [128, 4], F32)
        lv = gs_("diff_lambda", 0, 128).rearrange("p (a b e) -> p a b e", a=2, b=2, e=32)
        k.tt("dve", lt, lt[:, :].rearrange("p (a e) -> p a e", e=32), lv[:, :, 0, :], lv[:, :, 1, :], ALU.mult, [gsm])
        k.op("dve", lambda e: e.tensor_reduce(out=ls[:, 0:2], in_=lt[:, :].rearrange("p (a e) -> p a e", e=32),
                                              axis=AX.X, op=ALU.add), reads=[lt], writes=[ls])
        k.act(ls, ls[:, 2:4], ls[:, 0:2], AF.Exp, [ls])
        k.tt("dve", ls, ls[:, 0:1], ls[:, 2:3], ls[:, 3:4], ALU.subtract, [ls])
        k.ts("dve", self.lam, self.lam[:, l:l + 1], ls[:, 0:1], lam_init, None, ALU.add, None, [ls])
        ab = self.sb("ab", [128, 16], F32)
        k.ts("dve", ab, ab[:, 0:8], self.modcol(l, 1), 1.0, None, ALU.add, None, [self.modT])
        k.tt("dve", ab, ab[:, 0:8], ab[:, 0:8], self.colv[:, 96 + 8 * l:104 + 8 * l], ALU.mult, [self.colv])
        k.cp("dve", ab, ab[:, 8:16], self.modcol(l, 0), [self.modT])
        sq = self.sb("sq", [128, D], F32)
        qn = self.sb("qn", [128, 512], F32)
        ssg = self.sb("ssg", [128, 48], F32)
        rawT = self.sb("rawT", [128, S + 16], F32)
        fl = self.sb("fl", [128, 32, 4], F32)
        gl = self.sb("gl", [128, 32, 12], F32)
        self.push()
        w = self.sb("win", [128, 8, PIN], BF16)
        src = I["w_in"][l].rearrange("(kc p) n -> p kc n", p=128)
        for name, s0, wd in _SEGS:
            o = WOFF[name]
            k.dma("pool", w[:, :, o:o + wd], src[:, :, s0:s0 + wd], writes=[w])
        xts = [self.sb("xt%d" % i, [128, D], F32) for i in range(2)]
        xn = self.sb("xn", [128, D], F32)
        sss = [self.sb("ss%d" % i, [128, 4], F32) for i in range(2)]
        hTs = [self.sb("hT%d" % i, [128, 8, 128], BF16) for i in range(2)]
        stgs = [self.sb("stg%d" % i, [128, 18 * 128], BF16) for i in range(2)]
        qsts = [self.sb("qst%d" % i, [128, 18, 512], BF16) for i in range(2)]
        vsts = [self.sb("vst%d" % i, [128, 650], BF16) for i in range(2)]
        cvb = self.sb("cvb", [128, 256], BF16)
        cvst = [self.sb("cvst%d" % i, [128, 256], BF16) for i in range(2)]
        for s_ in stgs:
            k.op("pool", lambda e, s_=s_: e.memset(s_[:, :], 0.0), writes=[s_])
        for v_ in vsts:
            k.op("pool", lambda e, v_=v_: e.memset(v_[:, :], 1.0), writes=[v_])
        k.op("pool", lambda e: e.memset(rawT[:, S:S + 16], 0.0), writes=[rawT])
        qkTv = self.qkT.rearrange("c p s -> p c s")
        qd = self.dd["qkT"]

        def normev(bank, n, gs):
            ng = n // gs
            k.act(sq, sq[:, 0:n], bank[:, 0:n], AF.Square, [bank])
            k.op("dve", lambda e: e.tensor_reduce(out=ssg[:, 0:ng], in_=sq[:, 0:n].rearrange("p (g e) -> p g e", e=gs),
                                                  axis=AX.X, op=ALU.add), reads=[sq], writes=[ssg])
            k.act(ssg, ssg[:, 16:16 + ng], ssg[:, 0:ng], AF.Ln, [ssg], bias=self.epsc[:, 0:1], scale=1.0 / gs)
            k.act(ssg, ssg[:, 32:32 + ng], ssg[:, 16:16 + ng], AF.Exp, [ssg], scale=-0.5)
            k.tt("dve", qn, qn[:, 0:n].rearrange("p (g e) -> p g e", e=gs),
                 bank[:, 0:n].rearrange("p (g e) -> p g e", e=gs), blast(ssg[:, 32:32 + ng], gs), ALU.mult, [bank, ssg])

        def front(t):
            xt = xts[t % 2]
            ss = sss[t % 2]
            hT = hTs[t % 2]
            stg = stgs[t % 2]
            vst = vsts[t % 2]
            qst = qsts[(t // 4) % 2]
            k.dma("sp", xt[:, :], xsrc[t * 128:(t + 1) * 128, :], reads=[xsd] if xsd else [], writes=[xt])
            k.act(sq, sq[:, :], xt[:, :], AF.Square, [xt], accum=ss[:, 0:1], extra_w=[ss])
            k.act(ss, ss[:, 1:2], ss[:, 0:1], AF.Ln, [ss], bias=self.epsc[:, 0:1], scale=1.0 / D)
            k.act(ss, ss[:, 2:3], ss[:, 1:2], AF.Exp, [ss], scale=-0.5)
            k.ts("dve", xn, xn[:, :], xt[:, :], ss[:, 2:3], None, ALU.mult, None, [xt, ss])
            for c in range(8):
                b = ps[c // 4]
                self.tr(b, b[:, (c % 4) * 128:(c % 4 + 1) * 128], xn[:, c * 128:(c + 1) * 128], ident, [xn, cf])
            for c in range(8):
                b = ps[c // 4]
                pin = b[:, (c % 4) * 128:(c % 4 + 1) * 128]
                if c % 2 == 0:
                    k.ts("dve", hT, hT[:, c, :], pin, ab[:, c:c + 1], ab[:, 8 + c:9 + c], ALU.mult, ALU.add, [b, ab])
                else:
                    k.act(hT, hT[:, c, :], pin, AF.Identity, [b, ab], bias=ab[:, 8 + c:9 + c], scale=ab[:, c:c + 1])

            def proj(bank, a, n):
                for c in range(8):
                    k.mm(bank, bank[:, 0:n], hT[:, c, :], w[:, c, a:a + n], [hT, w], start=(c == 0), stop=(c == 7))
            proj(ps[2], 0, 512)
            normev(ps[2], 512, 32)
            for m in range(2):
                srcq = qn[:, 0:256].rearrange("p (hh hl m e) -> p hh hl m e", hh=2, hl=2, m=2)[:, :, :, m, :]
                gq = gA[:, 0:256].rearrange("p (hh hl m e) -> p hh hl m e", hh=2, hl=2, m=2)[:, :, :, m, :]
                dst = stg[:, 0:512].rearrange("p (hh m hl f) -> p hh m hl f", hh=2, m=2, hl=2)[:, :, m, :, 32 * m:32 * m + 32]
                k.tt("pool", stg, dst, srcq, gq, ALU.mult, [qn, gA])
            k.tt("pool", stg, stg[:, 512:768], qn[:, 256:512], gA[:, 256:512], ALU.mult, [qn, gA])
            proj(ps[3], 512, 512)
            normev(ps[3], 512, 64)
            k.tt("pool", stg, stg[:, 768:1280], qn[:, 0:512], gB[:, :], ALU.mult, [qn, gB])
            proj(ps[4], 1024, 384)
            normev(ps[4], 384, 64)
            k.tt("pool", stg, stg[:, 1792:2048], qn[:, 0:256], gD[:, 0:256], ALU.mult, [qn, gD])
            for dup in range(2):
                k.tt("pool", stg, stg[:, 2048 + dup * 64:2112 + dup * 64], qn[:, 256:320], gD[:, 256:320], ALU.mult, [qn, gD])
                k.tt("pool", stg, stg[:, 2176 + dup * 64:2240 + dup * 64], qn[:, 320:384], gD[:, 320:384], ALU.mult, [qn, gD])
            proj(ps[5], 1408, 512)
            k.act(stg, stg[:, 1280:1536], ps[5][:, 0:256], AF.Copy, [ps[5]], scale=0.125)
            k.cp("act", stg, stg[:, 1536:1792], ps[5][:, 256:512], [ps[5]])
            proj(ps[2], 1920, 512)
            k.cp("act", vst, vst[:, 0:520].rearrange("p (h e) -> p h e", e=65)[:, :, 0:64],
                 ps[2][:, 0:512].rearrange("p (h e) -> p h e", e=64), [ps[2]])
            proj(ps[3], 2432, 400)
            k.cp("dve", cvb, cvb[:, :], ps[3][:, 0:256], [ps[3]])
            k.cp("dve", vst, vst[:, 520:584], ps[3][:, 256:320], [ps[3]])
            k.cp("dve", vst, vst[:, 585:649], ps[3][:, 320:384], [ps[3]])
            k.cp("dve", fl, fl[:, t, :], ps[3][:, 384:388], [ps[3]])
            k.cp("dve", gl, gl[:, t, :], ps[3][:, 388:400], [ps[3]])
            k.dma("pool", self.vtm[t * 128:(t + 1) * 128, :], vst[:, :], reads=[vst], writes=[self.dd["vtm"]])
            k.mm(ps[4], ps[4][:, 0:256], self.antib[:, :], cvb[:, :], [cvb, self.antib])
            cvs = cvst[t % 2]
            k.cp("act", cvs, cvs[:, :], ps[4][:, 0:256], [ps[4]])
            k.dma("pool", self.cvr[(31 - t) * 128:(32 - t) * 128, :], cvs[:, :], reads=[cvs], writes=[self.dd["cvr"]])
            for c in range(8):
                k.mm(ps[5], ps[5][:, 0:128], w[:, c, 2832:2960], hT[:, c, :], [hT, w], start=(c == 0), stop=(c == 7))
            k.cp("act", rawT, rawT[:, t * 128:(t + 1) * 128], ps[5][:, 0:128], [ps[5]])
        def back(t):
            stg = stgs[t % 2]
            qst = qsts[(t // 4) % 2]
            for g4 in range(5):
                chs = list(range(g4 * 4, min(g4 * 4 + 4, 18)))
                bank = [ps[6], ps[7]][(t * 5 + g4) % 2]
                for ci, ch in enumerate(chs):
                    idn = self.antib if ch in (12, 13) else self.identb
                    k.mm(bank, bank[:, ci * 128:(ci + 1) * 128], stg[:, ch * 128:(ch + 1) * 128], idn[:, :], [stg, idn])
                n = len(chs)
                for ci, ch in enumerate(chs):
                    slot = (3 - t % 4) if ch in (12, 13) else (t % 4)
                    eng = "act" if ci % 2 == 0 else "dve"
                    k.cp(eng, qst, qst[:, ch, slot * 128:(slot + 1) * 128], bank[:, ci * 128:(ci + 1) * 128], [bank])
            if t % 4 == 3:
                g = t // 4
                k.dma("pool", qkTv[:, 0:12, g * 512:(g + 1) * 512], qst[:, 0:12, :], reads=[qst], writes=[qd])
                k.dma("pool", qkTv[:, 12:14, (7 - g) * 512:(8 - g) * 512], qst[:, 12:14, :], reads=[qst], writes=[qd])
                k.dma("pool", qkTv[:, 14:18, g * 512:(g + 1) * 512], qst[:, 14:18, :], reads=[qst], writes=[qd])

        for t in range(NT + 1):
            if t < NT:
                front(t)
            if t >= 1:
                back(t - 1)
        self.pop()
        glf = gl[:, :, :].rearrange("p t g -> p (t g)")
        gaf = self.gates[:, :, :].rearrange("p t g -> p (t g)")
        k.act(self.gates, gaf, glf, AF.Exp, [gl], scale=-1.0)
        k.ts("dve", self.gates, gaf, gaf, 1.0, None, ALU.add, None, [self.gates])
        k.op("dve", lambda e: e.reciprocal(out=gaf, in_=gaf), reads=[self.gates], writes=[self.gates])
        sp = self.sb("fsp", [128, 128], F32)
        spv = sp[:, :].rearrange("p (t h) -> p t h", h=4)
        k.tt("dve", sp, spv, fl[:, :, :], bmid(gs_("fox_b_f", 0, 4), 32), ALU.add, [fl, gsm])
        k.act(sp, sp[:, :], sp[:, :], AF.Exp, [sp], scale=-1.0)
        k.act(sp, sp[:, :], sp[:, :], AF.Ln, [sp], bias=self.epsc[:, 1:2])
        k.mm(ps[0], ps[0][:, 0:128], self.cfs("U"), sp[:, :], [sp, cf])
        k.mm(ps[1], ps[1][:, 0:128], self.cfs("ones"), sp[:, :], [sp, cf])
        tot = self.sb("ftot", [128, 128], F32)
        inc = self.sb("finc", [128, 128], F32)
        cn = self.sb("fcn", [128, 128], F32)
        k.cp("dve", tot, tot[:, :], ps[1][:, 0:128], [ps[1]])
        totv = tot[:, :].rearrange("p (t h) -> p t h", h=4)
        incv = inc[:, :].rearrange("p (t h) -> p t h", h=4)
        for h in range(4):
            k.op("dve", lambda e, h=h: e.tensor_tensor_scan(out=incv[:, :, h], data0=totv[:, :, h],
                                                            data1=self.zerob[:, 0:32], initial=0.0,
                                                            op0=ALU.add, op1=ALU.add), reads=[tot, cf], writes=[inc])
        k.tt("dve", inc, inc[:, :], inc[:, :], tot[:, :], ALU.subtract, [tot])
        k.tt("dve", cn, cn[:, :], ps[0][:, 0:128], inc[:, :], ALU.add, [ps[0], inc])
        parts = [self.sb("fpart%d" % i, [128, 128], BF16) for i in range(3)]
        k.cp("dve", parts[0], parts[0][:, :], cn[:, :], [cn])
        k.tt("dve", cn, cn[:, :], cn[:, :], parts[0][:, :], ALU.subtract, [parts[0]])
        k.cp("dve", parts[1], parts[1][:, :], cn[:, :], [cn])
        k.tt("dve", cn, cn[:, :], cn[:, :], parts[1][:, :], ALU.subtract, [parts[1]])
        k.cp("dve", parts[2], parts[2][:, :], cn[:, :], [cn])
        aug = [self.sb("aug%d" % i, [128, 32, 128], BF16) for i in range(4)]
        for a_ in aug:
            k.op("pool", lambda e, a_=a_: e.memset(a_[:, :, :], 0.0), writes=[a_])
        for pair in range(2):
            aq, ak = aug[pair], aug[2 + pair]
            for hl in range(2):
                h = 2 * pair + hl
                for r in range(3):
                    pv = parts[r][:, :].rearrange("p (t h) -> p t h", h=4)[:, :, h]
                    k.ts("dve", aq, aq[:, :, hl * 64 + r], pv, -1.0, None, ALU.mult, None, [parts[r]])
                    k.cp("dve", ak, ak[:, :, hl * 64 + 3 + r], pv, [parts[r]])
                k.op("pool", lambda e, aq=aq, hl=hl: e.memset(aq[:, :, hl * 64 + 3:hl * 64 + 6], 1.0), writes=[aq])
                k.op("pool", lambda e, ak=ak, hl=hl: e.memset(ak[:, :, hl * 64:hl * 64 + 3], 1.0), writes=[ak])
        ast = [self.sb("augst%d" % i, [128, 512], BF16) for i in range(2)]
        n_ = 0
        for ai in range(4):
            for g in range(8):
                bank = ps[2 + n_ % 2]
                st_ = ast[n_ % 2]
                n_ += 1
                for tt_ in range(4):
                    k.mm(bank, bank[:, tt_ * 128:(tt_ + 1) * 128], aug[ai][:, g * 4 + tt_, :], self.identb[:, :],
                         [aug[ai], self.identb])
                k.cp("act", st_, st_[:, :], bank[:, :], [bank])
                k.dma("pool", self.baug[ai, :, g * 512:(g + 1) * 512], st_[:, :], reads=[st_], writes=[self.dd["baug"]])
        phibd = self.sb("phibd", [128, 32, 128], BF16)
        k.op("pool", lambda e: e.memset(phibd[:, :, :], 0.0), writes=[phibd])
        for wi in range(2):
            k.dma("pool", phibd[wi * 64:(wi + 1) * 64, :, wi * 64:(wi + 1) * 64],
                  I["nsa_phi_w"][l, wi].rearrange("(r d) e -> d r e", d=64), writes=[phibd])
        pst = self.sb("pest", [32, 128], F32)
        for wi in range(2):
            k.dma("sp", pst[:, wi * 64:(wi + 1) * 64], I["nsa_pe"][l, wi], writes=[pst])
        self.tr(ps[4], ps[4][:, 0:32], pst[:, :], self.cfs("ident")[0:32, 0:32], [pst, cf])
        peT = self.sb("peT", [128, 32], F32)
        k.cp("dve", peT, peT[:, :], ps[4][:, 0:32], [ps[4]])
        tmps = [self.sb("ctmp%d" % i, [128, 256], BF16) for i in range(4)]
        for t_ in tmps:
            k.op("pool", lambda e, t_=t_: e.memset(t_[:, :], 0.0), writes=[t_])
        r0 = rawT[:, 0:1]
        for r in range(32):
            tm = tmps[r % 4]
            src_ = bass.AP(tensor=r0.tensor, offset=r0.offset + r, ap=[list(r0.ap[0]), [16, 255]])
            k.ts("dve", tm, tm[:, 0:255], src_, peT[:, r:r + 1], None, ALU.add, None, [rawT, peT])
            for nb in range(2):
                k.mm(ps[5 + nb], ps[5 + nb][:, 0:128], tm[:, nb * 128:(nb + 1) * 128], phibd[:, r, :], [tm, phibd],
                     start=(r == 0), stop=(r == 31))
        kcn = self.sb("kcn", [128, 2, 128], BF16)
        for nb in range(2):
            bank = ps[5 + nb]
            k.act(sq, sq[:, 0:64], bank[:, 0:64], AF.Square, [bank], accum=ssg[:, 0:1], extra_w=[ssg])
            k.act(ssg, ssg[:, 1:2], ssg[:, 0:1], AF.Ln, [ssg], bias=self.epsc[:, 0:1], scale=1.0 / 64)
            k.act(ssg, ssg[:, 2:3], ssg[:, 1:2], AF.Exp, [ssg], scale=-0.5)
            k.ts("dve", qn, qn[:, 0:64], bank[:, 0:64], ssg[:, 2:3], None, ALU.mult, None, [bank, ssg])
            for dup in range(2):
                k.tt("dve", kcn, kcn[:, nb, dup * 64:(dup + 1) * 64], qn[:, 0:64], gs_("nsa_knorm_g", 0, 64), ALU.mult, [qn, gsm])
            k.cp("dve", self.vcs, self.vcs[:, nb, :], bank[:, 64:128], [bank])
        for nb in range(2):
            k.mm(ps[7], ps[7][:, nb * 128:(nb + 1) * 128], kcn[:, nb, :], self.identb[:, :], [kcn, self.identb])
        k.cp("dve", self.kcT, self.kcT[:, :], ps[7][:, 0:256], [ps[7]])
        k.cp("dve", self.subg, self.subg[:, :], subg[:, :], [subg])
        if self.dbg:
            k.dma("pool", self.qkT[18, :, 0:256], self.kcT[:, :], reads=[self.kcT], writes=[qd])
        self.pop()


def make_inputs(inp, b):
    f = np.float32
    m = {
        "x": np.ascontiguousarray(inp["x"][b], dtype=f),
        "c": np.ascontiguousarray(inp["c"][b].reshape(8, 128), dtype=f),
        "rel_bias": np.ascontiguousarray(inp["rel_bias"].reshape(1, 256), dtype=f),
        "ada_w": np.ascontiguousarray(inp["ada_w"], dtype=f),
        "ada_b": np.ascontiguousarray(inp["ada_b"].reshape(2, 48, 128), dtype=f),
        "norm_mix_g": np.ascontiguousarray(inp["norm_mix_g"].reshape(2, 8, 128), dtype=f),
        "norm_ffn_g": np.ascontiguousarray(inp["norm_ffn_g"].reshape(2, 8, 128), dtype=f),
        "w_in": np.ascontiguousarray(inp["w_in"], dtype=f),
        "w_out": np.ascontiguousarray(inp["w_out"], dtype=f),
        "diff_qnorm_g": inp["diff_qnorm_g"].reshape(2, 1, 32), "diff_knorm_g": inp["diff_knorm_g"].reshape(2, 1, 32),
        "diff_lambda": inp["diff_lambda"].reshape(2, 1, 128), "diff_subln_g": inp["diff_subln_g"].reshape(2, 1, 64),
        "fox_qnorm_g": inp["fox_qnorm_g"].reshape(2, 1, 64), "fox_knorm_g": inp["fox_knorm_g"].reshape(2, 1, 64),
        "fox_b_f": inp["fox_b_f"].reshape(2, 1, 4), "nsa_qnorm_g": inp["nsa_qnorm_g"].reshape(2, 1, 64),
        "nsa_knorm_g": inp["nsa_knorm_g"].reshape(2, 1, 192), "nsa_pe": inp["nsa_pe"], "nsa_phi_w": inp["nsa_phi_w"],
        "ffn_w_up": inp["ffn_w_up"], "ffn_conv_w": inp["ffn_conv_w"].reshape(2, 66, 128),
        "ffn_conv_b": inp["ffn_conv_b"].reshape(2, 22, 128), "ffn_w_down": inp["ffn_w_down"],
        "cf": _CF_ARR, "cf2": _CF2_ARR, "cex": _EX_ARR,
    }
    return {k_: np.ascontiguousarray(v, dtype=f) for k_, v in m.items()}


def _phase_b(self, l):
    k, I, cf, ps = self.k, self.I, self.cf, self.ps
    self.push()
    tthi = self.sb("tthi", [128, 8, 256], BF16)
    ttlo = self.sb("ttlo", [128, 8, 256], BF16)
    brel = self.sb("brel", [128, 4, 512], F32)
    k.dma("sp", tthi[:, :, :].rearrange("p h j -> p (h j)"), self.tts[:, 0:2048], reads=[self.dd["tts"]], writes=[tthi])
    k.dma("sp", ttlo[:, :, :].rearrange("p h j -> p (h j)"), self.tts[:, 2048:4096], reads=[self.dd["tts"]], writes=[ttlo])
    k.dma("sp", brel[:, :, :].rearrange("p h j -> p (h j)"), self.brs[:, :], reads=[self.dd["brs"]], writes=[brel])
    pTs = [self.sb("pT%d" % i, [128, 512], BF16) for i in range(6)]
    oTs = [self.sb("oTs%d" % i, [128, 512], F32) for i in range(2)]
    osts = [self.sb("ost%d" % i, [128, 4, 64], BF16) for i in range(4)]
    rzs = [self.sb("rz%d" % i, [128, 8], F32) for i in range(2)]
    ons = [self.sb("on%d" % i, [128, 4, 64], F32) for i in range(2)]
    sqss = [self.sb("sqs%d" % i, [128, 4, 64], F32) for i in range(2)]
    qkTv = self.qkT
    qd = self.dd["qkT"]
    ident = self.cfs("ident")
    cnt = {"s": 0, "c": 0, "o": 0}

    def load_chunk(name, ch, src=None, dep=None):
        t = self.sb(name, [128, S], BF16)
        s_ = qkTv[ch] if src is None else src
        for hh in range(2):
            k.dma("sp" if hh == 0 else "act", t[:, hh * 2048:(hh + 1) * 2048], s_[:, hh * 2048:(hh + 1) * 2048],
                  reads=[qd if dep is None else dep], writes=[t])
        return t

    def load_rows(t, r0, src_rows, dep=None, q="sp"):
        n_ = src_rows.shape[0]
        for hh in range(2):
            k.dma(q if hh == 0 else "act", t[r0:r0 + n_, hh * 2048:(hh + 1) * 2048], src_rows[:, hh * 2048:(hh + 1) * 2048],
                  reads=[qd if dep is None else dep], writes=[t])

    def padded(name, parts):
        t = self.sb(name, [128, S], BF16)
        k.op("pool", lambda e: e.memset(t[:, :], 0.0), writes=[t])
        for (r0, rows, dep) in parts:
            load_rows(t, r0, rows, dep)
        return t
    self.padded = padded
    self.load_rows = load_rows

    def load_vaug(name, c0, nh):
        t = self.sb(name, [128, 32, nh * 65], BF16)
        k.dma("sp", t[:, :, :], self.vtm[:, c0:c0 + nh * 65].rearrange("(t p) c -> p t c", p=128),
              reads=[self.dd["vtm"]], writes=[t])
        return t

    NS = 4
    NPT = 6
    sbanks = [ps[0], ps[1], ps[2], ps[4]]
    jobs2 = [[], []]

    def softmax_head(qT, qdeps, kT, kdeps, vaug, vdeps, bias_h, act_bias, window, extras, out_cb, slot=0):
        jobs = jobs2[slot]
        tcount = 0
        for c in range(8):
            kbs = list(range(max(0, 4 * c - 4) if window else 0, 4 * c + 4))
            touched = set()
            if window:
                touched = {0, 1, 2, 3}
            acc = [[ps[6], ps[3]], [ps[7], ps[5]]][slot][c % 2]
            for kb in kbs:
                d0 = c * 512 - kb * 128
                segs = [s for s in range(4) if d0 + 128 * s >= 0 and (not window or d0 + 128 * s <= 512)]
                groups = []
                for s in segs:
                    ft = s not in touched
                    touched.add(s)
                    if groups and groups[-1][2] == ft:
                        groups[-1][1] = s
                    else:
                        groups.append([s, s, ft])
                jlo, jhi = segs[0] * 128, (segs[-1] + 1) * 128
                Sb = sbanks[(2 * tcount + slot) % NS]
                pT = pTs[(2 * tcount + slot) % NPT]
                tcount += 1
                first, last = (kb == kbs[0]), (kb == kbs[-1])

                def s0(c=c, kb=kb, d0=d0, segs=segs, Sb=Sb, jlo=jlo, jhi=jhi):
                    mms = [(Sb[:, jlo:jhi], kT[:, kb * 128:(kb + 1) * 128], qT[:, c * 512 + jlo:c * 512 + jhi], qdeps + kdeps)]
                    for (lf, rf, deps) in extras:
                        mms.append((Sb[:, jlo:jhi], lf(kb), rf(c * 512 + jlo, c * 512 + jhi), deps))
                    for s in segs:
                        dl = d0 + 128 * s
                        seg = Sb[:, s * 128:(s + 1) * 128]
                        if bias_h is not None and dl == 0 and (s + 1) in segs:
                            seg2 = Sb[:, s * 128:(s + 2) * 128]
                            mms.append((seg2, self.identb[:, :], tthi[:, bias_h, 0:256], [self.identb, tthi]))
                            mms.append((seg2, self.identb[:, :], ttlo[:, bias_h, 0:256], [self.identb, ttlo]))
                        elif bias_h is not None and dl == 128 and (s - 1) in segs:
                            pass
                        elif bias_h is not None and dl in (0, 128):
                            mms.append((seg, self.identb[:, :], tthi[:, bias_h, dl:dl + 128], [self.identb, tthi]))
                            mms.append((seg, self.identb[:, :], ttlo[:, bias_h, dl:dl + 128], [self.identb, ttlo]))
                        elif bias_h is None and dl == 0:
                            mms.append((seg, self.identb[:, :], self.trib[:, :], [self.identb, self.trib]))
                        if window and dl == 512:
                            mms.append((seg, self.identb[:, :], self.twb[:, :], [self.identb, self.twb]))
                    for i, (o_, a_, b_, deps) in enumerate(mms):
                        k.mm(Sb, o_, a_, b_, deps, start=(i == 0), stop=(i == len(mms) - 1))

                def s1(Sb=Sb, pT=pT, jlo=jlo, jhi=jhi):
                    if act_bias is None:
                        k.act(pT, pT[:, jlo:jhi], Sb[:, jlo:jhi], AF.Exp, [Sb])
                    else:
                        k.act(pT, pT[:, jlo:jhi], Sb[:, jlo:jhi], AF.Exp, [Sb, self.tab], bias=act_bias)

                def s2(kb=kb, segs=segs, pT=pT, acc=acc, first=first, last=last):
                    if first:
                        k.mm(acc, acc[:, 0:260], self.zerob[:, 0:128], self.zerob[:, 0:260], [self.zerob], start=True, stop=False)
                    for s in segs:
                        k.mm(acc, acc[:, s * 65:(s + 1) * 65], pT[:, s * 128:(s + 1) * 128], vaug(kb), [pT] + vdeps,
                             start=False, stop=last)
                stages = [s0, s1, s2]
                if last:
                    def s3(c=c, acc=acc):
                        out_cb(c, acc, acc[:, 0:260].rearrange("p (s e) -> p s e", e=65))
                    stages += [s3]
                jobs.append(stages)

    def run_jobs():
        ja, jb = jobs2
        merged = []
        for i in range(max(len(ja), len(jb))):
            sa = ja[i] if i < len(ja) else []
            sb_ = jb[i] if i < len(jb) else []
            st = []
            for si in range(max(len(sa), len(sb_))):
                fa = sa[si] if si < len(sa) else None
                fb = sb_[si] if si < len(sb_) else None
                st.append(lambda fa=fa, fb=fb: ((fa() if fa else None), (fb() if fb else None)))
            merged.append(st)
        run_staged(merged)
        del ja[:]
        del jb[:]
    self.run_jobs = run_jobs

    def normalize(bank, v3, dst_tt, dst_ap, slot=0):
        rz = rzs[slot]
        k.op("dve", lambda e: e.reciprocal(out=rz[:, 0:4], in_=v3[:, :, 64]), reads=[bank], writes=[rz])
        k.tt("dve", dst_tt, dst_ap, v3[:, :, 0:64], blast(rz[:, 0:4], 64), ALU.mult, [bank, rz])

    def store_o(c, col, src_tt, src_ap):
        ost = osts[cnt["o"] % 4]
        cnt["o"] += 1
        k.cp("act", ost, ost[:, :, :], src_ap, [src_tt])
        k.dma("pool", self.o_d[c * 512:(c + 1) * 512, col:col + 64].rearrange("(s p) e -> p s e", p=128), ost[:, :, :],
              reads=[ost], writes=[self.dd["o_d"]])

    if "A" in self.mixers:
        self.push()
        kA = [load_chunk("kA%d" % i, 4 + i) for i in range(2)]
        vA = load_vaug("vA", 0, 4)
        o1s = [self.sb("o1all%d" % i, [128, 32, 64], F32) for i in range(2)]
        nlam = self.sb("nlam", [128, 1], F32)
        k.ts("dve", nlam, nlam[:, :], self.lam[:, l:l + 1], -1.0, None, ALU.mult, None, [self.lam])
        qpad = {}
        for hp in range(2):
            for m in range(2):
                for slot in range(2):
                    pb = 64 * slot
                    qpad[(hp, m, slot)] = padded("qAp%d%d%d" % (hp, m, slot), [(pb, qkTv[2 * hp + m][pb:pb + 64, :], None)])
        for hp in range(2):
            for m in range(2):
                for slot in range(2):
                    h = 2 * hp + slot
                    pb = 64 * slot
                    qt = qpad[(hp, m, slot)]
                    kt = kA[hp]

                    def cb(c, bank, v3, h=h, m=m, slot=slot):
                        o1, on, sqs, rz = o1s[slot], ons[slot], sqss[slot], rzs[slot]
                        if m == 0:
                            normalize(bank, v3, o1, o1[:, 4 * c:4 * c + 4, :], slot)
                        else:
                            normalize(bank, v3, on, on[:, :, :], slot)
                            k.op("dve", lambda e: e.scalar_tensor_tensor(
                                out=on[:, :, :].rearrange("p s e -> p (s e)"), in0=on[:, :, :].rearrange("p s e -> p (s e)"),
                                scalar=nlam[:, 0:1], in1=o1[:, 4 * c:4 * c + 4, :].rearrange("p s e -> p (s e)"),
                                op0=ALU.mult, op1=ALU.add), reads=[on, o1, nlam], writes=[on])
                            k.act(sqs, sqs[:, :, :], on[:, :, :], AF.Square, [on])
                            k.op("dve", lambda e: e.tensor_reduce(out=rz[:, 4:8], in_=sqs[:, :, :], axis=AX.X, op=ALU.add),
                                 reads=[sqs], writes=[rz])
                            k.act(rz, rz[:, 4:8], rz[:, 4:8], AF.Ln, [rz], bias=self.epsc[:, 0:1], scale=1.0 / 64)
                            k.act(rz, rz[:, 4:8], rz[:, 4:8], AF.Exp, [rz], scale=-0.5)
                            k.tt("dve", on, on[:, :, :], on[:, :, :], blast(rz[:, 4:8], 64), ALU.mult, [rz])
                            k.tt("dve", on, on[:, :, :], on[:, :, :], bmid(self.subg[:, :], 4), ALU.mult, [self.subg])
                            store_o(c, h * 64, on, on[:, :, :])
                    softmax_head(qt[:, :], [qt], kt[:, :], [kt],
                                 lambda kb, h=h: vA[:, kb, h * 65:(h + 1) * 65], [vA], h, None,
                                 False, [], cb, slot)
        run_jobs()
        self.pop()

    if "B" in self.mixers:
        self.push()
        bd = self.dd["baug"]
        qBp, kBp = [], []
        for h in range(4):
            pr, pb = h // 2, 64 * (h % 2)
            qBp.append(padded("qBp%d" % h, [(0, qkTv[6 + pr][pb:pb + 64, :], None), (64, self.baug[pr][pb:pb + 6, :], bd)]))
            kBp.append(padded("kBp%d" % h, [(0, qkTv[8 + pr][pb:pb + 64, :], None), (64, self.baug[2 + pr][pb:pb + 6, :], bd)]))
        vB = load_vaug("vB", 260, 4)
        for pair in range(2):
            for slot in range(2):
                h = 2 * pair + slot
                pb = 64 * slot

                def cb(c, bank, v3, h=h, slot=slot):
                    on = ons[slot]
                    normalize(bank, v3, on, on[:, :, :], slot)
                    store_o(c, 256 + h * 64, on, on[:, :, :])
                softmax_head(qBp[h][:, :], [qBp[h]], kBp[h][:, :], [kBp[h]],
                             lambda kb, h=h: vB[:, kb, h * 65:(h + 1) * 65], [vB], None, None, False, [], cb, slot)
            run_jobs()
        self.pop()

    if "C" in self.mixers:
        self.mixer_c(l, load_chunk)

    if "D" in self.mixers:
        self.mixer_d(l, load_chunk, load_vaug, softmax_head, normalize, tthi, ttlo, brel)
    self.pop()


Prog.phase_b = _phase_b


def run_staged(jobs):
    if not jobs:
        return
    ns = max(len(j) for j in jobs)
    for step in range(len(jobs) + ns):
        for si in range(ns - 1, -1, -1):
            j = step - si
            if 0 <= j < len(jobs) and si < len(jobs[j]):
                jobs[j][si]()


def _mixer_c(self, l, load_chunk):
    k, ps, cf = self.k, self.ps, self.cf
    self.push()
    qC = [self.padded("qCp%d" % h, [(64 * (h % 2), self.qkT[10 + h // 2][64 * (h % 2):64 * (h % 2) + 64, :], None)])
          for h in range(4)]
    kC = [load_chunk("kC%d" % i, 12 + i) for i in range(2)]
    vC = self.sb("vC", [128, 32, 256], BF16)
    k.dma("sp", vC[:, :, :], self.cvr.rearrange("(t p) c -> p t c", p=128), reads=[self.dd["cvr"]], writes=[vC])
    NB = 9
    SPb = [self.sb("cSP%d" % i, [128, 512], F32) for i in range(NB)]
    PRb = [self.sb("cPR%d" % i, [128, 512], F32) for i in range(NB)]
    Eb = [self.sb("cE%d" % i, [128, 512], F32) for i in range(NB)]
    ab = [self.sb("ca%d" % i, [128, 512], BF16) for i in range(NB)]
    aTb = [self.sb("caT%d" % i, [128, 512], BF16) for i in range(NB)]
    oc = [self.sb("coc%d" % i, [128, 256], BF16) for i in range(2)]
    zeros = self.zerob
    zbanks = [ps[0], ps[1], ps[2]]
    tbanks = [ps[3], ps[4], ps[5]]
    jobs = []
    n = 0
    for i in range(32):
        ost = oc[i % 2]
        accb = ps[6 + i % 2]
        k0 = 128 * (31 - i)
        nblk = i + 1
        chunks = [(k0 + 512 * j, min(512, 128 * nblk - 512 * j)) for j in range((nblk + 3) // 4)]
        carries = [None] * 4
        nch = len(chunks)
        for ci, (ks_, w_) in enumerate(chunks):
            for h in range(4):
                pb = 64 * (h % 2)
                qt, kt = qC[h], kC[h // 2]
                zb, tb = zbanks[n % 3], tbanks[n % 3]
                SP, PR, E, a, at = SPb[n % NB], PRb[n % NB], Eb[n % NB], ab[n % NB], aTb[n % NB]
                n += 1
                car = carries[h]
                carries[h] = (PR, PR[:, w_ - 1:w_])
                nb_ = w_ // 128

                def s0(i=i, pb=pb, qt=qt, kt=kt, zb=zb, ks_=ks_, w_=w_):
                    k.mm(zb, zb[:, 0:w_], qt[pb:pb + 64, i * 128:(i + 1) * 128], kt[pb:pb + 64, ks_:ks_ + w_], [qt, kt])

                def s1(zb=zb, E=E, w_=w_, ci=ci):
                    k.act(E, E[:, 0:w_], zb[:, 0:w_], AF.Exp, [zb])
                    if ci == 0:
                        k.tt("pool", E, E[:, 0:128], E[:, 0:128], self.cfs("M01"), ALU.mult, [cf])

                def s2(E=E, SP=SP, w_=w_):
                    k.act(SP, SP[:, 0:w_], E[:, 0:w_], AF.Ln, [E, self.epsc], bias=self.epsc[:, 1:2])

                def s3(SP=SP, PR=PR, w_=w_, car=car):
                    if car is None:
                        k.op("dve", lambda e: e.tensor_tensor_scan(
                            out=PR[:, 0:w_], data0=SP[:, 0:w_], data1=zeros[:, 0:w_], initial=0.0, op0=ALU.add, op1=ALU.add),
                            reads=[SP, zeros], writes=[PR])
                    else:
                        ctt, cap = car
                        k.op("dve", lambda e: e.tensor_tensor_scan(
                            out=PR[:, 0:w_], data0=SP[:, 0:w_], data1=zeros[:, 0:w_], initial=cap, op0=ALU.add, op1=ALU.add),
                            reads=[SP, zeros, ctt], writes=[PR])

                def s4(SP=SP, PR=PR, w_=w_):
                    k.act(SP, SP[:, 0:w_], PR[:, 0:w_], AF.Exp, [PR], scale=-1.0)

                def s5(SP=SP, E=E, a=a, w_=w_):
                    k.tt("pool", a, a[:, 0:w_], E[:, 0:w_], SP[:, 0:w_], ALU.mult, [E, SP])

                def s6(a=a, tb=tb, nb_=nb_):
                    for b in range(nb_):
                        k.mm(tb, tb[:, b * 128:(b + 1) * 128], a[:, b * 128:(b + 1) * 128], self.identb[:, :], [a, self.identb])

                def s7(at=at, tb=tb, w_=w_):
                    k.cp("dve", at, at[:, 0:w_], tb[:, 0:w_], [tb])

                def s8(at=at, accb=accb, h=h, ci=ci, ks_=ks_, nb_=nb_, nch=nch):
                    if ci == 0 and h == 0:
                        k.mm(accb, accb[:, 0:256], self.zerob[:, 0:128], self.zerob[:, 0:256], [self.zerob], start=True, stop=False)
                    for b in range(nb_):
                        kblk = ks_ // 128 + b
                        k.mm(accb, accb[:, h * 64:(h + 1) * 64], at[:, b * 128:(b + 1) * 128], vC[:, kblk, h * 64:(h + 1) * 64],
                             [at, vC], start=False, stop=(ci == nch - 1 and b == nb_ - 1 and h == 3))

                stages = [s0, s1, s2, s3, s4, s5, s6, s7, s8]
                if ci == nch - 1 and h == 3:
                    def s9(accb=accb, ost=ost, i=i):
                        k.cp("act", ost, ost[:, :], accb[:, 0:256], [accb])

                    def s10(ost=ost, i=i):
                        k.dma("pool", self.o_d[i * 128:(i + 1) * 128, 512:768], ost[:, :], reads=[ost], writes=[self.dd["o_d"]])
                    stages += [s9, s10]
                jobs.append(stages)
    run_staged(jobs)
    self.pop()


def _mixer_d(self, l, load_chunk, load_vaug, softmax_head, normalize, tthi, ttlo, brel):
    k, ps, cf, I = self.k, self.ps, self.cf, self.I
    self.push()
    qD = [load_chunk("qD%d" % i, 14 + i) for i in range(2)]
    oD = self.sb("oDacc", [128, 32, 256], F32)
    nmT = self.sb("nmT", [128, S], BF16)
    kcT, vcs = self.kcT, self.vcs
    NBc = 6
    sC = [self.sb("dsC%d" % i, [128, 256], F32) for i in range(NBc)]
    eC = [self.sb("deC%d" % i, [128, 256], F32) for i in range(NBc)]
    pbf = [self.sb("dpb%d" % i, [128, 256], BF16) for i in range(NBc)]
    pTc = [self.sb("dpT%d" % i, [128, 256], BF16) for i in range(NBc)]
    psumCs = [self.sb("dpsum%d" % i, [128, 256], F32) for i in range(3)]
    zcs = [self.sb("dzc%d" % i, [128, 8], F32) for i in range(3)]
    imps = [self.sb("dimp%d" % i, [128, 64], F32) for i in range(3)]
    imp2s = [self.sb("dimp2%d" % i, [128, 64], F32) for i in range(3)]
    nmfs = [self.sb("dnmf%d" % i, [128, 64], F32) for i in range(3)]
    m8s = [self.sb("dm8%d" % i, [128, 16], F32) for i in range(3)]
    nms = [self.sb("dnm%d" % i, [128, 128], BF16) for i in range(3)]
    dons = [self.sb("don%d" % i, [128, 4, 64], F32) for i in range(2)]
    import os
    dstage = int(os.environ.get("DSTAGE", "9"))
    jobs = []
    n = 0
    zbanks = [ps[0], ps[1], ps[2]]
    for i in range(32):
        off = 255 - 8 * i
        nnb = 1 if i < 16 else 2
        psumC, zc, imp, imp2, nmf, m8, nm = (psumCs[i % 3], zcs[i % 3], imps[i % 3], imp2s[i % 3], nmfs[i % 3],
                                             m8s[i % 3], nms[i % 3])
        ocb = [ps[4], ps[6]][i % 2]
        nmb = [ps[5], ps[7]][i % 2]
        for h in range(4):
            pb = 64 * (h % 2)
            qt = qD[h // 2]
            zb = zbanks[n % 3]
            tb = ps[3]
            s_, e_, pb_, pt_ = sC[n % NBc], eC[n % NBc], pbf[n % NBc], pTc[n % NBc]
            n += 1

            def s0(i=i, pb=pb, qt=qt, zb=zb):
                k.mm(zb, zb[:, 0:256], qt[pb:pb + 64, i * 128:(i + 1) * 128], kcT[pb:pb + 64, :], [qt, kcT])

            def s1(zb=zb, s_=s_, h=h, off=off):
                k.tt("dve", s_, s_[:, :], zb[:, 0:256], brel[:, h, off:off + 256], ALU.add, [zb, brel])

            def s2(s_=s_, e_=e_, zc=zc, h=h):
                k.act(e_, e_[:, :], s_[:, :], AF.Exp, [s_], accum=zc[:, h:h + 1], extra_w=[zc])

            def s3(e_=e_, pb_=pb_, zc=zc, psumC=psumC, h=h):
                k.ts("dve", zc, zc[:, 4 + h:5 + h], zc[:, h:h + 1], 1e-30, None, ALU.add, None, [zc])
                k.op("dve", lambda e: e.reciprocal(out=zc[:, 4 + h:5 + h], in_=zc[:, 4 + h:5 + h]), reads=[zc], writes=[zc])
                k.ts("dve", pb_, pb_[:, :], e_[:, :], zc[:, 4 + h:5 + h], None, ALU.mult, None, [e_, zc])
                if h == 0:
                    k.ts("dve", psumC, psumC[:, :], e_[:, :], zc[:, 4:5], None, ALU.mult, None, [e_, zc])
                else:
                    k.op("dve", lambda e: e.scalar_tensor_tensor(
                        out=psumC[:, :], in0=e_[:, :], scalar=zc[:, 4 + h:5 + h], in1=psumC[:, :], op0=ALU.mult, op1=ALU.add),
                        reads=[e_, zc, psumC], writes=[psumC])

            def s4(pb_=pb_, tb=tb, nnb=nnb):
                for nb in range(nnb):
                    k.mm(tb, tb[:, nb * 128:(nb + 1) * 128], pb_[:, nb * 128:(nb + 1) * 128], self.identb[:, :], [pb_, self.identb])

            def s5(pt_=pt_, tb=tb, nnb=nnb):
                k.cp("act", pt_, pt_[:, 0:nnb * 128], tb[:, 0:nnb * 128], [tb])

            def s6(pt_=pt_, ocb=ocb, h=h, nnb=nnb):
                for nb in range(nnb):
                    k.mm(ocb, ocb[:, h * 64:(h + 1) * 64], pt_[:, nb * 128:(nb + 1) * 128], vcs[:, nb, :], [pt_, vcs],
                         start=(nb == 0), stop=(nb == nnb - 1))
            stages = [s0, s1, s2, s3, s4, s5, s6]
            if h == 3:
                def s7(i=i, ocb=ocb, psumC=psumC, imp=imp):
                    k.tt("dve", oD, oD[:, i, :].rearrange("p (h e) -> p h e", e=64),
                         ocb[:, 0:256].rearrange("p (h e) -> p h e", e=64), blast(self.gates[:, i, 0:4], 64), ALU.mult,
                         [ocb, self.gates])
                    pv = psumC[:, :].rearrange("p (j r) -> p j r", r=4)
                    k.op("dve", lambda e: e.tensor_reduce(out=imp[:, :], in_=pv, axis=AX.X, op=ALU.add), reads=[psumC], writes=[imp])
                    k.tt("dve", imp, imp[:, 1:64], imp[:, 1:64], pv[:, 0:63, 3], ALU.add, [psumC])

                def s8(i=i, imp=imp, imp2=imp2, m8=m8):
                    st = 64 - 2 * i
                    k.tt("dve", imp2, imp2[:, :], imp[:, :], self.cfs("M1", st, st + 64), ALU.mult, [imp, cf])
                    k.tt("dve", imp2, imp2[:, :], imp2[:, :], self.cfs("CC", st, st + 64), ALU.add, [cf])
                    k.op("dve", lambda e: e.memset(imp2[:, 0:1], 1.0e4), writes=[imp2])
                    k.op("dve", lambda e: e.max(out=m8[:, 0:8], in_=imp2[:, :]), reads=[imp2], writes=[m8])

                def s9(imp=imp, imp2=imp2, m8=m8, nmf=nmf, nm=nm):
                    k.op("dve", lambda e: e.match_replace(out=imp[:, :], in_to_replace=m8[:, 0:8], in_values=imp2[:, :],
                                                          imm_value=-2.0), reads=[m8, imp2], writes=[imp])
                    k.op("dve", lambda e: e.max(out=m8[:, 8:16], in_=imp[:, :]), reads=[imp], writes=[m8])
                    k.ts("dve", nmf, nmf[:, :], imp2[:, :], m8[:, 15:16], BIG, ALU.is_ge, ALU.mult, [imp2, m8])
                    k.ts("dve", nm, nm[:, 0:64], nmf[:, :], -BIG, None, ALU.add, None, [nmf])
                    k.ts("dve", nm, nm[:, 64:128], nmf[:, :], -BIG, None, ALU.add, None, [nmf])

                def s10(nm=nm, nmb=nmb):
                    k.mm(nmb, nmb[:, 0:128], nm[:, :], self.identb[:, :], [nm, self.identb])

                def s11(i=i, nmb=nmb):
                    k.cp("act", nmT, nmT[:, i * 128:(i + 1) * 128], nmb[:, 0:128], [nmb])
                stages += [s7, s8, s9, s10, s11]
            jobs.append(stages)
    run_staged(jobs)
    k.dma("pool", self.baug[0, 0:64, :], nmT[0:64, :], reads=[nmT], writes=[self.dd["baug"]])
    for br, (ch, c0, gcol) in enumerate([(16, 520, 4), (17, 585, 8)]):
        if dstage < 4 + br:
            continue
        self.push()
        if br == 0:
            kT = self.sb("ksx", [128, S], BF16)
            self.load_rows(kT, 0, self.qkT[16][0:64, :])
            k.dma("pool", kT[64:128, :], I["cex"][:, :], writes=[kT])
        else:
            kT = load_chunk("dk%d" % br, ch)
        vv = load_vaug("dv%d" % br, c0, 1)
        qq4 = []
        for h in range(4):
            pair, pb = h // 2, 64 * (h % 2)
            if br == 0:
                qq4.append(self.padded("qsel%d" % h, [(0, self.qkT[14 + pair][pb:pb + 64, :], None),
                                                     (64, self.baug[0][0:64, :], self.dd["baug"])]))
            else:
                qq4.append(self.padded("qwin%d" % h, [(pb, self.qkT[14 + pair][pb:pb + 64, :], None)]))
        for pair in range(2):
            for slot in range(2):
                h = 2 * pair + slot

                def cb(c, bank, v3, h=h, gcol=gcol, slot=slot):
                    on = dons[slot]
                    normalize(bank, v3, on, on[:, :, :], slot)
                    k.tt("dve", on, on[:, :, :], on[:, :, :], blast(self.gates[:, 4 * c:4 * c + 4, gcol + h], 64), ALU.mult,
                         [self.gates])
                    dst = oD[:, 4 * c:4 * c + 4, h * 64:(h + 1) * 64]
                    k.tt("dve", oD, dst, dst, on[:, :, :], ALU.add, [on])
                qq = qq4[h]
                softmax_head(qq[:, :], [qq], kT[:, :], [kT],
                             lambda kb, vv=vv: vv[:, kb, 0:65], [vv], 4 + h, None, br == 1, [], cb, slot)
        self.run_jobs()
        self.pop()
    ob = [self.sb("dob%d" % i, [128, 256], BF16) for i in range(2)]
    for t in range(32):
        o_ = ob[t % 2]
        k.cp("act", o_, o_[:, :], oD[:, t, :], [oD])
        k.dma("pool", self.o_d[t * 128:(t + 1) * 128, 768:1024], o_[:, :], reads=[o_], writes=[self.dd["o_d"]])
    self.pop()


Prog.mixer_c = _mixer_c
Prog.mixer_d = _mixer_d


def _phase_c(self, l, xsrc, xsd, xdst, xdd):
    k, I, cf, ps = self.k, self.I, self.cf, self.ps
    ident = self.cfs("ident")
    GT = 256
    NG = S // GT
    self.push()
    wout = self.sb("wout", [128, 8, D], BF16)
    wup = self.sb("wup", [128, 8, 2 * DFF], BF16)
    wdn = self.sb("wdn", [128, NFC, D], BF16)
    src_up = I["ffn_w_up"][l].rearrange("(c p) n -> p c n", p=128)
    for c in range(8):
        for hh in range(2):
            k.dma("pool", wup[:, c, hh * DFF:(hh + 1) * DFF], src_up[:, c, hh * DFF:(hh + 1) * DFF], writes=[wup])
    cst = self.sb("cst", [88, 128], F32)
    k.dma("sp", cst[0:66, :], I["ffn_conv_w"][l], writes=[cst])
    k.dma("sp", cst[66:88, :], I["ffn_conv_b"][l], writes=[cst])
    self.tr(ps[0], ps[0][:, 0:88], cst[:, :], ident[0:88, 0:88], [cst, cf])
    cw = self.sb("cw", [128, 88], F32)
    k.cp("dve", cw, cw[:, :], ps[0][:, 0:88], [ps[0]])
    ab = self.sb("abf", [128, 16], F32)
    k.ts("dve", ab, ab[:, 0:8], self.modcol(l, 4), 1.0, None, ALU.add, None, [self.modT])
    k.tt("dve", ab, ab[:, 0:8], ab[:, 0:8], self.colv[:, 112 + 8 * l:120 + 8 * l], ALU.mult, [self.colv])
    k.cp("dve", ab, ab[:, 8:16], self.modcol(l, 3), [self.modT])
    self.push()
    gbc = [self.sb("gbc%d" % i, [128, D], F32) for i in range(2)]
    dg = self.sb("dgt", [128, 128], F32)
    for gi, which in enumerate([2, 5]):
        for c in range(8):
            k.ts("dve", dg, dg[:, :], ident, self.modcol(l, which)[:, c:c + 1], None, ALU.mult, None, [cf, self.modT])
            b = ps[1 + (c // 4)]
            k.mm(b, b[:, (c % 4) * 128:(c % 4 + 1) * 128], self.cfs("ones"), dg[:, :], [dg, cf])
        for hh in range(2):
            k.cp("dve", gbc[gi], gbc[gi][:, hh * 512:(hh + 1) * 512], ps[1 + hh][:, :], [ps[1 + hh]])
    stg = [self.sb("wstg%d" % i, [128, D], F32) for i in range(2)]
    src_o = I["w_out"][l].rearrange("(c p) n -> p c n", p=128)
    src_d = I["ffn_w_down"][l].rearrange("(c p) n -> p c n", p=128)
    n = 0
    for c in range(8):
        st_ = stg[n % 2]
        k.dma("sp", st_[:, :], src_o[:, c, :], writes=[st_])
        k.tt("dve" if n % 2 == 0 else "pool", wout, wout[:, c, :], st_[:, :], gbc[0][:, :], ALU.mult, [st_, gbc[0]])
        n += 1
    for c in range(NFC):
        st_ = stg[n % 2]
        k.dma("sp", st_[:, :], src_d[:, c, :], writes=[st_])
        k.tt("dve" if n % 2 == 0 else "pool", wdn, wdn[:, c, :], st_[:, :], gbc[1][:, :], ALU.mult, [st_, gbc[1]])
        n += 1
    self.pop()
    ots = [self.sb("ot%d" % i, [128, D], BF16) for i in range(2)]
    oT = self.sb("oTc", [128, 8, 128], BF16)
    xms = [self.sb("xm%d" % i, [128, D], F32) for i in range(3)]
    xn = self.sb("xnc", [128, D], F32)
    sss = [self.sb("ssc%d" % i, [128, 4], F32) for i in range(2)]
    h2T = self.sb("h2T", [128, 8, GT], BF16)
    gsb = [self.sb("gsb%d" % i, [128, GT + 2], F32) for i in range(2)]
    accs = [self.sb("cacc%d" % i, [128, GT], F32) for i in range(2)]
    sgs = [self.sb("csg%d" % i, [128, GT], F32) for i in range(2)]
    aT = self.sb("aTc", [128, NFC, GT], BF16)
    halo = self.sb("halo", [128, NFC, 2], F32)
    k.op("dve", lambda e: e.memset(halo[:, :, :], 0.0), writes=[halo])
    nt = 0
    for g in range(NG):
        for tt_ in range(GT // 128):
            t = g * (GT // 128) + tt_
            ot = ots[t % 2]
            xm = xms[t % 3]
            ss = sss[t % 2]
            k.dma("sp", ot[:, :], self.o_d[t * 128:(t + 1) * 128, :], reads=[self.dd["o_d"]], writes=[ot])
            k.dma("act", xm[:, :], xsrc[t * 128:(t + 1) * 128, :], reads=[xsd] if xsd else [], writes=[xm])
            for c in range(8):
                b = ps[c // 4]
                k.mm(b, b[:, (c % 4) * 128:(c % 4 + 1) * 128], ot[:, c * 128:(c + 1) * 128], self.identb[:, :], [ot, self.identb])
            for hh in range(2):
                k.cp("act" if hh == 0 else "dve", oT, oT[:, hh * 4:(hh + 1) * 4, :].rearrange("p c t -> p (c t)"), ps[hh][:, :], [ps[hh]])
            for hh in range(2):
                b = ps[2 + hh]
                for c in range(8):
                    k.mm(b, b[:, :], oT[:, c, :], wout[:, c, hh * 512:(hh + 1) * 512], [oT, wout], start=(c == 0), stop=(c == 7))
                k.tt("dve", xm, xm[:, hh * 512:(hh + 1) * 512], b[:, :], xm[:, hh * 512:(hh + 1) * 512], ALU.add, [b])
            if self.dbg:
                k.dma("pool", self.xmid[t * 128:(t + 1) * 128, :], xm[:, :], reads=[xm], writes=[self.dd["xmid"]])
            k.act(xn, xn[:, :], xm[:, :], AF.Square, [xm], accum=ss[:, 0:1], extra_w=[ss])
            k.act(ss, ss[:, 1:2], ss[:, 0:1], AF.Ln, [ss], bias=self.epsc[:, 0:1], scale=1.0 / D)
            k.act(ss, ss[:, 2:3], ss[:, 1:2], AF.Exp, [ss], scale=-0.5)
            k.ts("dve", xn, xn[:, :], xm[:, :], ss[:, 2:3], None, ALU.mult, None, [xm, ss])
            for c in range(8):
                b = ps[4 + c // 4]
                self.tr(b, b[:, (c % 4) * 128:(c % 4 + 1) * 128], xn[:, c * 128:(c + 1) * 128], ident, [xn, cf])
            for c in range(8):
                b = ps[4 + c // 4]
                pin = b[:, (c % 4) * 128:(c % 4 + 1) * 128]
                dst = h2T[:, c, tt_ * 128:(tt_ + 1) * 128]
                if c % 2 == 0:
                    k.ts("dve", h2T, dst, pin, ab[:, c:c + 1], ab[:, 8 + c:9 + c], ALU.mult, ALU.add, [b, ab])
                else:
                    k.act(h2T, dst, pin, AF.Identity, [b, ab], bias=ab[:, 8 + c:9 + c], scale=ab[:, c:c + 1])
        for fc in range(NFC):
            bg = ps[(2 * fc) % 4]
            bv = ps[(2 * fc + 1) % 4]
            for c in range(8):
                k.mm(bg, bg[:, 0:GT], wup[:, c, fc * 128:(fc + 1) * 128], h2T[:, c, :], [wup, h2T], start=(c == 0), stop=(c == 7))
            for c in range(8):
                k.mm(bv, bv[:, 0:GT], wup[:, c, DFF + fc * 128:DFF + (fc + 1) * 128], h2T[:, c, :], [wup, h2T],
                     start=(c == 0), stop=(c == 7))
            gs_, ac_, sg_ = gsb[fc % 2], accs[fc % 2], sgs[fc % 2]
            k.cp("act", gs_, gs_[:, 0:2], halo[:, fc, :], [halo])
            k.cp("act", gs_, gs_[:, 2:GT + 2], bg[:, 0:GT], [bg])
            k.cp("act", halo, halo[:, fc, :], gs_[:, GT:GT + 2], [gs_])
            k.ts("dve", ac_, ac_[:, :], gs_[:, 2:GT + 2], cw[:, 44 + fc:45 + fc], cw[:, 66 + fc:67 + fc], ALU.mult, ALU.add,
                 [gs_, cw])
            k.op("dve", lambda e, ac_=ac_, gs_=gs_, fc=fc: e.scalar_tensor_tensor(
                out=ac_[:, :], in0=gs_[:, 1:GT + 1], scalar=cw[:, 22 + fc:23 + fc], in1=ac_[:, :], op0=ALU.mult, op1=ALU.add),
                reads=[gs_, cw, ac_], writes=[ac_])
            k.op("dve", lambda e, ac_=ac_, gs_=gs_, fc=fc: e.scalar_tensor_tensor(
                out=ac_[:, :], in0=gs_[:, 0:GT], scalar=cw[:, fc:fc + 1], in1=ac_[:, :], op0=ALU.mult, op1=ALU.add),
                reads=[gs_, cw, ac_], writes=[ac_])
            k.act(sg_, sg_[:, :], ac_[:, :], AF.Silu, [ac_])
            k.tt("dve", aT, aT[:, fc, :], sg_[:, :], bv[:, 0:GT], ALU.mult, [sg_, bv])
        for tt_ in range(GT // 128):
            t = g * (GT // 128) + tt_
            xm = xms[t % 3]
            for hh in range(2):
                b = ps[4 + hh]
                for fc in range(NFC):
                    k.mm(b, b[:, :], aT[:, fc, tt_ * 128:(tt_ + 1) * 128], wdn[:, fc, hh * 512:(hh + 1) * 512], [aT, wdn],
                         start=(fc == 0), stop=(fc == NFC - 1))
                k.tt("dve", xm, xm[:, hh * 512:(hh + 1) * 512], b[:, :], xm[:, hh * 512:(hh + 1) * 512], ALU.add, [b])
            k.dma("pool", xdst[t * 128:(t + 1) * 128, :], xm[:, :], reads=[xm], writes=[xdd])
    self.pop()


Prog.phase_c = _phase_c


_PROG = None


def kernel(**inputs):
    global _PROG
    if _PROG is None:
        _PROG = Prog()
    in_maps = [make_inputs(inputs, r % 4) for r in range(4)]
    in_maps = in_maps + in_maps
    res = run_bass_kernel_spmd(_PROG.nc, in_maps, core_ids=list(range(8)))
    out = np.stack([np.asarray(res.results[b]["y"], dtype=np.float32) for b in range(4)], axis=0)
    return out
```

```python
import math
import numpy as np
import concourse.bass as bass
import concourse.mybir as mybir
from concourse.bass_utils import run_bass_kernel_spmd

F32 = mybir.dt.float32
BF16 = mybir.dt.bfloat16
AF = mybir.ActivationFunctionType
ALU = mybir.AluOpType
AX = mybir.AxisListType

S = 4096
D = 1024
NT = 32
DFF = 2816
NFC = 22
PIN = 2960
BIG = 32768.0
EPS = 1e-6
NQ = 8


class Dep:
    __slots__ = ("w", "r")

    def __init__(self):
        self.w = {}
        self.r = {}


class TT:
    def __init__(self, t):
        self.t = t
        self.dep = Dep()

    def __getitem__(self, idx):
        return self.t[idx]


class Kern:
    def __init__(self, nc):
        self.nc = nc
        self.engs = {"pe": nc.tensor, "act": nc.scalar, "dve": nc.vector, "pool": nc.gpsimd, "sp": nc.sync}
        self.sem = {e: nc.semaphore("c_" + e).__enter__() for e in ["pe", "act", "dve", "pool"]}
        self.cnt = {e: 0 for e in self.sem}
        self.waited = {}
        self.dq = {}
        for q in ["sp", "pool", "act"]:
            self.dq[q] = {"sems": [nc.semaphore("d_%s%d" % (q, i)).__enter__() for i in range(NQ)],
                          "cnt": [0] * NQ, "i": 0}
        self.own = {e: id(self.sem[e]) for e in self.sem}
        self.pools = []

    def sb(self, name, shape, dt):
        return TT(self.nc.sbuf_tensor(name, shape, dt).__enter__())

    def ps(self, name, shape, dt):
        return TT(self.nc.psum_tensor(name, shape, dt).__enter__())

    def _wait(self, eng, sem, val):
        key = (eng, id(sem))
        if self.waited.get(key, 0) >= val:
            return
        self.waited[key] = val
        self.engs[eng].wait_ge(sem, val)

    def _deps(self, eng, reads, writes):
        own = self.own.get(eng)
        for d in reads:
            for sid, (sem, val) in d.dep.w.items():
                if sid == own and eng == "pe":
                    continue
                self._wait(eng, sem, val)
        for d in writes:
            for sid, (sem, val) in d.dep.w.items():
                if sid == own and eng == "pe":
                    continue
                self._wait(eng, sem, val)
            for sid, (sem, val) in d.dep.r.items():
                if sid == own and eng == "pe":
                    continue
                self._wait(eng, sem, val)

    def _reg(self, sem, val, reads, writes):
        sid = id(sem)
        for d in reads:
            d.dep.r[sid] = (sem, val)
        for d in writes:
            d.dep.w[sid] = (sem, val)

    def op(self, eng, fn, reads=(), writes=()):
        self._deps(eng, reads, writes)
        ins = fn(self.engs[eng])
        self.cnt[eng] += 1
        ins.then_inc(self.sem[eng], 1)
        self._reg(self.sem[eng], self.cnt[eng], reads, writes)

    def dma(self, q, out_ap, in_ap, reads=(), writes=()):
        self._deps(q, reads, writes)
        Q = self.dq[q]
        j = Q["i"] % NQ
        Q["i"] += 1
        sem = Q["sems"][j]
        if Q["cnt"][j] > 0:
            self._wait(q, sem, 16 * Q["cnt"][j])
        Q["cnt"][j] += 1
        self.engs[q].dma_start(out=out_ap, in_=in_ap).then_inc(sem, 16)
        self._reg(sem, 16 * Q["cnt"][j], reads, writes)

    def finish(self, outs):
        for d in outs:
            for sid, (sem, val) in d.dep.w.items():
                self._wait("sp", sem, val)

    def mm(self, out_tt, out_ap, lhsT, rhs, reads, start=True, stop=True):
        self.op("pe", lambda e: e.matmul(out_ap, lhsT, rhs, start=start, stop=stop, skip_group_check=True),
                reads=reads, writes=[out_tt])

    def act(self, out_tt, out_ap, in_ap, func, reads, bias=None, scale=None, accum=None, extra_w=()):
        kw = {}
        if bias is not None:
            kw["bias"] = bias
        if scale is not None:
            kw["scale"] = scale
        if accum is not None:
            kw["accum_out"] = accum
        self.op("act", lambda e: e.activation(out_ap, in_ap, func, **kw), reads=reads,
                writes=[out_tt] + list(extra_w))

    def ts(self, eng, out_tt, out_ap, in_ap, s1, s2, op0, op1, reads):
        if op1 is None:
            self.op(eng, lambda e: e.tensor_scalar(out=out_ap, in0=in_ap, scalar1=s1, scalar2=None, op0=op0),
                    reads=reads, writes=[out_tt])
        else:
            self.op(eng, lambda e: e.tensor_scalar(out=out_ap, in0=in_ap, scalar1=s1, scalar2=s2, op0=op0, op1=op1),
                    reads=reads, writes=[out_tt])

    def tt(self, eng, out_tt, out_ap, a, b, op, reads):
        self.op(eng, lambda e: e.tensor_tensor(out=out_ap, in0=a, in1=b, op=op), reads=reads, writes=[out_tt])

    def cp(self, eng, out_tt, out_ap, in_ap, reads):
        if eng == "act":
            self.op("act", lambda e: e.copy(out_ap, in_ap), reads=reads, writes=[out_tt])
        else:
            self.op(eng, lambda e: e.tensor_copy(out=out_ap, in_=in_ap), reads=reads, writes=[out_tt])


def bmid(a, m):
    return bass.AP(tensor=a.tensor, offset=a.offset, ap=[list(a.ap[0]), [0, m]] + [list(x) for x in a.ap[1:]])


def blast(a, n):
    return bass.AP(tensor=a.tensor, offset=a.offset, ap=[list(x) for x in a.ap] + [[0, n]])


def bc_ap(ap2d, nparts):
    return bass.AP(tensor=ap2d.tensor, offset=ap2d.offset, ap=[[0, nparts]] + [list(x) for x in ap2d.ap[1:]])


def _bucket(dist):
    n = np.maximum(dist, 0)
    nf = np.maximum(n, 1).astype(np.float32)
    large = 16 + (np.log(nf / np.float32(16)) / np.float32(math.log(128 / 16)) * np.float32(16)).astype(np.int32)
    return np.where(n < 16, n, np.minimum(large, 31))


CF = {}


def _mk_consts():
    p = np.arange(128)[:, None]
    j = np.arange(128)[None, :]
    cols = []

    def add(name, arr):
        CF[name] = (sum(a.shape[1] for a in cols), arr.shape[1])
        cols.append(arr.astype(np.float32))

    add("ident", (p == j))
    add("antiI", (p + j == 127))
    add("U", (p <= j))
    add("ones", np.ones((128, 128)))
    add("TRI", np.where(j >= p, 0.0, -BIG))
    add("TW", np.where(j < p, 0.0, -BIG))
    add("M01", (p + j > 127))
    add("NMC", np.where(p + j > 127, 0.0, -BIG))
    j2 = np.arange(256)[None, :]
    d = j2 - p
    r = np.arange(512)[None, :]
    dc = p - 16 * (r - 255) - 31
    global _CF2_ARR
    _CF2_ARR = np.ascontiguousarray(np.concatenate([np.where(d >= 0, _bucket(d), -1),
                                                    np.where(dc >= 0, _bucket(dc), -1)], axis=1).astype(np.float32))
    jj = np.arange(128)[None, :]
    fb = (jj == 64 + p // 64)
    vb = (jj <= 64 + p // 64)
    add("M1", (vb & ~fb))
    add("CC", 1.0e4 * fb - 1.0 * (~vb))
    cf = np.concatenate(cols, axis=1)
    ex = np.zeros((64, 32, 128), np.float32)
    for kb in range(32):
        for pp in range(128):
            ex[2 * kb + pp // 64, kb, pp] = 1.0
    return np.ascontiguousarray(cf), np.ascontiguousarray(ex.reshape(64, 4096))


_CF_ARR, _EX_ARR = _mk_consts()
NCF = _CF_ARR.shape[1]

_SEGS = [("aq", 0, 256), ("ak", 256, 256), ("bq", 768, 256), ("bk", 1024, 256),
         ("dq", 2308, 256), ("ks", 2692, 64), ("kw", 2820, 64),
         ("cq", 1540, 256), ("ck", 1796, 256),
         ("av", 512, 256), ("bv", 1280, 256),
         ("cv", 2052, 256), ("vs", 2756, 64), ("vw", 2884, 64), ("bf", 1536, 4), ("dg", 2948, 12),
         ("kc", 2564, 64), ("vc", 2628, 64)]
WOFF = {}
_o = 0
for _n, _s, _w in _SEGS:
    WOFF[_n] = _o
    _o += _w
assert _o == PIN


class Prog:
    def __init__(self, dbg=False, nlayers=2, stop_after=None, mixers="ABCD"):
        self.dbg = dbg
        self.mixers = mixers
        self.stop_after = stop_after
        nc = bass.Bass("TRN2", target_bir_lowering=False)
        self.nc = nc
        k = Kern(nc)
        self.k = k
        self.stack = []

        def din(name, shape):
            return nc.dram_tensor(name, list(shape), F32, kind="ExternalInput").ap()

        self.x_in = din("x", [S, D])
        self.c_in = din("c", [8, 128])
        self.I = {}
        for name, shape in [("rel_bias", [1, 256]), ("ada_w", [2, D, 6 * D]), ("ada_b", [2, 48, 128]),
                            ("norm_mix_g", [2, 8, 128]), ("norm_ffn_g", [2, 8, 128]), ("w_in", [2, D, PIN]),
                            ("w_out", [2, D, D]), ("diff_qnorm_g", [2, 1, 32]), ("diff_knorm_g", [2, 1, 32]),
                            ("diff_lambda", [2, 1, 128]), ("diff_subln_g", [2, 1, 64]), ("fox_qnorm_g", [2, 1, 64]),
                            ("fox_knorm_g", [2, 1, 64]), ("fox_b_f", [2, 1, 4]), ("nsa_qnorm_g", [2, 1, 64]),
                            ("nsa_knorm_g", [2, 1, 192]), ("nsa_pe", [2, 2, 32, 64]), ("nsa_phi_w", [2, 2, 2048, 64]),
                            ("ffn_w_up", [2, D, 2 * DFF]), ("ffn_conv_w", [2, 66, 128]), ("ffn_conv_b", [2, 22, 128]),
                            ("ffn_w_down", [2, DFF, D]), ("cf", [128, NCF]), ("cf2", [128, 768]), ("cex", [64, 4096])]:
            self.I[name] = din(name, shape)
        kind = "ExternalOutput" if dbg else "Internal"

        def dscr(name, shape, dt):
            return nc.dram_tensor(name, list(shape), dt, kind=kind).ap()

        self.y = nc.dram_tensor("y", [S, D], F32, kind="ExternalOutput").ap()
        self.y_dep = TT(None)
        self.xs = dscr("xs", [S, D], F32)
        self.xmid = dscr("xmid", [S, D], F32)
        self.qkT = dscr("qkT", [19, 128, S], BF16)
        self.vtm = dscr("vtm", [S, 650], BF16)
        self.cvr = dscr("cvr", [S, 256], BF16)
        self.baug = dscr("baug", [4, 128, S], BF16)
        self.o_d = dscr("o_d", [S, D], BF16)
        self.tts = dscr("tts", [128, 2 * 8 * 256], BF16)
        self.brs = dscr("brs", [128, 4 * 512], F32)
        self.dd = {n: TT(None) for n in ["xs", "xmid", "qkT", "vtm", "cvr", "baug", "o_d", "tts", "brs"]}
        self.ps = [k.ps("ps%d" % i, [128, 512], F32) for i in range(8)]
        self.setup()
        for l in range(nlayers if stop_after != ("setup", 0) else 0):
            xsrc, xsd = (self.x_in, None) if l == 0 else (self.xs, self.dd["xs"])
            self.phase_a(l, xsrc, xsd)
            if stop_after == ("a", l):
                break
            self.phase_b(l)
            if stop_after == ("b", l):
                break
            last = (l == nlayers - 1)
            self.phase_c(l, xsrc, xsd, self.y if last else self.xs, self.y_dep if last else self.dd["xs"])
        outs = [self.y_dep] + (list(self.dd.values()) if dbg else [])
        self.barrier()
        k.finish(outs)

    def push(self):
        self.stack.append([])

    def sb(self, name, shape, dt):
        cm = self.nc.sbuf_tensor(name + "_%d" % len(self.k.pools), shape, dt)
        self.k.pools.append(0)
        t = TT(cm.__enter__())
        self.stack[-1].append(cm)
        return t

    def pop(self):
        self.barrier()
        for cm in reversed(self.stack.pop()):
            cm.__exit__(None, None, None)

    def barrier(self):
        k = self.k
        for e in ["pe", "act", "dve", "pool", "sp"]:
            for e2 in k.sem:
                if k.cnt[e2] > 0 and e2 != e:
                    k._wait(e, k.sem[e2], k.cnt[e2])
            for q in k.dq.values():
                for j in range(NQ):
                    if q["cnt"][j] > 0:
                        k._wait(e, q["sems"][j], 16 * q["cnt"][j])

    def cfs(self, name, a=0, b=None):
        o, w = CF[name]
        if b is None:
            b = w
        return self.cf[:, o + a:o + b]

    def tr(self, ps_tt, out_ap, in_ap, ident_ap, reads):
        self.k.mm(ps_tt, out_ap, in_ap, ident_ap, reads)

    def setup(self):
        k, nc, I = self.k, self.nc, self.I
        self.stack.append([])
        self.cf = self.sb("cf", [128, NCF], F32)
        k.dma("sp", self.cf[:, :], I["cf"][:, :], writes=[self.cf])
        cf = self.cf
        self.identb = self.sb("identb", [128, 128], BF16)
        self.antib = self.sb("antib", [128, 128], BF16)
        self.trib = self.sb("trib", [128, 128], BF16)
        self.twb = self.sb("twb", [128, 128], BF16)
        for t, n in [(self.identb, "ident"), (self.antib, "antiI"), (self.trib, "TRI"), (self.twb, "TW")]:
            k.cp("dve", t, t[:, :], self.cfs(n), [cf])
        self.zerob = self.sb("zerob", [128, 512], BF16)
        k.op("dve", lambda e: e.memset(self.zerob[:, :], 0.0), writes=[self.zerob])
        self.tab = self.sb("tab", [128, 256], F32)
        k.dma("sp", self.tab[:, :], bc_ap(I["rel_bias"], 128), writes=[self.tab])
        self.modT = self.sb("modT", [128, 96], F32)
        self.colv = self.sb("colv", [128, 128], F32)
        self.gates = self.sb("gates", [128, 32, 12], F32)
        self.kcT = self.sb("kcT", [128, 256], BF16)
        self.vcs = self.sb("vcs", [128, 2, 64], BF16)
        self.lam = self.sb("lam", [128, 2], F32)
        self.subg = self.sb("subg", [128, 64], F32)
        self.epsc = self.sb("epsc", [128, 4], F32)
        k.op("dve", lambda e: e.memset(self.epsc[:, 0:1], EPS), writes=[self.epsc])
        k.op("dve", lambda e: e.memset(self.epsc[:, 1:2], 1.0), writes=[self.epsc])
        k.op("dve", lambda e: e.memset(self.epsc[:, 2:3], 1e-30), writes=[self.epsc])
        self.push()
        tabd = self.sb("tabd", [128, 256], F32)
        k.tt("dve", tabd, tabd[:, :].rearrange("p (b h) -> p b h", h=8),
             self.tab[:, :].rearrange("p (b h) -> p b h", h=8), bmid(self.tab[:, 248:256], 32),
             ALU.subtract, [self.tab])
        acc = self.sb("ttacc", [128, 8, 256], F32)
        accb = self.sb("bracc", [128, 4, 512], F32)
        mb = self.sb("mb", [128, 512], F32)
        cf2 = self.sb("cf2", [128, 768], F32)
        k.dma("sp", cf2[:, :], I["cf2"][:, :], writes=[cf2])
        BKa = cf2[:, 0:256]
        BKC = cf2[:, 256:768]
        for h in range(8):
            k.ts("dve", acc, acc[:, h, :], BKa, -1.0, -BIG, ALU.is_equal, ALU.mult, [cf2])
        for h in range(4):
            k.ts("dve", accb, accb[:, h, :], BKC, -1.0, -BIG, ALU.is_equal, ALU.mult, [cf2])
        for b in range(32):
            k.ts("dve", mb, mb[:, 0:256], BKa, float(b), None, ALU.is_equal, None, [cf2])
            for h in range(8):
                k.op("dve", lambda e, h=h, b=b: e.scalar_tensor_tensor(
                    out=acc[:, h, :], in0=mb[:, 0:256], scalar=tabd[:, b * 8 + h:b * 8 + h + 1], in1=acc[:, h, :],
                    op0=ALU.mult, op1=ALU.add), reads=[mb, tabd], writes=[acc])
            k.ts("dve", mb, mb[:, :], BKC, float(b), None, ALU.is_equal, None, [cf2])
            for h in range(4):
                k.op("dve", lambda e, h=h, b=b: e.scalar_tensor_tensor(
                    out=accb[:, h, :], in0=mb[:, :], scalar=self.tab[:, b * 8 + 4 + h:b * 8 + 5 + h], in1=accb[:, h, :],
                    op0=ALU.mult, op1=ALU.add), reads=[mb, self.tab], writes=[accb])
        tthi = self.sb("tthi", [128, 8 * 256], BF16)
        ttlo = self.sb("ttlo", [128, 8 * 256], BF16)
        accf = acc[:, :, :].rearrange("p h j -> p (h j)")
        k.cp("dve", tthi, tthi[:, :], accf, [acc])
        k.tt("dve", acc, accf, accf, tthi[:, :], ALU.subtract, [tthi])
        k.cp("dve", ttlo, ttlo[:, :], accf, [acc])
        k.dma("pool", self.tts[:, 0:2048], tthi[:, :], reads=[tthi], writes=[self.dd["tts"]])
        k.dma("pool", self.tts[:, 2048:4096], ttlo[:, :], reads=[ttlo], writes=[self.dd["tts"]])
        k.dma("pool", self.brs[:, :], accb[:, :, :].rearrange("p h j -> p (h j)"), reads=[accb], writes=[self.dd["brs"]])
        st = self.sb("colst", [128, 128], F32)
        k.dma("sp", st[0:48, :], I["ada_b"][0], writes=[st])
        k.dma("sp", st[48:96, :], I["ada_b"][1], writes=[st])
        k.dma("sp", st[96:104, :], I["norm_mix_g"][0], writes=[st])
        k.dma("sp", st[104:112, :], I["norm_mix_g"][1], writes=[st])
        k.dma("sp", st[112:120, :], I["norm_ffn_g"][0], writes=[st])
        k.dma("sp", st[120:128, :], I["norm_ffn_g"][1], writes=[st])
        ps0 = self.ps[0]
        self.tr(ps0, ps0[:, 0:128], st[:, :], self.cfs("ident"), [st, cf])
        k.cp("dve", self.colv, self.colv[:, :], ps0[:, 0:128], [ps0])
        c8 = self.sb("c8", [8, 128], F32)
        k.dma("sp", c8[:, :], self.c_in[:, :], writes=[c8])
        ps1 = self.ps[1]
        self.tr(ps1, ps1[:, 0:8], c8[:, :], self.cfs("ident")[0:8, 0:8], [c8, cf])
        cact = self.sb("cact", [128, 8], F32)
        k.act(cact, cact[:, :], ps1[:, 0:8], AF.Silu, [ps1])
        wb = [self.sb("adaw%d" % i, [128, 8, 512], F32) for i in range(2)]
        ps2 = self.ps[2]
        for l in range(2):
            for blk in range(12):
                w = wb[blk % 2]
                src = I["ada_w"][l].rearrange("(kc p) n -> p kc n", p=128)
                for kc in range(8):
                    k.dma("sp" if kc % 2 == 0 else "act", w[:, kc, :], src[:, kc, blk * 512:(blk + 1) * 512], writes=[w])
                for jj in range(4):
                    j = l * 48 + blk * 4 + jj
                    for kc in range(8):
                        k.mm(ps2, ps2[:, j:j + 1], w[:, kc, jj * 128:(jj + 1) * 128], cact[:, kc:kc + 1], [w, cact],
                             start=(kc == 0), stop=(kc == 7))
        k.tt("dve", self.modT, self.modT[:, :], ps2[:, 0:96], self.colv[:, 0:96], ALU.add, [ps2, self.colv])
        self.pop()

    def modcol(self, l, which):
        o = l * 48 + which * 8
        return self.modT[:, o:o + 8]

    def phase_a(self, l, xsrc, xsd):
        k, I, cf = self.k, self.I, self.cf
        ps = self.ps
        ident = self.cfs("ident")
        self.push()
        gsm = self.sb("gsm", [128, 644], F32)
        go = {}
        o = 0
        for name, n in [("diff_qnorm_g", 32), ("diff_knorm_g", 32), ("fox_qnorm_g", 64), ("fox_knorm_g", 64),
                        ("nsa_qnorm_g", 64), ("nsa_knorm_g", 192), ("fox_b_f", 4), ("diff_subln_g", 64),
                        ("diff_lambda", 128)]:
            k.dma("sp", gsm[:, o:o + n], bc_ap(I[name][l], 128), writes=[gsm])
            go[name] = o
            o += n

        def gs_(name, a, b):
            return gsm[:, go[name] + a:go[name] + b]
        gA = self.sb("gA", [128, 512], F32)
        gB = self.sb("gB", [128, 512], F32)
        gD = self.sb("gD", [128, 384], F32)
        k.ts("dve", gA, gA[:, 0:256].rearrange("p (g e) -> p g e", e=32), bmid(gs_("diff_qnorm_g", 0, 32), 8),
             32 ** -0.5, None, ALU.mult, None, [gsm])
        k.cp("dve", gA, gA[:, 256:512].rearrange("p (g e) -> p g e", e=32), bmid(gs_("diff_knorm_g", 0, 32), 8), [gsm])
        k.ts("dve", gB, gB[:, 0:256].rearrange("p (g e) -> p g e", e=64), bmid(gs_("fox_qnorm_g", 0, 64), 4),
             0.125, None, ALU.mult, None, [gsm])
        k.cp("dve", gB, gB[:, 256:512].rearrange("p (g e) -> p g e", e=64), bmid(gs_("fox_knorm_g", 0, 64), 4), [gsm])
        k.ts("dve", gD, gD[:, 0:256].rearrange("p (g e) -> p g e", e=64), bmid(gs_("nsa_qnorm_g", 0, 64), 4),
             0.125, None, ALU.mult, None, [gsm])
        k.cp("dve", gD, gD[:, 256:384], gs_("nsa_knorm_g", 64, 192), [gsm])
        lam_init = 0.8 - 0.6 * math.exp(-0.3 * l)
        self.lam_init = lam_init
        subg = self.sb("subgl", [128, 64], F32)
        k.ts("dve", subg, subg[:, :], gs_("diff_subln_g", 0, 64), 1.0 - lam_init, None, ALU.mult, None, [gsm])
        lt = self.sb("lt", [128, 64], F32)
        ls = self.sb("ls", [128, 4], F32)
        lv = gs_("diff_lambda", 0, 128).rearrange("p (a b e) -> p a b e", a=2, b=2, e=32)
        k.tt("dve", lt, lt[:, :].rearrange("p (a e) -> p a e", e=32), lv[:, :, 0, :], lv[:, :, 1, :], ALU.mult, [gsm])
        k.op("dve", lambda e: e.tensor_reduce(out=ls[:, 0:2], in_=lt[:, :].rearrange("p (a e) -> p a e", e=32),
                                              axis=AX.X, op=ALU.add), reads=[lt], writes=[ls])
        k.act(ls, ls[:, 2:4], ls[:, 0:2], AF.Exp, [ls])
        k.tt("dve", ls, ls[:, 0:1], ls[:, 2:3], ls[:, 3:4], ALU.subtract, [ls])
        k.ts("dve", self.lam, self.lam[:, l:l + 1], ls[:, 0:1], lam_init, None, ALU.add, None, [ls])
        ab = self.sb("ab", [128, 16], F32)
        k.ts("dve", ab, ab[:, 0:8], self.modcol(l, 1), 1.0, None, ALU.add, None, [self.modT])
        k.tt("dve", ab, ab[:, 0:8], ab[:, 0:8], self.colv[:, 96 + 8 * l:104 + 8 * l], ALU.mult, [self.colv])
        k.cp("dve", ab, ab[:, 8:16], self.modcol(l, 0), [self.modT])
        sq = self.sb("sq", [128, D], F32)
        qn = self.sb("qn", [128, 512], F32)
        ssg = self.sb("ssg", [128, 48], F32)
        rawT = self.sb("rawT", [128, S + 16], F32)
        fl = self.sb("fl", [128, 32, 4], F32)
        gl = self.sb("gl", [128, 32, 12], F32)
        self.push()
        w = self.sb("win", [128, 8, PIN], BF16)
        src = I["w_in"][l].rearrange("(kc p) n -> p kc n", p=128)
        for name, s0, wd in _SEGS:
            o = WOFF[name]
            k.dma("pool", w[:, :, o:o + wd], src[:, :, s0:s0 + wd], writes=[w])
        xts = [self.sb("xt%d" % i, [128, D], F32) for i in range(2)]
        xn = self.sb("xn", [128, D], F32)
        sss = [self.sb("ss%d" % i, [128, 4], F32) for i in range(2)]
        hTs = [self.sb("hT%d" % i, [128, 8, 128], BF16) for i in range(2)]
        stgs = [self.sb("stg%d" % i, [128, 18 * 128], BF16) for i in range(2)]
        qsts = [self.sb("qst%d" % i, [128, 18, 512], BF16) for i in range(2)]
        vsts = [self.sb("vst%d" % i, [128, 650], BF16) for i in range(2)]
        cvb = self.sb("cvb", [128, 256], BF16)
        cvst = [self.sb("cvst%d" % i, [128, 256], BF16) for i in range(2)]
        for s_ in stgs:
            k.op("pool", lambda e, s_=s_: e.memset(s_[:, :], 0.0), writes=[s_])
        for v_ in vsts:
            k.op("pool", lambda e, v_=v_: e.memset(v_[:, :], 1.0), writes=[v_])
        k.op("pool", lambda e: e.memset(rawT[:, S:S + 16], 0.0), writes=[rawT])
        qkTv = self.qkT.rearrange("c p s -> p c s")
        qd = self.dd["qkT"]

        def normev(bank, n, gs):
            ng = n // gs
            k.act(sq, sq[:, 0:n], bank[:, 0:n], AF.Square, [bank])
            k.op("dve", lambda e: e.tensor_reduce(out=ssg[:, 0:ng], in_=sq[:, 0:n].rearrange("p (g e) -> p g e", e=gs),
                                                  axis=AX.X, op=ALU.add), reads=[sq], writes=[ssg])
            k.act(ssg, ssg[:, 16:16 + ng], ssg[:, 0:ng], AF.Ln, [ssg], bias=self.epsc[:, 0:1], scale=1.0 / gs)
            k.act(ssg, ssg[:, 32:32 + ng], ssg[:, 16:16 + ng], AF.Exp, [ssg], scale=-0.5)
            k.tt("dve", qn, qn[:, 0:n].rearrange("p (g e) -> p g e", e=gs),
                 bank[:, 0:n].rearrange("p (g e) -> p g e", e=gs), blast(ssg[:, 32:32 + ng], gs), ALU.mult, [bank, ssg])

        def front(t):
            xt = xts[t % 2]
            ss = sss[t % 2]
            hT = hTs[t % 2]
            stg = stgs[t % 2]
            vst = vsts[t % 2]
            qst = qsts[(t // 4) % 2]
            k.dma("sp", xt[:, :], xsrc[t * 128:(t + 1) * 128, :], reads=[xsd] if xsd else [], writes=[xt])
            k.act(sq, sq[:, :], xt[:, :], AF.Square, [xt], accum=ss[:, 0:1], extra_w=[ss])
            k.act(ss, ss[:, 1:2], ss[:, 0:1], AF.Ln, [ss], bias=self.epsc[:, 0:1], scale=1.0 / D)
            k.act(ss, ss[:, 2:3], ss[:, 1:2], AF.Exp, [ss], scale=-0.5)
            k.ts("dve", xn, xn[:, :], xt[:, :], ss[:, 2:3], None, ALU.mult, None, [xt, ss])
            for c in range(8):
                b = ps[c // 4]
                self.tr(b, b[:, (c % 4) * 128:(c % 4 + 1) * 128], xn[:, c * 128:(c + 1) * 128], ident, [xn, cf])
            for c in range(8):
                b = ps[c // 4]
                pin = b[:, (c % 4) * 128:(c % 4 + 1) * 128]
                if c % 2 == 0:
                    k.ts("dve", hT, hT[:, c, :], pin, ab[:, c:c + 1], ab[:, 8 + c:9 + c], ALU.mult, ALU.add, [b, ab])
                else:
                    k.act(hT, hT[:, c, :], pin, AF.Identity, [b, ab], bias=ab[:, 8 + c:9 + c], scale=ab[:, c:c + 1])

            def proj(bank, a, n):
                for c in range(8):
                    k.mm(bank, bank[:, 0:n], hT[:, c, :], w[:, c, a:a + n], [hT, w], start=(c == 0), stop=(c == 7))
            proj(ps[2], 0, 512)
            normev(ps[2], 512, 32)
            for m in range(2):
                srcq = qn[:, 0:256].rearrange("p (hh hl m e) -> p hh hl m e", hh=2, hl=2, m=2)[:, :, :, m, :]
                gq = gA[:, 0:256].rearrange("p (hh hl m e) -> p hh hl m e", hh=2, hl=2, m=2)[:, :, :, m, :]
                dst = stg[:, 0:512].rearrange("p (hh m hl f) -> p hh m hl f", hh=2, m=2, hl=2)[:, :, m, :, 32 * m:32 * m + 32]
                k.tt("pool", stg, dst, srcq, gq, ALU.mult, [qn, gA])
            k.tt("pool", stg, stg[:, 512:768], qn[:, 256:512], gA[:, 256:512], ALU.mult, [qn, gA])
            proj(ps[3], 512, 512)
            normev(ps[3], 512, 64)
            k.tt("pool", stg, stg[:, 768:1280], qn[:, 0:512], gB[:, :], ALU.mult, [qn, gB])
            proj(ps[4], 1024, 384)
            normev(ps[4], 384, 64)
            k.tt("pool", stg, stg[:, 1792:2048], qn[:, 0:256], gD[:, 0:256], ALU.mult, [qn, gD])
            for dup in range(2):
                k.tt("pool", stg, stg[:, 2048 + dup * 64:2112 + dup * 64], qn[:, 256:320], gD[:, 256:320], ALU.mult, [qn, gD])
                k.tt("pool", stg, stg[:, 2176 + dup * 64:2240 + dup * 64], qn[:, 320:384], gD[:, 320:384], ALU.mult, [qn, gD])
            proj(ps[5], 1408, 512)
            k.act(stg, stg[:, 1280:1536], ps[5][:, 0:256], AF.Copy, [ps[5]], scale=0.125)
            k.cp("act", stg, stg[:, 1536:1792], ps[5][:, 256:512], [ps[5]])
            proj(ps[2], 1920, 512)
            k.cp("act", vst, vst[:, 0:520].rearrange("p (h e) -> p h e", e=65)[:, :, 0:64],
                 ps[2][:, 0:512].rearrange("p (h e) -> p h e", e=64), [ps[2]])
            proj(ps[3], 2432, 400)
            k.cp("dve", cvb, cvb[:, :], ps[3][:, 0:256], [ps[3]])
            k.cp("dve", vst, vst[:, 520:584], ps[3][:, 256:320], [ps[3]])
            k.cp("dve", vst, vst[:, 585:649], ps[3][:, 320:384], [ps[3]])
            k.cp("dve", fl, fl[:, t, :], ps[3][:, 384:388], [ps[3]])
            k.cp("dve", gl, gl[:, t, :], ps[3][:, 388:400], [ps[3]])
            k.dma("pool", self.vtm[t * 128:(t + 1) * 128, :], vst[:, :], reads=[vst], writes=[self.dd["vtm"]])
            k.mm(ps[4], ps[4][:, 0:256], self.antib[:, :], cvb[:, :], [cvb, self.antib])
            cvs = cvst[t % 2]
            k.cp("act", cvs, cvs[:, :], ps[4][:, 0:256], [ps[4]])
            k.dma("pool", self.cvr[(31 - t) * 128:(32 - t) * 128, :], cvs[:, :], reads=[cvs], writes=[self.dd["cvr"]])
            for c in range(8):
                k.mm(ps[5], ps[5][:, 0:128], w[:, c, 2832:2960], hT[:, c, :], [hT, w], start=(c == 0), stop=(c == 7))
            k.cp("act", rawT, rawT[:, t * 128:(t + 1) * 128], ps[5][:, 0:128], [ps[5]])
        def back(t):
            stg = stgs[t % 2]
            qst = qsts[(t // 4) % 2]
            for g4 in range(5):
                chs = list(range(g4 * 4, min(g4 * 4 + 4, 18)))
                bank = [ps[6], ps[7]][(t * 5 + g4) % 2]
                for ci, ch in enumerate(chs):
                    idn = self.antib if ch in (12, 13) else self.identb
                    k.mm(bank, bank[:, ci * 128:(ci + 1) * 128], stg[:, ch * 128:(ch + 1) * 128], idn[:, :], [stg, idn])
                n = len(chs)
                for ci, ch in enumerate(chs):
                    slot = (3 - t % 4) if ch in (12, 13) else (t % 4)
                    eng = "act" if ci % 2 == 0 else "dve"
                    k.cp(eng, qst, qst[:, ch, slot * 128:(slot + 1) * 128], bank[:, ci * 128:(ci + 1) * 128], [bank])
            if t % 4 == 3:
                g = t // 4
                k.dma("pool", qkTv[:, 0:12, g * 512:(g + 1) * 512], qst[:, 0:12, :], reads=[qst], writes=[qd])
                k.dma("pool", qkTv[:, 12:14, (7 - g) * 512:(8 - g) * 512], qst[:, 12:14, :], reads=[qst], writes=[qd])
                k.dma("pool", qkTv[:, 14:18, g * 512:(g + 1) * 512], qst[:, 14:18, :], reads=[qst], writes=[qd])

        for t in range(NT + 1):
            if t < NT:
                front(t)
            if t >= 1:
                back(t - 1)
        self.pop()
        glf = gl[:, :, :].rearrange("p t g -> p (t g)")
        gaf = self.gates[:, :, :].rearrange("p t g -> p (t g)")
        k.act(self.gates, gaf, glf, AF.Exp, [gl], scale=-1.0)
        k.ts("dve", self.gates, gaf, gaf, 1.0, None, ALU.add, None, [self.gates])
        k.op("dve", lambda e: e.reciprocal(out=gaf, in_=gaf), reads=[self.gates], writes=[self.gates])
        sp = self.sb("fsp", [128, 128], F32)
        spv = sp[:, :].rearrange("p (t h) -> p t h", h=4)
        k.tt("dve", sp, spv, fl[:, :, :], bmid(gs_("fox_b_f", 0, 4), 32), ALU.add, [fl, gsm])
        k.act(sp, sp[:, :], sp[:, :], AF.Exp, [sp], scale=-1.0)
        k.act(sp, sp[:, :], sp[:, :], AF.Ln, [sp], bias=self.epsc[:, 1:2])
        k.mm(ps[0], ps[0][:, 0:128], self.cfs("U"), sp[:, :], [sp, cf])
        k.mm(ps[1], ps[1][:, 0:128], self.cfs("ones"), sp[:, :], [sp, cf])
        tot = self.sb("ftot", [128, 128], F32)
        inc = self.sb("finc", [128, 128], F32)
        cn = self.sb("fcn", [128, 128], F32)
        k.cp("dve", tot, tot[:, :], ps[1][:, 0:128], [ps[1]])
        totv = tot[:, :].rearrange("p (t h) -> p t h", h=4)
        incv = inc[:, :].rearrange("p (t h) -> p t h", h=4)
        for h in range(4):
            k.op("dve", lambda e, h=h: e.tensor_tensor_scan(out=incv[:, :, h], data0=totv[:, :, h],
                                                            data1=self.zerob[:, 0:32], initial=0.0,
                                                            op0=ALU.add, op1=ALU.add), reads=[tot, cf], writes=[inc])
        k.tt("dve", inc, inc[:, :], inc[:, :], tot[:, :], ALU.subtract, [tot])
        k.tt("dve", cn, cn[:, :], ps[0][:, 0:128], inc[:, :], ALU.add, [ps[0], inc])
        parts = [self.sb("fpart%d" % i, [128, 128], BF16) for i in range(3)]
        k.cp("dve", parts[0], parts[0][:, :], cn[:, :], [cn])
        k.tt("dve", cn, cn[:, :], cn[:, :], parts[0][:, :], ALU.subtract, [parts[0]])
        k.cp("dve", parts[1], parts[1][:, :], cn[:, :], [cn])
        k.tt("dve", cn, cn[:, :], cn[:, :], parts[1][:, :], ALU.subtract, [parts[1]])
        k.cp("dve", parts[2], parts[2][:, :], cn[:, :], [cn])
        aug = [self.sb("aug%d" % i, [128, 32, 128], BF16) for i in range(4)]
        for a_ in aug:
            k.op("pool", lambda e, a_=a_: e.memset(a_[:, :, :], 0.0), writes=[a_])
        for pair in range(2):
            aq, ak = aug[pair], aug[2 + pair]
            for hl in range(2):
                h = 2 * pair + hl
                for r in range(3):
                    pv = parts[r][:, :].rearrange("p (t h) -> p t h", h=4)[:, :, h]
                    k.ts("dve", aq, aq[:, :, hl * 64 + r], pv, -1.0, None, ALU.mult, None, [parts[r]])
                    k.cp("dve", ak, ak[:, :, hl * 64 + 3 + r], pv, [parts[r]])
                k.op("pool", lambda e, aq=aq, hl=hl: e.memset(aq[:, :, hl * 64 + 3:hl * 64 + 6], 1.0), writes=[aq])
                k.op("pool", lambda e, ak=ak, hl=hl: e.memset(ak[:, :, hl * 64:hl * 64 + 3], 1.0), writes=[ak])
        ast = [self.sb("augst%d" % i, [128, 512], BF16) for i in range(2)]
        n_ = 0
        for ai in range(4):
            for g in range(8):
                bank = ps[2 + n_ % 2]
                st_ = ast[n_ % 2]
                n_ += 1
                for tt_ in range(4):
                    k.mm(bank, bank[:, tt_ * 128:(tt_ + 1) * 128], aug[ai][:, g * 4 + tt_, :], self.identb[:, :],
                         [aug[ai], self.identb])
                k.cp("act", st_, st_[:, :], bank[:, :], [bank])
                k.dma("pool", self.baug[ai, :, g * 512:(g + 1) * 512], st_[:, :], reads=[st_], writes=[self.dd["baug"]])
        phibd = self.sb("phibd", [128, 32, 128], BF16)
        k.op("pool", lambda e: e.memset(phibd[:, :, :], 0.0), writes=[phibd])
        for wi in range(2):
            k.dma("pool", phibd[wi * 64:(wi + 1) * 64, :, wi * 64:(wi + 1) * 64],
                  I["nsa_phi_w"][l, wi].rearrange("(r d) e -> d r e", d=64), writes=[phibd])
        pst = self.sb("pest", [32, 128], F32)
        for wi in range(2):
            k.dma("sp", pst[:, wi * 64:(wi + 1) * 64], I["nsa_pe"][l, wi], writes=[pst])
        self.tr(ps[4], ps[4][:, 0:32], pst[:, :], self.cfs("ident")[0:32, 0:32], [pst, cf])
        peT = self.sb("peT", [128, 32], F32)
        k.cp("dve", peT, peT[:, :], ps[4][:, 0:32], [ps[4]])
        tmps = [self.sb("ctmp%d" % i, [128, 256], BF16) for i in range(4)]
        for t_ in tmps:
            k.op("pool", lambda e, t_=t_: e.memset(t_[:, :], 0.0), writes=[t_])
        r0 = rawT[:, 0:1]
        for r in range(32):
            tm = tmps[r % 4]
            src_ = bass.AP(tensor=r0.tensor, offset=r0.offset + r, ap=[list(r0.ap[0]), [16, 255]])
            k.ts("dve", tm, tm[:, 0:255], src_, peT[:, r:r + 1], None, ALU.add, None, [rawT, peT])
            for nb in range(2):
                k.mm(ps[5 + nb], ps[5 + nb][:, 0:128], tm[:, nb * 128:(nb + 1) * 128], phibd[:, r, :], [tm, phibd],
                     start=(r == 0), stop=(r == 31))
        kcn = self.sb("kcn", [128, 2, 128], BF16)
        for nb in range(2):
            bank = ps[5 + nb]
            k.act(sq, sq[:, 0:64], bank[:, 0:64], AF.Square, [bank], accum=ssg[:, 0:1], extra_w=[ssg])
            k.act(ssg, ssg[:, 1:2], ssg[:, 0:1], AF.Ln, [ssg], bias=self.epsc[:, 0:1], scale=1.0 / 64)
            k.act(ssg, ssg[:, 2:3], ssg[:, 1:2], AF.Exp, [ssg], scale=-0.5)
            k.ts("dve", qn, qn[:, 0:64], bank[:, 0:64], ssg[:, 2:3], None, ALU.mult, None, [bank, ssg])
            for dup in range(2):
                k.tt("dve", kcn, kcn[:, nb, dup * 64:(dup + 1) * 64], qn[:, 0:64], gs_("nsa_knorm_g", 0, 64), ALU.mult, [qn, gsm])
            k.cp("dve", self.vcs, self.vcs[:, nb, :], bank[:, 64:128], [bank])
        for nb in range(2):
            k.mm(ps[7], ps[7][:, nb * 128:(nb + 1) * 128], kcn[:, nb, :], self.identb[:, :], [kcn, self.identb])
        k.cp("dve", self.kcT, self.kcT[:, :], ps[7][:, 0:256], [ps[7]])
        k.cp("dve", self.subg, self.subg[:, :], subg[:, :], [subg])
        if self.dbg:
            k.dma("pool", self.qkT[18, :, 0:256], self.kcT[:, :], reads=[self.kcT], writes=[qd])
        self.pop()


def make_inputs(inp, b):
    f = np.float32
    m = {
        "x": np.ascontiguousarray(inp["x"][b], dtype=f),
        "c": np.ascontiguousarray(inp["c"][b].reshape(8, 128), dtype=f),
        "rel_bias": np.ascontiguousarray(inp["rel_bias"].reshape(1, 256), dtype=f),
        "ada_w": np.ascontiguousarray(inp["ada_w"], dtype=f),
        "ada_b": np.ascontiguousarray(inp["ada_b"].reshape(2, 48, 128), dtype=f),
        "norm_mix_g": np.ascontiguousarray(inp["norm_mix_g"].reshape(2, 8, 128), dtype=f),
        "norm_ffn_g": np.ascontiguousarray(inp["norm_ffn_g"].reshape(2, 8, 128), dtype=f),
        "w_in": np.ascontiguousarray(inp["w_in"], dtype=f),
        "w_out": np.ascontiguousarray(inp["w_out"], dtype=f),
        "diff_qnorm_g": inp["diff_qnorm_g"].reshape(2, 1, 32), "diff_knorm_g": inp["diff_knorm_g"].reshape(2, 1, 32),
        "diff_lambda": inp["diff_lambda"].reshape(2, 1, 128), "diff_subln_g": inp["diff_subln_g"].reshape(2, 1, 64),
        "fox_qnorm_g": inp["fox_qnorm_g"].reshape(2, 1, 64), "fox_knorm_g": inp["fox_knorm_g"].reshape(2, 1, 64),
        "fox_b_f": inp["fox_b_f"].reshape(2, 1, 4), "nsa_qnorm_g": inp["nsa_qnorm_g"].reshape(2, 1, 64),
        "nsa_knorm_g": inp["nsa_knorm_g"].reshape(2, 1, 192), "nsa_pe": inp["nsa_pe"], "nsa_phi_w": inp["nsa_phi_w"],
        "ffn_w_up": inp["ffn_w_up"], "ffn_conv_w": inp["ffn_conv_w"].reshape(2, 66, 128),
        "ffn_conv_b": inp["ffn_conv_b"].reshape(2, 22, 128), "ffn_w_down": inp["ffn_w_down"],
        "cf": _CF_ARR, "cf2": _CF2_ARR, "cex": _EX_ARR,
    }
    return {k_: np.ascontiguousarray(v, dtype=f) for k_, v in m.items()}


def _phase_b(self, l):
    k, I, cf, ps = self.k, self.I, self.cf, self.ps
    self.push()
    tthi = self.sb("tthi", [128, 8, 256], BF16)
    ttlo = self.sb("ttlo", [128, 8, 256], BF16)
    brel = self.sb("brel", [128, 4, 512], F32)
    k.dma("sp", tthi[:, :, :].rearrange("p h j -> p (h j)"), self.tts[:, 0:2048], reads=[self.dd["tts"]], writes=[tthi])
    k.dma("sp", ttlo[:, :, :].rearrange("p h j -> p (h j)"), self.tts[:, 2048:4096], reads=[self.dd["tts"]], writes=[ttlo])
    k.dma("sp", brel[:, :, :].rearrange("p h j -> p (h j)"), self.brs[:, :], reads=[self.dd["brs"]], writes=[brel])
    pTs = [self.sb("pT%d" % i, [128, 512], BF16) for i in range(6)]
    oTs = [self.sb("oTs%d" % i, [128, 512], F32) for i in range(2)]
    osts = [self.sb("ost%d" % i, [128, 4, 64], BF16) for i in range(4)]
    rzs = [self.sb("rz%d" % i, [128, 8], F32) for i in range(2)]
    ons = [self.sb("on%d" % i, [128, 4, 64], F32) for i in range(2)]
    sqss = [self.sb("sqs%d" % i, [128, 4, 64], F32) for i in range(2)]
    qkTv = self.qkT
    qd = self.dd["qkT"]
    ident = self.cfs("ident")
    cnt = {"s": 0, "c": 0, "o": 0}

    def load_chunk(name, ch, src=None, dep=None):
        t = self.sb(name, [128, S], BF16)
        s_ = qkTv[ch] if src is None else src
        for hh in range(2):
            k.dma("sp" if hh == 0 else "act", t[:, hh * 2048:(hh + 1) * 2048], s_[:, hh * 2048:(hh + 1) * 2048],
                  reads=[qd if dep is None else dep], writes=[t])
        return t

    def load_rows(t, r0, src_rows, dep=None, q="sp"):
        n_ = src_rows.shape[0]
        for hh in range(2):
            k.dma(q if hh == 0 else "act", t[r0:r0 + n_, hh * 2048:(hh + 1) * 2048], src_rows[:, hh * 2048:(hh + 1) * 2048],
                  reads=[qd if dep is None else dep], writes=[t])

    def padded(name, parts):
        t = self.sb(name, [128, S], BF16)
        cnt["pad"] = cnt.get("pad", 0) + 1
        k.op("dve" if cnt["pad"] % 2 == 0 else "pool", lambda e: e.memset(t[:, :], 0.0), writes=[t])
        for (r0, rows, dep) in parts:
            load_rows(t, r0, rows, dep)
        return t
    self.padded = padded
    self.load_rows = load_rows

    def load_vaug(name, c0, nh):
        t = self.sb(name, [128, 32, nh * 65], BF16)
        k.dma("sp", t[:, :, :], self.vtm[:, c0:c0 + nh * 65].rearrange("(t p) c -> p t c", p=128),
              reads=[self.dd["vtm"]], writes=[t])
        return t

    NS = 4
    NPT = 6
    sbanks = [ps[0], ps[1], ps[2], ps[4]]
    jobs2 = [[], []]

    def softmax_head(qT, qdeps, kT, kdeps, vaug, vdeps, bias_h, act_bias, window, extras, out_cb, slot=0):
        jobs = jobs2[slot]
        tcount = 0
        for c in range(8):
            kbs = list(range(max(0, 4 * c - 4) if window else 0, 4 * c + 4))
            touched = set()
            if window:
                touched = {0, 1, 2, 3}
            acc = [[ps[6], ps[3]], [ps[7], ps[5]]][slot][c % 2]
            for kb in kbs:
                d0 = c * 512 - kb * 128
                segs = [s for s in range(4) if d0 + 128 * s >= 0 and (not window or d0 + 128 * s <= 512)]
                groups = []
                for s in segs:
                    ft = s not in touched
                    touched.add(s)
                    if groups and groups[-1][2] == ft:
                        groups[-1][1] = s
                    else:
                        groups.append([s, s, ft])
                jlo, jhi = segs[0] * 128, (segs[-1] + 1) * 128
                Sb = sbanks[(2 * tcount + slot) % NS]
                pT = pTs[(2 * tcount + slot) % NPT]
                tcount += 1
                first, last = (kb == kbs[0]), (kb == kbs[-1])

                def s0(c=c, kb=kb, d0=d0, segs=segs, Sb=Sb, jlo=jlo, jhi=jhi):
                    mms = [(Sb[:, jlo:jhi], kT[:, kb * 128:(kb + 1) * 128], qT[:, c * 512 + jlo:c * 512 + jhi], qdeps + kdeps)]
                    for (lf, rf, deps) in extras:
                        mms.append((Sb[:, jlo:jhi], lf(kb), rf(c * 512 + jlo, c * 512 + jhi), deps))
                    for s in segs:
                        dl = d0 + 128 * s
                        seg = Sb[:, s * 128:(s + 1) * 128]
                        if bias_h is not None and dl in (0, 128):
                            mms.append((seg, self.identb[:, :], tthi[:, bias_h, dl:dl + 128], [self.identb, tthi]))
                            mms.append((seg, self.identb[:, :], ttlo[:, bias_h, dl:dl + 128], [self.identb, ttlo]))
                        elif bias_h is None and dl == 0:
                            mms.append((seg, self.identb[:, :], self.trib[:, :], [self.identb, self.trib]))
                        if window and dl == 512:
                            mms.append((seg, self.identb[:, :], self.twb[:, :], [self.identb, self.twb]))
                    for i, (o_, a_, b_, deps) in enumerate(mms):
                        k.mm(Sb, o_, a_, b_, deps, start=(i == 0), stop=(i == len(mms) - 1))

                def s1(Sb=Sb, pT=pT, jlo=jlo, jhi=jhi):
                    if act_bias is None:
                        k.act(pT, pT[:, jlo:jhi], Sb[:, jlo:jhi], AF.Exp, [Sb])
                    else:
                        k.act(pT, pT[:, jlo:jhi], Sb[:, jlo:jhi], AF.Exp, [Sb, self.tab], bias=act_bias)

                def s2(kb=kb, segs=segs, pT=pT, acc=acc, first=first, last=last):
                    if first:
                        k.mm(acc, acc[:, 0:260], self.zerob[:, 0:128], self.zerob[:, 0:260], [self.zerob], start=True, stop=False)
                    for s in segs:
                        k.mm(acc, acc[:, s * 65:(s + 1) * 65], pT[:, s * 128:(s + 1) * 128], vaug(kb), [pT] + vdeps,
                             start=False, stop=last)
                stages = [s0, s1, s2]
                if last:
                    def s3(c=c, acc=acc):
                        out_cb(c, acc, acc[:, 0:260].rearrange("p (s e) -> p s e", e=65))
                    stages += [s3]
                jobs.append(stages)

    def run_jobs():
        ja, jb = jobs2
        merged = []
        for i in range(max(len(ja), len(jb))):
            sa = ja[i] if i < len(ja) else []
            sb_ = jb[i] if i < len(jb) else []
            st = []
            for si in range(max(len(sa), len(sb_))):
                fa = sa[si] if si < len(sa) else None
                fb = sb_[si] if si < len(sb_) else None
                st.append(lambda fa=fa, fb=fb: ((fa() if fa else None), (fb() if fb else None)))
            merged.append(st)
        run_staged(merged)
        del ja[:]
        del jb[:]
    self.run_jobs = run_jobs

    def normalize(bank, v3, dst_tt, dst_ap, slot=0):
        rz = rzs[slot]
        k.op("dve", lambda e: e.reciprocal(out=rz[:, 0:4], in_=v3[:, :, 64]), reads=[bank], writes=[rz])
        k.tt("dve", dst_tt, dst_ap, v3[:, :, 0:64], blast(rz[:, 0:4], 64), ALU.mult, [bank, rz])

    def store_o(c, col, src_tt, src_ap):
        ost = osts[cnt["o"] % 4]
        cnt["o"] += 1
        k.cp("act", ost, ost[:, :, :], src_ap, [src_tt])
        k.dma("pool", self.o_d[c * 512:(c + 1) * 512, col:col + 64].rearrange("(s p) e -> p s e", p=128), ost[:, :, :],
              reads=[ost], writes=[self.dd["o_d"]])

    if "A" in self.mixers:
        self.push()
        kA = [load_chunk("kA%d" % i, 4 + i) for i in range(2)]
        vA = load_vaug("vA", 0, 4)
        o1s = [self.sb("o1all%d" % i, [128, 32, 64], F32) for i in range(2)]
        nlam = self.sb("nlam", [128, 1], F32)
        k.ts("dve", nlam, nlam[:, :], self.lam[:, l:l + 1], -1.0, None, ALU.mult, None, [self.lam])
        qpad = {}
        for hp in range(2):
            for m in range(2):
                for slot in range(2):
                    pb = 64 * slot
                    qpad[(hp, m, slot)] = padded("qAp%d%d%d" % (hp, m, slot), [(pb, qkTv[2 * hp + m][pb:pb + 64, :], None)])
        for hp in range(2):
            for m in range(2):
                for slot in range(2):
                    h = 2 * hp + slot
                    pb = 64 * slot
                    qt = qpad[(hp, m, slot)]
                    kt = kA[hp]

                    def cb(c, bank, v3, h=h, m=m, slot=slot):
                        o1, on, sqs, rz = o1s[slot], ons[slot], sqss[slot], rzs[slot]
                        if m == 0:
                            normalize(bank, v3, o1, o1[:, 4 * c:4 * c + 4, :], slot)
                        else:
                            normalize(bank, v3, on, on[:, :, :], slot)
                            k.op("dve", lambda e: e.scalar_tensor_tensor(
                                out=on[:, :, :].rearrange("p s e -> p (s e)"), in0=on[:, :, :].rearrange("p s e -> p (s e)"),
                                scalar=nlam[:, 0:1], in1=o1[:, 4 * c:4 * c + 4, :].rearrange("p s e -> p (s e)"),
                                op0=ALU.mult, op1=ALU.add), reads=[on, o1, nlam], writes=[on])
                            k.act(sqs, sqs[:, :, :], on[:, :, :], AF.Square, [on])
                            k.op("dve", lambda e: e.tensor_reduce(out=rz[:, 4:8], in_=sqs[:, :, :], axis=AX.X, op=ALU.add),
                                 reads=[sqs], writes=[rz])
                            k.act(rz, rz[:, 4:8], rz[:, 4:8], AF.Ln, [rz], bias=self.epsc[:, 0:1], scale=1.0 / 64)
                            k.act(rz, rz[:, 4:8], rz[:, 4:8], AF.Exp, [rz], scale=-0.5)
                            k.tt("dve", on, on[:, :, :], on[:, :, :], blast(rz[:, 4:8], 64), ALU.mult, [rz])
                            k.tt("dve", on, on[:, :, :], on[:, :, :], bmid(self.subg[:, :], 4), ALU.mult, [self.subg])
                            store_o(c, h * 64, on, on[:, :, :])
                    softmax_head(qt[:, :], [qt], kt[:, :], [kt],
                                 lambda kb, h=h: vA[:, kb, h * 65:(h + 1) * 65], [vA], h, None,
                                 False, [], cb, slot)
        run_jobs()
        self.pop()

    if "B" in self.mixers:
        self.push()
        bd = self.dd["baug"]
        qBp, kBp = [], []
        for h in range(4):
            pr, pb = h // 2, 64 * (h % 2)
            qBp.append(padded("qBp%d" % h, [(0, qkTv[6 + pr][pb:pb + 64, :], None), (64, self.baug[pr][pb:pb + 6, :], bd)]))
            kBp.append(padded("kBp%d" % h, [(0, qkTv[8 + pr][pb:pb + 64, :], None), (64, self.baug[2 + pr][pb:pb + 6, :], bd)]))
        vB = load_vaug("vB", 260, 4)
        for pair in range(2):
            for slot in range(2):
                h = 2 * pair + slot
                pb = 64 * slot

                def cb(c, bank, v3, h=h, slot=slot):
                    on = ons[slot]
                    normalize(bank, v3, on, on[:, :, :], slot)
                    store_o(c, 256 + h * 64, on, on[:, :, :])
                softmax_head(qBp[h][:, :], [qBp[h]], kBp[h][:, :], [kBp[h]],
                             lambda kb, h=h: vB[:, kb, h * 65:(h + 1) * 65], [vB], None, None, False, [], cb, slot)
            run_jobs()
        self.pop()

    if "C" in self.mixers:
        self.mixer_c(l, load_chunk)

    if "D" in self.mixers:
        self.mixer_d(l, load_chunk, load_vaug, softmax_head, normalize, tthi, ttlo, brel)
    self.pop()


Prog.phase_b = _phase_b


def run_staged(jobs):
    if not jobs:
        return
    ns = max(len(j) for j in jobs)
    for step in range(len(jobs) + ns):
        for si in range(ns - 1, -1, -1):
            j = step - si
            if 0 <= j < len(jobs) and si < len(jobs[j]):
                jobs[j][si]()


def _mixer_c(self, l, load_chunk):
    k, ps, cf = self.k, self.ps, self.cf
    self.push()
    qC = [self.padded("qCp%d" % h, [(64 * (h % 2), self.qkT[10 + h // 2][64 * (h % 2):64 * (h % 2) + 64, :], None)])
          for h in range(4)]
    kC = [load_chunk("kC%d" % i, 12 + i) for i in range(2)]
    vC = self.sb("vC", [128, 32, 256], BF16)
    k.dma("sp", vC[:, :, :], self.cvr.rearrange("(t p) c -> p t c", p=128), reads=[self.dd["cvr"]], writes=[vC])
    NB = 9
    SPb = [self.sb("cSP%d" % i, [128, 512], F32) for i in range(NB)]
    PRb = [self.sb("cPR%d" % i, [128, 512], F32) for i in range(NB)]
    Eb = [self.sb("cE%d" % i, [128, 512], F32) for i in range(NB)]
    ab = [self.sb("ca%d" % i, [128, 512], BF16) for i in range(NB)]
    aTb = [self.sb("caT%d" % i, [128, 512], BF16) for i in range(NB)]
    oc = [self.sb("coc%d" % i, [128, 256], BF16) for i in range(2)]
    zeros = self.zerob
    zbanks = [ps[0], ps[1], ps[2]]
    tbanks = [ps[3], ps[4], ps[5]]
    jobs = []
    n = 0
    for i in range(32):
        ost = oc[i % 2]
        accb = ps[6 + i % 2]
        k0 = 128 * (31 - i)
        nblk = i + 1
        chunks = [(k0 + 512 * j, min(512, 128 * nblk - 512 * j)) for j in range((nblk + 3) // 4)]
        carries = [None] * 4
        nch = len(chunks)
        for ci, (ks_, w_) in enumerate(chunks):
            for h in range(4):
                pb = 64 * (h % 2)
                qt, kt = qC[h], kC[h // 2]
                zb, tb = zbanks[n % 3], tbanks[n % 3]
                SP, PR, E, a, at = SPb[n % NB], PRb[n % NB], Eb[n % NB], ab[n % NB], aTb[n % NB]
                n += 1
                car = carries[h]
                carries[h] = (PR, PR[:, w_ - 1:w_])
                nb_ = w_ // 128

                def s0(i=i, pb=pb, qt=qt, kt=kt, zb=zb, ks_=ks_, w_=w_):
                    k.mm(zb, zb[:, 0:w_], qt[pb:pb + 64, i * 128:(i + 1) * 128], kt[pb:pb + 64, ks_:ks_ + w_], [qt, kt])

                def s1(zb=zb, E=E, w_=w_, ci=ci):
                    k.act(E, E[:, 0:w_], zb[:, 0:w_], AF.Exp, [zb])
                    if ci == 0:
                        k.tt("pool", E, E[:, 0:128], E[:, 0:128], self.cfs("M01"), ALU.mult, [cf])

                def s2(E=E, SP=SP, w_=w_):
                    k.act(SP, SP[:, 0:w_], E[:, 0:w_], AF.Ln, [E, self.epsc], bias=self.epsc[:, 1:2])

                def s3(SP=SP, PR=PR, w_=w_, car=car):
                    if car is None:
                        k.op("dve", lambda e: e.tensor_tensor_scan(
                            out=PR[:, 0:w_], data0=SP[:, 0:w_], data1=zeros[:, 0:w_], initial=0.0, op0=ALU.add, op1=ALU.add),
                            reads=[SP, zeros], writes=[PR])
                    else:
                        ctt, cap = car
                        k.op("dve", lambda e: e.tensor_tensor_scan(
                            out=PR[:, 0:w_], data0=SP[:, 0:w_], data1=zeros[:, 0:w_], initial=cap, op0=ALU.add, op1=ALU.add),
                            reads=[SP, zeros, ctt], writes=[PR])

                def s4(SP=SP, PR=PR, w_=w_):
                    k.act(SP, SP[:, 0:w_], PR[:, 0:w_], AF.Exp, [PR], scale=-1.0)

                def s5(SP=SP, E=E, a=a, w_=w_):
                    k.tt("pool", a, a[:, 0:w_], E[:, 0:w_], SP[:, 0:w_], ALU.mult, [E, SP])

                def s6(a=a, tb=tb, nb_=nb_):
                    for b in range(nb_):
                        k.mm(tb, tb[:, b * 128:(b + 1) * 128], a[:, b * 128:(b + 1) * 128], self.identb[:, :], [a, self.identb])

                def s7(at=at, tb=tb, w_=w_):
                    k.cp("dve", at, at[:, 0:w_], tb[:, 0:w_], [tb])

                def s8(at=at, accb=accb, h=h, ci=ci, ks_=ks_, nb_=nb_, nch=nch):
                    if ci == 0 and h == 0:
                        k.mm(accb, accb[:, 0:256], self.zerob[:, 0:128], self.zerob[:, 0:256], [self.zerob], start=True, stop=False)
                    for b in range(nb_):
                        kblk = ks_ // 128 + b
                        k.mm(accb, accb[:, h * 64:(h + 1) * 64], at[:, b * 128:(b + 1) * 128], vC[:, kblk, h * 64:(h + 1) * 64],
                             [at, vC], start=False, stop=(ci == nch - 1 and b == nb_ - 1 and h == 3))

                stages = [s0, s1, s2, s3, s4, s5, s6, s7, s8]
                if ci == nch - 1 and h == 3:
                    def s9(accb=accb, ost=ost, i=i):
                        k.cp("act", ost, ost[:, :], accb[:, 0:256], [accb])

                    def s10(ost=ost, i=i):
                        k.dma("pool", self.o_d[i * 128:(i + 1) * 128, 512:768], ost[:, :], reads=[ost], writes=[self.dd["o_d"]])
                    stages += [s9, s10]
                jobs.append(stages)
    run_staged(jobs)
    self.pop()


def _mixer_d(self, l, load_chunk, load_vaug, softmax_head, normalize, tthi, ttlo, brel):
    k, ps, cf, I = self.k, self.ps, self.cf, self.I
    self.push()
    qD = [load_chunk("qD%d" % i, 14 + i) for i in range(2)]
    oD = self.sb("oDacc", [128, 32, 256], F32)
    nmT = self.sb("nmT", [128, S], BF16)
    kcT, vcs = self.kcT, self.vcs
    NBc = 6
    sC = [self.sb("dsC%d" % i, [128, 256], F32) for i in range(NBc)]
    eC = [self.sb("deC%d" % i, [128, 256], F32) for i in range(NBc)]
    pbf = [self.sb("dpb%d" % i, [128, 256], BF16) for i in range(NBc)]
    pTc = [self.sb("dpT%d" % i, [128, 256], BF16) for i in range(NBc)]
    psumCs = [self.sb("dpsum%d" % i, [128, 256], F32) for i in range(3)]
    zcs = [self.sb("dzc%d" % i, [128, 8], F32) for i in range(3)]
    imps = [self.sb("dimp%d" % i, [128, 64], F32) for i in range(3)]
    imp2s = [self.sb("dimp2%d" % i, [128, 64], F32) for i in range(3)]
    nmfs = [self.sb("dnmf%d" % i, [128, 64], F32) for i in range(3)]
    m8s = [self.sb("dm8%d" % i, [128, 16], F32) for i in range(3)]
    nms = [self.sb("dnm%d" % i, [128, 128], BF16) for i in range(3)]
    dons = [self.sb("don%d" % i, [128, 4, 64], F32) for i in range(2)]
    import os
    dstage = int(os.environ.get("DSTAGE", "9"))
    jobs = []
    n = 0
    zbanks = [ps[0], ps[1], ps[2]]
    for i in range(32):
        off = 255 - 8 * i
        nnb = 1 if i < 16 else 2
        psumC, zc, imp, imp2, nmf, m8, nm = (psumCs[i % 3], zcs[i % 3], imps[i % 3], imp2s[i % 3], nmfs[i % 3],
                                             m8s[i % 3], nms[i % 3])
        ocb = [ps[4], ps[6]][i % 2]
        nmb = [ps[5], ps[7]][i % 2]
        for h in range(4):
            pb = 64 * (h % 2)
            qt = qD[h // 2]
            zb = zbanks[n % 3]
            tb = ps[3]
            s_, e_, pb_, pt_ = sC[n % NBc], eC[n % NBc], pbf[n % NBc], pTc[n % NBc]
            n += 1

            def s0(i=i, pb=pb, qt=qt, zb=zb):
                k.mm(zb, zb[:, 0:256], qt[pb:pb + 64, i * 128:(i + 1) * 128], kcT[pb:pb + 64, :], [qt, kcT])

            def s1(zb=zb, s_=s_, h=h, off=off):
                k.tt("dve", s_, s_[:, :], zb[:, 0:256], brel[:, h, off:off + 256], ALU.add, [zb, brel])

            def s2(s_=s_, e_=e_, zc=zc, h=h):
                k.act(e_, e_[:, :], s_[:, :], AF.Exp, [s_], accum=zc[:, h:h + 1], extra_w=[zc])

            def s3(e_=e_, pb_=pb_, zc=zc, psumC=psumC, h=h):
                k.ts("dve", zc, zc[:, 4 + h:5 + h], zc[:, h:h + 1], 1e-30, None, ALU.add, None, [zc])
                k.op("dve", lambda e: e.reciprocal(out=zc[:, 4 + h:5 + h], in_=zc[:, 4 + h:5 + h]), reads=[zc], writes=[zc])
                k.ts("dve", pb_, pb_[:, :], e_[:, :], zc[:, 4 + h:5 + h], None, ALU.mult, None, [e_, zc])
                if h == 0:
                    k.ts("dve", psumC, psumC[:, :], e_[:, :], zc[:, 4:5], None, ALU.mult, None, [e_, zc])
                else:
                    k.op("dve", lambda e: e.scalar_tensor_tensor(
                        out=psumC[:, :], in0=e_[:, :], scalar=zc[:, 4 + h:5 + h], in1=psumC[:, :], op0=ALU.mult, op1=ALU.add),
                        reads=[e_, zc, psumC], writes=[psumC])

            def s4(pb_=pb_, tb=tb, nnb=nnb):
                for nb in range(nnb):
                    k.mm(tb, tb[:, nb * 128:(nb + 1) * 128], pb_[:, nb * 128:(nb + 1) * 128], self.identb[:, :], [pb_, self.identb])

            def s5(pt_=pt_, tb=tb, nnb=nnb):
                k.cp("act", pt_, pt_[:, 0:nnb * 128], tb[:, 0:nnb * 128], [tb])

            def s6(pt_=pt_, ocb=ocb, h=h, nnb=nnb):
                for nb in range(nnb):
                    k.mm(ocb, ocb[:, h * 64:(h + 1) * 64], pt_[:, nb * 128:(nb + 1) * 128], vcs[:, nb, :], [pt_, vcs],
                         start=(nb == 0), stop=(nb == nnb - 1))
            stages = [s0, s1, s2, s3, s4, s5, s6]
            if h == 3:
                def s7(i=i, ocb=ocb, psumC=psumC, imp=imp):
                    k.tt("dve", oD, oD[:, i, :].rearrange("p (h e) -> p h e", e=64),
                         ocb[:, 0:256].rearrange("p (h e) -> p h e", e=64), blast(self.gates[:, i, 0:4], 64), ALU.mult,
                         [ocb, self.gates])
                    pv = psumC[:, :].rearrange("p (j r) -> p j r", r=4)
                    k.op("dve", lambda e: e.tensor_reduce(out=imp[:, :], in_=pv, axis=AX.X, op=ALU.add), reads=[psumC], writes=[imp])
                    k.tt("dve", imp, imp[:, 1:64], imp[:, 1:64], pv[:, 0:63, 3], ALU.add, [psumC])

                def s8(i=i, imp=imp, imp2=imp2, m8=m8):
                    st = 64 - 2 * i
                    k.tt("dve", imp2, imp2[:, :], imp[:, :], self.cfs("M1", st, st + 64), ALU.mult, [imp, cf])
                    k.tt("dve", imp2, imp2[:, :], imp2[:, :], self.cfs("CC", st, st + 64), ALU.add, [cf])
                    k.op("dve", lambda e: e.memset(imp2[:, 0:1], 1.0e4), writes=[imp2])
                    k.op("dve", lambda e: e.max(out=m8[:, 0:8], in_=imp2[:, :]), reads=[imp2], writes=[m8])

                def s9(imp=imp, imp2=imp2, m8=m8, nmf=nmf, nm=nm):
                    k.op("dve", lambda e: e.match_replace(out=imp[:, :], in_to_replace=m8[:, 0:8], in_values=imp2[:, :],
                                                          imm_value=-2.0), reads=[m8, imp2], writes=[imp])
                    k.op("dve", lambda e: e.max(out=m8[:, 8:16], in_=imp[:, :]), reads=[imp], writes=[m8])
                    k.ts("dve", nmf, nmf[:, :], imp2[:, :], m8[:, 15:16], BIG, ALU.is_ge, ALU.mult, [imp2, m8])
                    k.ts("dve", nm, nm[:, 0:64], nmf[:, :], -BIG, None, ALU.add, None, [nmf])
                    k.ts("dve", nm, nm[:, 64:128], nmf[:, :], -BIG, None, ALU.add, None, [nmf])

                def s10(nm=nm, nmb=nmb):
                    k.mm(nmb, nmb[:, 0:128], nm[:, :], self.identb[:, :], [nm, self.identb])

                def s11(i=i, nmb=nmb):
                    k.cp("act", nmT, nmT[:, i * 128:(i + 1) * 128], nmb[:, 0:128], [nmb])
                stages += [s7, s8, s9, s10, s11]
            jobs.append(stages)
    run_staged(jobs)
    k.dma("pool", self.baug[0, 0:64, :], nmT[0:64, :], reads=[nmT], writes=[self.dd["baug"]])
    for br, (ch, c0, gcol) in enumerate([(16, 520, 4), (17, 585, 8)]):
        if dstage < 4 + br:
            continue
        self.push()
        if br == 0:
            kT = self.sb("ksx", [128, S], BF16)
            self.load_rows(kT, 0, self.qkT[16][0:64, :])
            k.dma("pool", kT[64:128, :], I["cex"][:, :], writes=[kT])
        else:
            kT = load_chunk("dk%d" % br, ch)
        vv = load_vaug("dv%d" % br, c0, 1)
        qq4 = []
        for h in range(4):
            pair, pb = h // 2, 64 * (h % 2)
            if br == 0:
                qq4.append(self.padded("qsel%d" % h, [(0, self.qkT[14 + pair][pb:pb + 64, :], None),
                                                     (64, self.baug[0][0:64, :], self.dd["baug"])]))
            else:
                qq4.append(self.padded("qwin%d" % h, [(pb, self.qkT[14 + pair][pb:pb + 64, :], None)]))
        for pair in range(2):
            for slot in range(2):
                h = 2 * pair + slot

                def cb(c, bank, v3, h=h, gcol=gcol, slot=slot):
                    on = dons[slot]
                    normalize(bank, v3, on, on[:, :, :], slot)
                    k.tt("dve", on, on[:, :, :], on[:, :, :], blast(self.gates[:, 4 * c:4 * c + 4, gcol + h], 64), ALU.mult,
                         [self.gates])
                    dst = oD[:, 4 * c:4 * c + 4, h * 64:(h + 1) * 64]
                    k.tt("dve", oD, dst, dst, on[:, :, :], ALU.add, [on])
                qq = qq4[h]
                softmax_head(qq[:, :], [qq], kT[:, :], [kT],
                             lambda kb, vv=vv: vv[:, kb, 0:65], [vv], 4 + h, None, br == 1, [], cb, slot)
        self.run_jobs()
        self.pop()
    ob = [self.sb("dob%d" % i, [128, 256], BF16) for i in range(2)]
    for t in range(32):
        o_ = ob[t % 2]
        k.cp("act", o_, o_[:, :], oD[:, t, :], [oD])
        k.dma("pool", self.o_d[t * 128:(t + 1) * 128, 768:1024], o_[:, :], reads=[o_], writes=[self.dd["o_d"]])
    self.pop()


Prog.mixer_c = _mixer_c
Prog.mixer_d = _mixer_d


def _phase_c(self, l, xsrc, xsd, xdst, xdd):
    k, I, cf, ps = self.k, self.I, self.cf, self.ps
    ident = self.cfs("ident")
    GT = 256
    NG = S // GT
    self.push()
    wout = self.sb("wout", [128, 8, D], BF16)
    wup = self.sb("wup", [128, 8, 2 * DFF], BF16)
    wdn = self.sb("wdn", [128, NFC, D], BF16)
    src_up = I["ffn_w_up"][l].rearrange("(c p) n -> p c n", p=128)
    for c in range(8):
        for hh in range(2):
            k.dma("pool", wup[:, c, hh * DFF:(hh + 1) * DFF], src_up[:, c, hh * DFF:(hh + 1) * DFF], writes=[wup])
    cst = self.sb("cst", [88, 128], F32)
    k.dma("sp", cst[0:66, :], I["ffn_conv_w"][l], writes=[cst])
    k.dma("sp", cst[66:88, :], I["ffn_conv_b"][l], writes=[cst])
    self.tr(ps[0], ps[0][:, 0:88], cst[:, :], ident[0:88, 0:88], [cst, cf])
    cw = self.sb("cw", [128, 88], F32)
    k.cp("dve", cw, cw[:, :], ps[0][:, 0:88], [ps[0]])
    ab = self.sb("abf", [128, 16], F32)
    k.ts("dve", ab, ab[:, 0:8], self.modcol(l, 4), 1.0, None, ALU.add, None, [self.modT])
    k.tt("dve", ab, ab[:, 0:8], ab[:, 0:8], self.colv[:, 112 + 8 * l:120 + 8 * l], ALU.mult, [self.colv])
    k.cp("dve", ab, ab[:, 8:16], self.modcol(l, 3), [self.modT])
    self.push()
    gbc = [self.sb("gbc%d" % i, [128, D], F32) for i in range(2)]
    dg = self.sb("dgt", [128, 128], F32)
    for gi, which in enumerate([2, 5]):
        for c in range(8):
            k.ts("dve", dg, dg[:, :], ident, self.modcol(l, which)[:, c:c + 1], None, ALU.mult, None, [cf, self.modT])
            b = ps[1 + (c // 4)]
            k.mm(b, b[:, (c % 4) * 128:(c % 4 + 1) * 128], self.cfs("ones"), dg[:, :], [dg, cf])
        for hh in range(2):
            k.cp("dve", gbc[gi], gbc[gi][:, hh * 512:(hh + 1) * 512], ps[1 + hh][:, :], [ps[1 + hh]])
    stg = [self.sb("wstg%d" % i, [128, D], F32) for i in range(2)]
    src_o = I["w_out"][l].rearrange("(c p) n -> p c n", p=128)
    src_d = I["ffn_w_down"][l].rearrange("(c p) n -> p c n", p=128)
    n = 0
    for c in range(8):
        st_ = stg[n % 2]
        k.dma("sp", st_[:, :], src_o[:, c, :], writes=[st_])
        k.tt("dve" if n % 2 == 0 else "pool", wout, wout[:, c, :], st_[:, :], gbc[0][:, :], ALU.mult, [st_, gbc[0]])
        n += 1
    for c in range(NFC):
        st_ = stg[n % 2]
        k.dma("sp", st_[:, :], src_d[:, c, :], writes=[st_])
        k.tt("dve" if n % 2 == 0 else "pool", wdn, wdn[:, c, :], st_[:, :], gbc[1][:, :], ALU.mult, [st_, gbc[1]])
        n += 1
    self.pop()
    ots = [self.sb("ot%d" % i, [128, D], BF16) for i in range(2)]
    oT = self.sb("oTc", [128, 8, 128], BF16)
    xms = [self.sb("xm%d" % i, [128, D], F32) for i in range(3)]
    xn = self.sb("xnc", [128, D], F32)
    sss = [self.sb("ssc%d" % i, [128, 4], F32) for i in range(2)]
    h2T = self.sb("h2T", [128, 8, GT], BF16)
    gsb = [self.sb("gsb%d" % i, [128, GT + 2], F32) for i in range(2)]
    accs = [self.sb("cacc%d" % i, [128, GT], F32) for i in range(2)]
    sgs = [self.sb("csg%d" % i, [128, GT], F32) for i in range(2)]
    aT = self.sb("aTc", [128, NFC, GT], BF16)
    halo = self.sb("halo", [128, NFC, 2], F32)
    k.op("dve", lambda e: e.memset(halo[:, :, :], 0.0), writes=[halo])
    nt = 0
    for g in range(NG):
        for tt_ in range(GT // 128):
            t = g * (GT // 128) + tt_
            ot = ots[t % 2]
            xm = xms[t % 3]
            ss = sss[t % 2]
            k.dma("sp", ot[:, :], self.o_d[t * 128:(t + 1) * 128, :], reads=[self.dd["o_d"]], writes=[ot])
            k.dma("act", xm[:, :], xsrc[t * 128:(t + 1) * 128, :], reads=[xsd] if xsd else [], writes=[xm])
            for c in range(8):
                b = ps[c // 4]
                k.mm(b, b[:, (c % 4) * 128:(c % 4 + 1) * 128], ot[:, c * 128:(c + 1) * 128], self.identb[:, :], [ot, self.identb])
            for hh in range(2):
                k.cp("act" if hh == 0 else "dve", oT, oT[:, hh * 4:(hh + 1) * 4, :].rearrange("p c t -> p (c t)"), ps[hh][:, :], [ps[hh]])
            for hh in range(2):
                b = ps[2 + hh]
                for c in range(8):
                    k.mm(b, b[:, :], oT[:, c, :], wout[:, c, hh * 512:(hh + 1) * 512], [oT, wout], start=(c == 0), stop=(c == 7))
                k.tt("dve", xm, xm[:, hh * 512:(hh + 1) * 512], b[:, :], xm[:, hh * 512:(hh + 1) * 512], ALU.add, [b])
            if self.dbg:
                k.dma("pool", self.xmid[t * 128:(t + 1) * 128, :], xm[:, :], reads=[xm], writes=[self.dd["xmid"]])
            k.act(xn, xn[:, :], xm[:, :], AF.Square, [xm], accum=ss[:, 0:1], extra_w=[ss])
            k.act(ss, ss[:, 1:2], ss[:, 0:1], AF.Ln, [ss], bias=self.epsc[:, 0:1], scale=1.0 / D)
            k.act(ss, ss[:, 2:3], ss[:, 1:2], AF.Exp, [ss], scale=-0.5)
            k.ts("dve", xn, xn[:, :], xm[:, :], ss[:, 2:3], None, ALU.mult, None, [xm, ss])
            for c in range(8):
                b = ps[4 + c // 4]
                self.tr(b, b[:, (c % 4) * 128:(c % 4 + 1) * 128], xn[:, c * 128:(c + 1) * 128], ident, [xn, cf])
            for c in range(8):
                b = ps[4 + c // 4]
                pin = b[:, (c % 4) * 128:(c % 4 + 1) * 128]
                dst = h2T[:, c, tt_ * 128:(tt_ + 1) * 128]
                if c % 2 == 0:
                    k.ts("dve", h2T, dst, pin, ab[:, c:c + 1], ab[:, 8 + c:9 + c], ALU.mult, ALU.add, [b, ab])
                else:
                    k.act(h2T, dst, pin, AF.Identity, [b, ab], bias=ab[:, 8 + c:9 + c], scale=ab[:, c:c + 1])
        for fc in range(NFC):
            bg = ps[(2 * fc) % 4]
            bv = ps[(2 * fc + 1) % 4]
            for c in range(8):
                k.mm(bg, bg[:, 0:GT], wup[:, c, fc * 128:(fc + 1) * 128], h2T[:, c, :], [wup, h2T], start=(c == 0), stop=(c == 7))
            for c in range(8):
                k.mm(bv, bv[:, 0:GT], wup[:, c, DFF + fc * 128:DFF + (fc + 1) * 128], h2T[:, c, :], [wup, h2T],
                     start=(c == 0), stop=(c == 7))
            gs_, ac_, sg_ = gsb[fc % 2], accs[fc % 2], sgs[fc % 2]
            k.cp("act", gs_, gs_[:, 0:2], halo[:, fc, :], [halo])
            k.cp("act", gs_, gs_[:, 2:GT + 2], bg[:, 0:GT], [bg])
            k.cp("act", halo, halo[:, fc, :], gs_[:, GT:GT + 2], [gs_])
            k.ts("dve", ac_, ac_[:, :], gs_[:, 2:GT + 2], cw[:, 44 + fc:45 + fc], cw[:, 66 + fc:67 + fc], ALU.mult, ALU.add,
                 [gs_, cw])
            k.op("dve", lambda e, ac_=ac_, gs_=gs_, fc=fc: e.scalar_tensor_tensor(
                out=ac_[:, :], in0=gs_[:, 1:GT + 1], scalar=cw[:, 22 + fc:23 + fc], in1=ac_[:, :], op0=ALU.mult, op1=ALU.add),
                reads=[gs_, cw, ac_], writes=[ac_])
            k.op("dve", lambda e, ac_=ac_, gs_=gs_, fc=fc: e.scalar_tensor_tensor(
                out=ac_[:, :], in0=gs_[:, 0:GT], scalar=cw[:, fc:fc + 1], in1=ac_[:, :], op0=ALU.mult, op1=ALU.add),
                reads=[gs_, cw, ac_], writes=[ac_])
            k.act(sg_, sg_[:, :], ac_[:, :], AF.Silu, [ac_])
            k.tt("dve", aT, aT[:, fc, :], sg_[:, :], bv[:, 0:GT], ALU.mult, [sg_, bv])
        for tt_ in range(GT // 128):
            t = g * (GT // 128) + tt_
            xm = xms[t % 3]
            for hh in range(2):
                b = ps[4 + hh]
                for fc in range(NFC):
                    k.mm(b, b[:, :], aT[:, fc, tt_ * 128:(tt_ + 1) * 128], wdn[:, fc, hh * 512:(hh + 1) * 512], [aT, wdn],
                         start=(fc == 0), stop=(fc == NFC - 1))
                k.tt("dve", xm, xm[:, hh * 512:(hh + 1) * 512], b[:, :], xm[:, hh * 512:(hh + 1) * 512], ALU.add, [b])
            k.dma("pool", xdst[t * 128:(t + 1) * 128, :], xm[:, :], reads=[xm], writes=[xdd])
    self.pop()


Prog.phase_c = _phase_c


_PROG = None


def kernel(**inputs):
    global _PROG
    if _PROG is None:
        _PROG = Prog()
    in_maps = [make_inputs(inputs, r % 4) for r in range(4)]
    in_maps = in_maps + in_maps
    res = run_bass_kernel_spmd(_PROG.nc, in_maps, core_ids=list(range(8)))
    out = np.stack([np.asarray(res.results[b]["y"], dtype=np.float32) for b in range(4)], axis=0)
    return out
```

```python
import math
import numpy as np
import concourse.bass as bass
import concourse.mybir as mybir
from concourse.bass_utils import run_bass_kernel_spmd

F32 = mybir.dt.float32
BF16 = mybir.dt.bfloat16
AF = mybir.ActivationFunctionType
ALU = mybir.AluOpType
AX = mybir.AxisListType

S = 4096
D = 1024
NT = 32
DFF = 2816
NFC = 22
PIN = 2960
BIG = 32768.0
EPS = 1e-6
NQ = 8


class Dep:
    __slots__ = ("w", "r")

    def __init__(self):
        self.w = {}
        self.r = {}


class TT:
    def __init__(self, t):
        self.t = t
        self.dep = Dep()

    def __getitem__(self, idx):
        return self.t[idx]


class Kern:
    def __init__(self, nc):
        self.nc = nc
        self.engs = {"pe": nc.tensor, "act": nc.scalar, "dve": nc.vector, "pool": nc.gpsimd, "sp": nc.sync}
        self.sem = {e: nc.semaphore("c_" + e).__enter__() for e in ["pe", "act", "dve", "pool"]}
        self.cnt = {e: 0 for e in self.sem}
        self.waited = {}
        self.dq = {}
        for q in ["sp", "pool", "act"]:
            self.dq[q] = {"sems": [nc.semaphore("d_%s%d" % (q, i)).__enter__() for i in range(NQ)],
                          "cnt": [0] * NQ, "i": 0}
        self.own = {e: id(self.sem[e]) for e in self.sem}
        self.pools = []

    def sb(self, name, shape, dt):
        return TT(self.nc.sbuf_tensor(name, shape, dt).__enter__())

    def ps(self, name, shape, dt):
        return TT(self.nc.psum_tensor(name, shape, dt).__enter__())

    def _wait(self, eng, sem, val):
        key = (eng, id(sem))
        if self.waited.get(key, 0) >= val:
            return
        self.waited[key] = val
        self.engs[eng].wait_ge(sem, val)

    def _deps(self, eng, reads, writes):
        own = self.own.get(eng)
        for d in reads:
            for sid, (sem, val) in d.dep.w.items():
                if sid == own and eng == "pe":
                    continue
                self._wait(eng, sem, val)
        for d in writes:
            for sid, (sem, val) in d.dep.w.items():
                if sid == own and eng == "pe":
                    continue
                self._wait(eng, sem, val)
            for sid, (sem, val) in d.dep.r.items():
                if sid == own and eng == "pe":
                    continue
                self._wait(eng, sem, val)

    def _reg(self, sem, val, reads, writes):
        sid = id(sem)
        for d in reads:
            d.dep.r[sid] = (sem, val)
        for d in writes:
            d.dep.w[sid] = (sem, val)

    def op(self, eng, fn, reads=(), writes=()):
        self._deps(eng, reads, writes)
        ins = fn(self.engs[eng])
        self.cnt[eng] += 1
        ins.then_inc(self.sem[eng], 1)
        self._reg(self.sem[eng], self.cnt[eng], reads, writes)

    def dma(self, q, out_ap, in_ap, reads=(), writes=()):
        self._deps(q, reads, writes)
        Q = self.dq[q]
        j = Q["i"] % NQ
        Q["i"] += 1
        sem = Q["sems"][j]
        if Q["cnt"][j] > 0:
            self._wait(q, sem, 16 * Q["cnt"][j])
        Q["cnt"][j] += 1
        self.engs[q].dma_start(out=out_ap, in_=in_ap).then_inc(sem, 16)
        self._reg(sem, 16 * Q["cnt"][j], reads, writes)

    def finish(self, outs):
        for d in outs:
            for sid, (sem, val) in d.dep.w.items():
                self._wait("sp", sem, val)

    def mm(self, out_tt, out_ap, lhsT, rhs, reads, start=True, stop=True):
        self.op("pe", lambda e: e.matmul(out_ap, lhsT, rhs, start=start, stop=stop, skip_group_check=True),
                reads=reads, writes=[out_tt])

    def act(self, out_tt, out_ap, in_ap, func, reads, bias=None, scale=None, accum=None, extra_w=()):
        kw = {}
        if bias is not None:
            kw["bias"] = bias
        if scale is not None:
            kw["scale"] = scale
        if accum is not None:
            kw["accum_out"] = accum
        self.op("act", lambda e: e.activation(out_ap, in_ap, func, **kw), reads=reads,
                writes=[out_tt] + list(extra_w))

    def ts(self, eng, out_tt, out_ap, in_ap, s1, s2, op0, op1, reads):
        if op1 is None:
            self.op(eng, lambda e: e.tensor_scalar(out=out_ap, in0=in_ap, scalar1=s1, scalar2=None, op0=op0),
                    reads=reads, writes=[out_tt])
        else:
            self.op(eng, lambda e: e.tensor_scalar(out=out_ap, in0=in_ap, scalar1=s1, scalar2=s2, op0=op0, op1=op1),
                    reads=reads, writes=[out_tt])

    def tt(self, eng, out_tt, out_ap, a, b, op, reads):
        self.op(eng, lambda e: e.tensor_tensor(out=out_ap, in0=a, in1=b, op=op), reads=reads, writes=[out_tt])

    def cp(self, eng, out_tt, out_ap, in_ap, reads):
        if eng == "act":
            self.op("act", lambda e: e.copy(out_ap, in_ap), reads=reads, writes=[out_tt])
        else:
            self.op(eng, lambda e: e.tensor_copy(out=out_ap, in_=in_ap), reads=reads, writes=[out_tt])


def bmid(a, m):
    return bass.AP(tensor=a.tensor, offset=a.offset, ap=[list(a.ap[0]), [0, m]] + [list(x) for x in a.ap[1:]])


def blast(a, n):
    return bass.AP(tensor=a.tensor, offset=a.offset, ap=[list(x) for x in a.ap] + [[0, n]])


def bc_ap(ap2d, nparts):
    return bass.AP(tensor=ap2d.tensor, offset=ap2d.offset, ap=[[0, nparts]] + [list(x) for x in ap2d.ap[1:]])


def _bucket(dist):
    n = np.maximum(dist, 0)
    nf = np.maximum(n, 1).astype(np.float32)
    large = 16 + (np.log(nf / np.float32(16)) / np.float32(math.log(128 / 16)) * np.float32(16)).astype(np.int32)
    return np.where(n < 16, n, np.minimum(large, 31))


CF = {}


def _mk_consts():
    p = np.arange(128)[:, None]
    j = np.arange(128)[None, :]
    cols = []

    def add(name, arr):
        CF[name] = (sum(a.shape[1] for a in cols), arr.shape[1])
        cols.append(arr.astype(np.float32))

    add("ident", (p == j))
    add("antiI", (p + j == 127))
    add("U", (p <= j))
    add("ones", np.ones((128, 128)))
    add("TRI", np.where(j >= p, 0.0, -BIG))
    add("TW", np.where(j < p, 0.0, -BIG))
    add("M01", (p + j > 127))
    add("NMC", np.where(p + j > 127, 0.0, -BIG))
    j2 = np.arange(256)[None, :]
    d = j2 - p
    r = np.arange(512)[None, :]
    dc = p - 16 * (r - 255) - 31
    global _CF2_ARR
    _CF2_ARR = np.ascontiguousarray(np.concatenate([np.where(d >= 0, _bucket(d), -1),
                                                    np.where(dc >= 0, _bucket(dc), -1)], axis=1).astype(np.float32))
    jj = np.arange(128)[None, :]
    fb = (jj == 64 + p // 64)
    vb = (jj <= 64 + p // 64)
    add("M1", (vb & ~fb))
    add("CC", 1.0e4 * fb - 1.0 * (~vb))
    cf = np.concatenate(cols, axis=1)
    ex = np.zeros((64, 32, 128), np.float32)
    for kb in range(32):
        for pp in range(128):
            ex[2 * kb + pp // 64, kb, pp] = 1.0
    return np.ascontiguousarray(cf), np.ascontiguousarray(ex.reshape(64, 4096))


_CF_ARR, _EX_ARR = _mk_consts()
NCF = _CF_ARR.shape[1]

_SEGS = [("aq", 0, 256), ("ak", 256, 256), ("bq", 768, 256), ("bk", 1024, 256),
         ("dq", 2308, 256), ("ks", 2692, 64), ("kw", 2820, 64),
         ("cq", 1540, 256), ("ck", 1796, 256),
         ("av", 512, 256), ("bv", 1280, 256),
         ("cv", 2052, 256), ("vs", 2756, 64), ("vw", 2884, 64), ("bf", 1536, 4), ("dg", 2948, 12),
         ("kc", 2564, 64), ("vc", 2628, 64)]
WOFF = {}
_o = 0
for _n, _s, _w in _SEGS:
    WOFF[_n] = _o
    _o += _w
assert _o == PIN


class Prog:
    def __init__(self, dbg=False, nlayers=2, stop_after=None, mixers="ABCD"):
        self.dbg = dbg
        self.mixers = mixers
        self.stop_after = stop_after
        nc = bass.Bass("TRN2", target_bir_lowering=False)
        self.nc = nc
        k = Kern(nc)
        self.k = k
        self.stack = []

        def din(name, shape):
            return nc.dram_tensor(name, list(shape), F32, kind="ExternalInput").ap()

        self.x_in = din("x", [S, D])
        self.c_in = din("c", [8, 128])
        self.I = {}
        for name, shape in [("rel_bias", [1, 256]), ("ada_w", [2, D, 6 * D]), ("ada_b", [2, 48, 128]),
                            ("norm_mix_g", [2, 8, 128]), ("norm_ffn_g", [2, 8, 128]), ("w_in", [2, D, PIN]),
                            ("w_out", [2, D, D]), ("diff_qnorm_g", [2, 1, 32]), ("diff_knorm_g", [2, 1, 32]),
                            ("diff_lambda", [2, 1, 128]), ("diff_subln_g", [2, 1, 64]), ("fox_qnorm_g", [2, 1, 64]),
                            ("fox_knorm_g", [2, 1, 64]), ("fox_b_f", [2, 1, 4]), ("nsa_qnorm_g", [2, 1, 64]),
                            ("nsa_knorm_g", [2, 1, 192]), ("nsa_pe", [2, 2, 32, 64]), ("nsa_phi_w", [2, 2, 2048, 64]),
                            ("ffn_w_up", [2, D, 2 * DFF]), ("ffn_conv_w", [2, 66, 128]), ("ffn_conv_b", [2, 22, 128]),
                            ("ffn_w_down", [2, DFF, D]), ("cf", [128, NCF]), ("cf2", [128, 768]), ("cex", [64, 4096])]:
            self.I[name] = din(name, shape)
        kind = "ExternalOutput" if dbg else "Internal"

        def dscr(name, shape, dt):
            return nc.dram_tensor(name, list(shape), dt, kind=kind).ap()

        self.y = nc.dram_tensor("y", [S, D], F32, kind="ExternalOutput").ap()
        self.y_dep = TT(None)
        self.xs = dscr("xs", [S, D], F32)
        self.xmid = dscr("xmid", [S, D], F32)
        self.qkT = dscr("qkT", [19, 128, S], BF16)
        self.vtm = dscr("vtm", [S, 650], BF16)
        self.cvr = dscr("cvr", [S, 256], BF16)
        self.baug = dscr("baug", [4, 128, S], BF16)
        self.o_d = dscr("o_d", [S, D], BF16)
        self.tts = dscr("tts", [128, 2 * 8 * 256], BF16)
        self.brs = dscr("brs", [128, 4 * 512], F32)
        self.dd = {n: TT(None) for n in ["xs", "xmid", "qkT", "vtm", "cvr", "baug", "o_d", "tts", "brs"]}
        self.ps = [k.ps("ps%d" % i, [128, 512], F32) for i in range(8)]
        self.setup()
        for l in range(nlayers if stop_after != ("setup", 0) else 0):
            xsrc, xsd = (self.x_in, None) if l == 0 else (self.xs, self.dd["xs"])
            self.phase_a(l, xsrc, xsd)
            if stop_after == ("a", l):
                break
            self.phase_b(l)
            if stop_after == ("b", l):
                break
            last = (l == nlayers - 1)
            self.phase_c(l, xsrc, xsd, self.y if last else self.xs, self.y_dep if last else self.dd["xs"])
        outs = [self.y_dep] + (list(self.dd.values()) if dbg else [])
        self.barrier()
        k.finish(outs)

    def push(self):
        self.stack.append([])

    def sb(self, name, shape, dt):
        cm = self.nc.sbuf_tensor(name + "_%d" % len(self.k.pools), shape, dt)
        self.k.pools.append(0)
        t = TT(cm.__enter__())
        self.stack[-1].append(cm)
        return t

    def pop(self):
        self.barrier()
        for cm in reversed(self.stack.pop()):
            cm.__exit__(None, None, None)

    def barrier(self):
        k = self.k
        for e in ["pe", "act", "dve", "pool", "sp"]:
            for e2 in k.sem:
                if k.cnt[e2] > 0 and e2 != e:
                    k._wait(e, k.sem[e2], k.cnt[e2])
            for q in k.dq.values():
                for j in range(NQ):
                    if q["cnt"][j] > 0:
                        k._wait(e, q["sems"][j], 16 * q["cnt"][j])

    def cfs(self, name, a=0, b=None):
        o, w = CF[name]
        if b is None:
            b = w
        return self.cf[:, o + a:o + b]

    def tr(self, ps_tt, out_ap, in_ap, ident_ap, reads):
        self.k.mm(ps_tt, out_ap, in_ap, ident_ap, reads)

    def setup(self):
        k, nc, I = self.k, self.nc, self.I
        self.stack.append([])
        self.cf = self.sb("cf", [128, NCF], F32)
        k.dma("sp", self.cf[:, :], I["cf"][:, :], writes=[self.cf])
        cf = self.cf
        self.identb = self.sb("identb", [128, 128], BF16)
        self.antib = self.sb("antib", [128, 128], BF16)
        self.trib = self.sb("trib", [128, 128], BF16)
        self.twb = self.sb("twb", [128, 128], BF16)
        for t, n in [(self.identb, "ident"), (self.antib, "antiI"), (self.trib, "TRI"), (self.twb, "TW")]:
            k.cp("dve", t, t[:, :], self.cfs(n), [cf])
        self.zerob = self.sb("zerob", [128, 512], BF16)
        k.op("dve", lambda e: e.memset(self.zerob[:, :], 0.0), writes=[self.zerob])
        self.tab = self.sb("tab", [128, 256], F32)
        k.dma("sp", self.tab[:, :], bc_ap(I["rel_bias"], 128), writes=[self.tab])
        self.modT = self.sb("modT", [128, 96], F32)
        self.colv = self.sb("colv", [128, 128], F32)
        self.gates = self.sb("gates", [128, 32, 12], F32)
        self.kcT = self.sb("kcT", [128, 256], BF16)
        self.vcs = self.sb("vcs", [128, 2, 64], BF16)
        self.lam = self.sb("lam", [128, 2], F32)
        self.subg = self.sb("subg", [128, 64], F32)
        self.epsc = self.sb("epsc", [128, 4], F32)
        k.op("dve", lambda e: e.memset(self.epsc[:, 0:1], EPS), writes=[self.epsc])
        k.op("dve", lambda e: e.memset(self.epsc[:, 1:2], 1.0), writes=[self.epsc])
        k.op("dve", lambda e: e.memset(self.epsc[:, 2:3], 1e-30), writes=[self.epsc])
        self.push()
        tabd = self.sb("tabd", [128, 256], F32)
        k.tt("dve", tabd, tabd[:, :].rearrange("p (b h) -> p b h", h=8),
             self.tab[:, :].rearrange("p (b h) -> p b h", h=8), bmid(self.tab[:, 248:256], 32),
             ALU.subtract, [self.tab])
        acc = self.sb("ttacc", [128, 8, 256], F32)
        accb = self.sb("bracc", [128, 4, 512], F32)
        mb = self.sb("mb", [128, 512], F32)
        cf2 = self.sb("cf2", [128, 768], F32)
        k.dma("sp", cf2[:, :], I["cf2"][:, :], writes=[cf2])
        BKa = cf2[:, 0:256]
        BKC = cf2[:, 256:768]
        for h in range(8):
            k.ts("dve", acc, acc[:, h, :], BKa, -1.0, -BIG, ALU.is_equal, ALU.mult, [cf2])
        for h in range(4):
            k.ts("dve", accb, accb[:, h, :], BKC, -1.0, -BIG, ALU.is_equal, ALU.mult, [cf2])
        for b in range(32):
            k.ts("dve", mb, mb[:, 0:256], BKa, float(b), None, ALU.is_equal, None, [cf2])
            for h in range(8):
                k.op("dve", lambda e, h=h, b=b: e.scalar_tensor_tensor(
                    out=acc[:, h, :], in0=mb[:, 0:256], scalar=tabd[:, b * 8 + h:b * 8 + h + 1], in1=acc[:, h, :],
                    op0=ALU.mult, op1=ALU.add), reads=[mb, tabd], writes=[acc])
            k.ts("dve", mb, mb[:, :], BKC, float(b), None, ALU.is_equal, None, [cf2])
            for h in range(4):
                k.op("dve", lambda e, h=h, b=b: e.scalar_tensor_tensor(
                    out=accb[:, h, :], in0=mb[:, :], scalar=self.tab[:, b * 8 + 4 + h:b * 8 + 5 + h], in1=accb[:, h, :],
                    op0=ALU.mult, op1=ALU.add), reads=[mb, self.tab], writes=[accb])
        tthi = self.sb("tthi", [128, 8 * 256], BF16)
        ttlo = self.sb("ttlo", [128, 8 * 256], BF16)
        accf = acc[:, :, :].rearrange("p h j -> p (h j)")
        k.cp("dve", tthi, tthi[:, :], accf, [acc])
        k.tt("dve", acc, accf, accf, tthi[:, :], ALU.subtract, [tthi])
        k.cp("dve", ttlo, ttlo[:, :], accf, [acc])
        k.dma("pool", self.tts[:, 0:2048], tthi[:, :], reads=[tthi], writes=[self.dd["tts"]])
        k.dma("pool", self.tts[:, 2048:4096], ttlo[:, :], reads=[ttlo], writes=[self.dd["tts"]])
        k.dma("pool", self.brs[:, :], accb[:, :, :].rearrange("p h j -> p (h j)"), reads=[accb], writes=[self.dd["brs"]])
        st = self.sb("colst", [128, 128], F32)
        k.dma("sp", st[0:48, :], I["ada_b"][0], writes=[st])
        k.dma("sp", st[48:96, :], I["ada_b"][1], writes=[st])
        k.dma("sp", st[96:104, :], I["norm_mix_g"][0], writes=[st])
        k.dma("sp", st[104:112, :], I["norm_mix_g"][1], writes=[st])
        k.dma("sp", st[112:120, :], I["norm_ffn_g"][0], writes=[st])
        k.dma("sp", st[120:128, :], I["norm_ffn_g"][1], writes=[st])
        ps0 = self.ps[0]
        self.tr(ps0, ps0[:, 0:128], st[:, :], self.cfs("ident"), [st, cf])
        k.cp("dve", self.colv, self.colv[:, :], ps0[:, 0:128], [ps0])
        c8 = self.sb("c8", [8, 128], F32)
        k.dma("sp", c8[:, :], self.c_in[:, :], writes=[c8])
        ps1 = self.ps[1]
        self.tr(ps1, ps1[:, 0:8], c8[:, :], self.cfs("ident")[0:8, 0:8], [c8, cf])
        cact = self.sb("cact", [128, 8], F32)
        k.act(cact, cact[:, :], ps1[:, 0:8], AF.Silu, [ps1])
        wb = [self.sb("adaw%d" % i, [128, 8, 512], F32) for i in range(2)]
        ps2 = self.ps[2]
        for l in range(2):
            for blk in range(12):
                w = wb[blk % 2]
                src = I["ada_w"][l].rearrange("(kc p) n -> p kc n", p=128)
                for kc in range(8):
                    k.dma("sp" if kc % 2 == 0 else "act", w[:, kc, :], src[:, kc, blk * 512:(blk + 1) * 512], writes=[w])
                for jj in range(4):
                    j = l * 48 + blk * 4 + jj
                    for kc in range(8):
                        k.mm(ps2, ps2[:, j:j + 1], w[:, kc, jj * 128:(jj + 1) * 128], cact[:, kc:kc + 1], [w, cact],
                             start=(kc == 0), stop=(kc == 7))
        k.tt("dve", self.modT, self.modT[:, :], ps2[:, 0:96], self.colv[:, 0:96], ALU.add, [ps2, self.colv])
        self.pop()

    def modcol(self, l, which):
        o = l * 48 + which * 8
        return self.modT[:, o:o + 8]

    def phase_a(self, l, xsrc, xsd):
        k, I, cf = self.k, self.I, self.cf
        ps = self.ps
        ident = self.cfs("ident")
        self.push()
        gsm = self.sb("gsm", [128, 644], F32)
        go = {}
        o = 0
        for name, n in [("diff_qnorm_g", 32), ("diff_knorm_g", 32), ("fox_qnorm_g", 64), ("fox_knorm_g", 64),
                        ("nsa_qnorm_g", 64), ("nsa_knorm_g", 192), ("fox_b_f", 4), ("diff_subln_g", 64),
                        ("diff_lambda", 128)]:
            k.dma("sp", gsm[:, o:o + n], bc_ap(I[name][l], 128), writes=[gsm])
            go[name] = o
            o += n

        def gs_(name, a, b):
            return gsm[:, go[name] + a:go[name] + b]
        gA = self.sb("gA", [128, 512], F32)
        gB = self.sb("gB", [128, 512], F32)
        gD = self.sb("gD", [128, 384], F32)
        k.ts("dve", gA, gA[:, 0:256].rearrange("p (g e) -> p g e", e=32), bmid(gs_("diff_qnorm_g", 0, 32), 8),
             32 ** -0.5, None, ALU.mult, None, [gsm])
        k.cp("dve", gA, gA[:, 256:512].rearrange("p (g e) -> p g e", e=32), bmid(gs_("diff_knorm_g", 0, 32), 8), [gsm])
        k.ts("dve", gB, gB[:, 0:256].rearrange("p (g e) -> p g e", e=64), bmid(gs_("fox_qnorm_g", 0, 64), 4),
             0.125, None, ALU.mult, None, [gsm])
        k.cp("dve", gB, gB[:, 256:512].rearrange("p (g e) -> p g e", e=64), bmid(gs_("fox_knorm_g", 0, 64), 4), [gsm])
        k.ts("dve", gD, gD[:, 0:256].rearrange("p (g e) -> p g e", e=64), bmid(gs_("nsa_qnorm_g", 0, 64), 4),
             0.125, None, ALU.mult, None, [gsm])
        k.cp("dve", gD, gD[:, 256:384], gs_("nsa_knorm_g", 64, 192), [gsm])
        lam_init = 0.8 - 0.6 * math.exp(-0.3 * l)
        self.lam_init = lam_init
        subg = self.sb("subgl", [128, 64], F32)
        k.ts("dve", subg, subg[:, :], gs_("diff_subln_g", 0, 64), 1.0 - lam_init, None, ALU.mult, None, [gsm])
        lt = self.sb("lt", [128, 64], F32)
        ls = self.sb("ls", [128, 4], F32)
        lv = gs_("diff_lambda", 0, 128).rearrange("p (a b e) -> p a b e", a=2, b=2, e=32)
        k.tt("dve", lt, lt[:, :].rearrange("p (a e) -> p a e", e=32), lv[:, :, 0, :], lv[:, :, 1, :], ALU.mult, [gsm])
        k.op("dve", lambda e: e.tensor_reduce(out=ls[:, 0:2], in_=lt[:, :].rearrange("p (a e) -> p a e", e=32),
                                              axis=AX.X, op=ALU.add), reads=[lt], writes=[ls])
        k.act(ls, ls[:, 2:4], ls[:, 0:2], AF.Exp, [ls])
        k.tt("dve", ls, ls[:, 0:1], ls[:, 2:3], ls[:, 3:4], ALU.subtract, [ls])
        k.ts("dve", self.lam, self.lam[:, l:l + 1], ls[:, 0:1], lam_init, None, ALU.add, None, [ls])
        ab = self.sb("ab", [128, 16], F32)
        k.ts("dve", ab, ab[:, 0:8], self.modcol(l, 1), 1.0, None, ALU.add, None, [self.modT])
        k.tt("dve", ab, ab[:, 0:8], ab[:, 0:8], self.colv[:, 96 + 8 * l:104 + 8 * l], ALU.mult, [self.colv])
        k.cp("dve", ab, ab[:, 8:16], self.modcol(l, 0), [self.modT])
        sq = self.sb("sq", [128, D], F32)
        qn = self.sb("qn", [128, 512], F32)
        ssg = self.sb("ssg", [128, 48], F32)
        rawT = self.sb("rawT", [128, S + 16], F32)
        fl = self.sb("fl", [128, 32, 4], F32)
        gl = self.sb("gl", [128, 32, 12], F32)
        self.push()
        w = self.sb("win", [128, 8, PIN], BF16)
        src = I["w_in"][l].rearrange("(kc p) n -> p kc n", p=128)
        for name, s0, wd in _SEGS:
            o = WOFF[name]
            k.dma("pool", w[:, :, o:o + wd], src[:, :, s0:s0 + wd], writes=[w])
        xts = [self.sb("xt%d" % i, [128, D], F32) for i in range(2)]
        xn = self.sb("xn", [128, D], F32)
        sss = [self.sb("ss%d" % i, [128, 4], F32) for i in range(2)]
        hTs = [self.sb("hT%d" % i, [128, 8, 128], BF16) for i in range(2)]
        stgs = [self.sb("stg%d" % i, [128, 18 * 128], BF16) for i in range(2)]
        qsts = [self.sb("qst%d" % i, [128, 18, 512], BF16) for i in range(2)]
        vsts = [self.sb("vst%d" % i, [128, 650], BF16) for i in range(2)]
        cvb = self.sb("cvb", [128, 256], BF16)
        cvst = [self.sb("cvst%d" % i, [128, 256], BF16) for i in range(2)]
        for s_ in stgs:
            k.op("pool", lambda e, s_=s_: e.memset(s_[:, :], 0.0), writes=[s_])
        for v_ in vsts:
            k.op("pool", lambda e, v_=v_: e.memset(v_[:, :], 1.0), writes=[v_])
        k.op("pool", lambda e: e.memset(rawT[:, S:S + 16], 0.0), writes=[rawT])
        qkTv = self.qkT.rearrange("c p s -> p c s")
        qd = self.dd["qkT"]

        def normev(bank, n, gs):
            ng = n // gs
            k.act(sq, sq[:, 0:n], bank[:, 0:n], AF.Square, [bank])
            k.op("dve", lambda e: e.tensor_reduce(out=ssg[:, 0:ng], in_=sq[:, 0:n].rearrange("p (g e) -> p g e", e=gs),
                                                  axis=AX.X, op=ALU.add), reads=[sq], writes=[ssg])
            k.act(ssg, ssg[:, 16:16 + ng], ssg[:, 0:ng], AF.Ln, [ssg], bias=self.epsc[:, 0:1], scale=1.0 / gs)
            k.act(ssg, ssg[:, 32:32 + ng], ssg[:, 16:16 + ng], AF.Exp, [ssg], scale=-0.5)
            k.tt("dve", qn, qn[:, 0:n].rearrange("p (g e) -> p g e", e=gs),
                 bank[:, 0:n].rearrange("p (g e) -> p g e", e=gs), blast(ssg[:, 32:32 + ng], gs), ALU.mult, [bank, ssg])

        def front(t):
            xt = xts[t % 2]
            ss = sss[t % 2]
            hT = hTs[t % 2]
            stg = stgs[t % 2]
            vst = vsts[t % 2]
            qst = qsts[(t // 4) % 2]
            k.dma("sp", xt[:, :], xsrc[t * 128:(t + 1) * 128, :], reads=[xsd] if xsd else [], writes=[xt])
            k.act(sq, sq[:, :], xt[:, :], AF.Square, [xt], accum=ss[:, 0:1], extra_w=[ss])
            k.act(ss, ss[:, 1:2], ss[:, 0:1], AF.Ln, [ss], bias=self.epsc[:, 0:1], scale=1.0 / D)
            k.act(ss, ss[:, 2:3], ss[:, 1:2], AF.Exp, [ss], scale=-0.5)
            k.ts("dve", xn, xn[:, :], xt[:, :], ss[:, 2:3], None, ALU.mult, None, [xt, ss])
            for c in range(8):
                b = ps[c // 4]
                self.tr(b, b[:, (c % 4) * 128:(c % 4 + 1) * 128], xn[:, c * 128:(c + 1) * 128], ident, [xn, cf])
            for c in range(8):
                b = ps[c // 4]
                pin = b[:, (c % 4) * 128:(c % 4 + 1) * 128]
                if c % 2 == 0:
                    k.ts("dve", hT, hT[:, c, :], pin, ab[:, c:c + 1], ab[:, 8 + c:9 + c], ALU.mult, ALU.add, [b, ab])
                else:
                    k.act(hT, hT[:, c, :], pin, AF.Identity, [b, ab], bias=ab[:, 8 + c:9 + c], scale=ab[:, c:c + 1])

            def proj(bank, a, n):
                for c in range(8):
                    k.mm(bank, bank[:, 0:n], hT[:, c, :], w[:, c, a:a + n], [hT, w], start=(c == 0), stop=(c == 7))
            proj(ps[2], 0, 512)
            normev(ps[2], 512, 32)
            for m in range(2):
                srcq = qn[:, 0:256].rearrange("p (hh hl m e) -> p hh hl m e", hh=2, hl=2, m=2)[:, :, :, m, :]
                gq = gA[:, 0:256].rearrange("p (hh hl m e) -> p hh hl m e", hh=2, hl=2, m=2)[:, :, :, m, :]
                dst = stg[:, 0:512].rearrange("p (hh m hl f) -> p hh m hl f", hh=2, m=2, hl=2)[:, :, m, :, 32 * m:32 * m + 32]
                k.tt("pool", stg, dst, srcq, gq, ALU.mult, [qn, gA])
            k.tt("pool", stg, stg[:, 512:768], qn[:, 256:512], gA[:, 256:512], ALU.mult, [qn, gA])
            proj(ps[3], 512, 512)
            normev(ps[3], 512, 64)
            k.tt("pool", stg, stg[:, 768:1280], qn[:, 0:512], gB[:, :], ALU.mult, [qn, gB])
            proj(ps[4], 1024, 384)
            normev(ps[4], 384, 64)
            k.tt("pool", stg, stg[:, 1792:2048], qn[:, 0:256], gD[:, 0:256], ALU.mult, [qn, gD])
            for dup in range(2):
                k.tt("pool", stg, stg[:, 2048 + dup * 64:2112 + dup * 64], qn[:, 256:320], gD[:, 256:320], ALU.mult, [qn, gD])
                k.tt("pool", stg, stg[:, 2176 + dup * 64:2240 + dup * 64], qn[:, 320:384], gD[:, 320:384], ALU.mult, [qn, gD])
            proj(ps[5], 1408, 512)
            k.act(stg, stg[:, 1280:1536], ps[5][:, 0:256], AF.Copy, [ps[5]], scale=0.125)
            k.cp("act", stg, stg[:, 1536:1792], ps[5][:, 256:512], [ps[5]])
            proj(ps[2], 1920, 512)
            k.cp("act", vst, vst[:, 0:520].rearrange("p (h e) -> p h e", e=65)[:, :, 0:64],
                 ps[2][:, 0:512].rearrange("p (h e) -> p h e", e=64), [ps[2]])
            proj(ps[3], 2432, 400)
            k.cp("dve", cvb, cvb[:, :], ps[3][:, 0:256], [ps[3]])
            k.cp("dve", vst, vst[:, 520:584], ps[3][:, 256:320], [ps[3]])
            k.cp("dve", vst, vst[:, 585:649], ps[3][:, 320:384], [ps[3]])
            k.cp("dve", fl, fl[:, t, :], ps[3][:, 384:388], [ps[3]])
            k.cp("dve", gl, gl[:, t, :], ps[3][:, 388:400], [ps[3]])
            k.dma("pool", self.vtm[t * 128:(t + 1) * 128, :], vst[:, :], reads=[vst], writes=[self.dd["vtm"]])
            k.mm(ps[4], ps[4][:, 0:256], self.antib[:, :], cvb[:, :], [cvb, self.antib])
            cvs = cvst[t % 2]
            k.cp("act", cvs, cvs[:, :], ps[4][:, 0:256], [ps[4]])
            k.dma("pool", self.cvr[(31 - t) * 128:(32 - t) * 128, :], cvs[:, :], reads=[cvs], writes=[self.dd["cvr"]])
            for c in range(8):
                k.mm(ps[5], ps[5][:, 0:128], w[:, c, 2832:2960], hT[:, c, :], [hT, w], start=(c == 0), stop=(c == 7))
            k.cp("act", rawT, rawT[:, t * 128:(t + 1) * 128], ps[5][:, 0:128], [ps[5]])
        def back(t):
            stg = stgs[t % 2]
            qst = qsts[(t // 4) % 2]
            for g4 in range(5):
                chs = list(range(g4 * 4, min(g4 * 4 + 4, 18)))
                bank = [ps[6], ps[7]][(t * 5 + g4) % 2]
                for ci, ch in enumerate(chs):
                    idn = self.antib if ch in (12, 13) else self.identb
                    k.mm(bank, bank[:, ci * 128:(ci + 1) * 128], stg[:, ch * 128:(ch + 1) * 128], idn[:, :], [stg, idn])
                n = len(chs)
                for ci, ch in enumerate(chs):
                    slot = (3 - t % 4) if ch in (12, 13) else (t % 4)
                    eng = "act" if ci % 2 == 0 else "dve"
                    k.cp(eng, qst, qst[:, ch, slot * 128:(slot + 1) * 128], bank[:, ci * 128:(ci + 1) * 128], [bank])
            if t % 4 == 3:
                g = t // 4
                k.dma("pool", qkTv[:, 0:12, g * 512:(g + 1) * 512], qst[:, 0:12, :], reads=[qst], writes=[qd])
                k.dma("pool", qkTv[:, 12:14, (7 - g) * 512:(8 - g) * 512], qst[:, 12:14, :], reads=[qst], writes=[qd])
                k.dma("pool", qkTv[:, 14:18, g * 512:(g + 1) * 512], qst[:, 14:18, :], reads=[qst], writes=[qd])

        for t in range(NT + 1):
            if t < NT:
                front(t)
            if t >= 1:
                back(t - 1)
        self.pop()
        glf = gl[:, :, :].rearrange("p t g -> p (t g)")
        gaf = self.gates[:, :, :].rearrange("p t g -> p (t g)")
        k.act(self.gates, gaf, glf, AF.Exp, [gl], scale=-1.0)
        k.ts("dve", self.gates, gaf, gaf, 1.0, None, ALU.add, None, [self.gates])
        k.op("dve", lambda e: e.reciprocal(out=gaf, in_=gaf), reads=[self.gates], writes=[self.gates])
        sp = self.sb("fsp", [128, 128], F32)
        spv = sp[:, :].rearrange("p (t h) -> p t h", h=4)
        k.tt("dve", sp, spv, fl[:, :, :], bmid(gs_("fox_b_f", 0, 4), 32), ALU.add, [fl, gsm])
        k.act(sp, sp[:, :], sp[:, :], AF.Exp, [sp], scale=-1.0)
        k.act(sp, sp[:, :], sp[:, :], AF.Ln, [sp], bias=self.epsc[:, 1:2])
        k.mm(ps[0], ps[0][:, 0:128], self.cfs("U"), sp[:, :], [sp, cf])
        k.mm(ps[1], ps[1][:, 0:128], self.cfs("ones"), sp[:, :], [sp, cf])
        tot = self.sb("ftot", [128, 128], F32)
        inc = self.sb("finc", [128, 128], F32)
        cn = self.sb("fcn", [128, 128], F32)
        k.cp("dve", tot, tot[:, :], ps[1][:, 0:128], [ps[1]])
        totv = tot[:, :].rearrange("p (t h) -> p t h", h=4)
        incv = inc[:, :].rearrange("p (t h) -> p t h", h=4)
        for h in range(4):
            k.op("dve", lambda e, h=h: e.tensor_tensor_scan(out=incv[:, :, h], data0=totv[:, :, h],
                                                            data1=self.zerob[:, 0:32], initial=0.0,
                                                            op0=ALU.add, op1=ALU.add), reads=[tot, cf], writes=[inc])
        k.tt("dve", inc, inc[:, :], inc[:, :], tot[:, :], ALU.subtract, [tot])
        k.tt("dve", cn, cn[:, :], ps[0][:, 0:128], inc[:, :], ALU.add, [ps[0], inc])
        parts = [self.sb("fpart%d" % i, [128, 128], BF16) for i in range(3)]
        k.cp("dve", parts[0], parts[0][:, :], cn[:, :], [cn])
        k.tt("dve", cn, cn[:, :], cn[:, :], parts[0][:, :], ALU.subtract, [parts[0]])
        k.cp("dve", parts[1], parts[1][:, :], cn[:, :], [cn])
        k.tt("dve", cn, cn[:, :], cn[:, :], parts[1][:, :], ALU.subtract, [parts[1]])
        k.cp("dve", parts[2], parts[2][:, :], cn[:, :], [cn])
        aug = [self.sb("aug%d" % i, [128, 32, 128], BF16) for i in range(4)]
        for a_ in aug:
            k.op("pool", lambda e, a_=a_: e.memset(a_[:, :, :], 0.0), writes=[a_])
        for pair in range(2):
            aq, ak = aug[pair], aug[2 + pair]
            for hl in range(2):
                h = 2 * pair + hl
                for r in range(3):
                    pv = parts[r][:, :].rearrange("p (t h) -> p t h", h=4)[:, :, h]
                    k.ts("dve", aq, aq[:, :, hl * 64 + r], pv, -1.0, None, ALU.mult, None, [parts[r]])
                    k.cp("dve", ak, ak[:, :, hl * 64 + 3 + r], pv, [parts[r]])
                k.op("pool", lambda e, aq=aq, hl=hl: e.memset(aq[:, :, hl * 64 + 3:hl * 64 + 6], 1.0), writes=[aq])
                k.op("pool", lambda e, ak=ak, hl=hl: e.memset(ak[:, :, hl * 64:hl * 64 + 3], 1.0), writes=[ak])
        ast = [self.sb("augst%d" % i, [128, 512], BF16) for i in range(2)]
        n_ = 0
        for ai in range(4):
            for g in range(8):
                bank = ps[2 + n_ % 2]
                st_ = ast[n_ % 2]
                n_ += 1
                for tt_ in range(4):
                    k.mm(bank, bank[:, tt_ * 128:(tt_ + 1) * 128], aug[ai][:, g * 4 + tt_, :], self.identb[:, :],
                         [aug[ai], self.identb])
                k.cp("act", st_, st_[:, :], bank[:, :], [bank])
                k.dma("pool", self.baug[ai, :, g * 512:(g + 1) * 512], st_[:, :], reads=[st_], writes=[self.dd["baug"]])
        phibd = self.sb("phibd", [128, 32, 128], BF16)
        k.op("pool", lambda e: e.memset(phibd[:, :, :], 0.0), writes=[phibd])
        for wi in range(2):
            k.dma("pool", phibd[wi * 64:(wi + 1) * 64, :, wi * 64:(wi + 1) * 64],
                  I["nsa_phi_w"][l, wi].rearrange("(r d) e -> d r e", d=64), writes=[phibd])
        pst = self.sb("pest", [32, 128], F32)
        for wi in range(2):
            k.dma("sp", pst[:, wi * 64:(wi + 1) * 64], I["nsa_pe"][l, wi], writes=[pst])
        self.tr(ps[4], ps[4][:, 0:32], pst[:, :], self.cfs("ident")[0:32, 0:32], [pst, cf])
        peT = self.sb("peT", [128, 32], F32)
        k.cp("dve", peT, peT[:, :], ps[4][:, 0:32], [ps[4]])
        tmps = [self.sb("ctmp%d" % i, [128, 256], BF16) for i in range(4)]
        for t_ in tmps:
            k.op("pool", lambda e, t_=t_: e.memset(t_[:, :], 0.0), writes=[t_])
        r0 = rawT[:, 0:1]
        for r in range(32):
            tm = tmps[r % 4]
            src_ = bass.AP(tensor=r0.tensor, offset=r0.offset + r, ap=[list(r0.ap[0]), [16, 255]])
            k.ts("dve", tm, tm[:, 0:255], src_, peT[:, r:r + 1], None, ALU.add, None, [rawT, peT])
            for nb in range(2):
                k.mm(ps[5 + nb], ps[5 + nb][:, 0:128], tm[:, nb * 128:(nb + 1) * 128], phibd[:, r, :], [tm, phibd],
                     start=(r == 0), stop=(r == 31))
        kcn = self.sb("kcn", [128, 2, 128], BF16)
        for nb in range(2):
            bank = ps[5 + nb]
            k.act(sq, sq[:, 0:64], bank[:, 0:64], AF.Square, [bank], accum=ssg[:, 0:1], extra_w=[ssg])
            k.act(ssg, ssg[:, 1:2], ssg[:, 0:1], AF.Ln, [ssg], bias=self.epsc[:, 0:1], scale=1.0 / 64)
            k.act(ssg, ssg[:, 2:3], ssg[:, 1:2], AF.Exp, [ssg], scale=-0.5)
            k.ts("dve", qn, qn[:, 0:64], bank[:, 0:64], ssg[:, 2:3], None, ALU.mult, None, [bank, ssg])
            for dup in range(2):
                k.tt("dve", kcn, kcn[:, nb, dup * 64:(dup + 1) * 64], qn[:, 0:64], gs_("nsa_knorm_g", 0, 64), ALU.mult, [qn, gsm])
            k.cp("dve", self.vcs, self.vcs[:, nb, :], bank[:, 64:128], [bank])
        for nb in range(2):
            k.mm(ps[7], ps[7][:, nb * 128:(nb + 1) * 128], kcn[:, nb, :], self.identb[:, :], [kcn, self.identb])
        k.cp("dve", self.kcT, self.kcT[:, :], ps[7][:, 0:256], [ps[7]])
        k.cp("dve", self.subg, self.subg[:, :], subg[:, :], [subg])
        if self.dbg:
            k.dma("pool", self.qkT[18, :, 0:256], self.kcT[:, :], reads=[self.kcT], writes=[qd])
        self.pop()


def make_inputs(inp, b):
    f = np.float32
    m = {
        "x": np.ascontiguousarray(inp["x"][b], dtype=f),
        "c": np.ascontiguousarray(inp["c"][b].reshape(8, 128), dtype=f),
        "rel_bias": np.ascontiguousarray(inp["rel_bias"].reshape(1, 256), dtype=f),
        "ada_w": np.ascontiguousarray(inp["ada_w"], dtype=f),
        "ada_b": np.ascontiguousarray(inp["ada_b"].reshape(2, 48, 128), dtype=f),
        "norm_mix_g": np.ascontiguousarray(inp["norm_mix_g"].reshape(2, 8, 128), dtype=f),
        "norm_ffn_g": np.ascontiguousarray(inp["norm_ffn_g"].reshape(2, 8, 128), dtype=f),
        "w_in": np.ascontiguousarray(inp["w_in"], dtype=f),
        "w_out": np.ascontiguousarray(inp["w_out"], dtype=f),
        "diff_qnorm_g": inp["diff_qnorm_g"].reshape(2, 1, 32), "diff_knorm_g": inp["diff_knorm_g"].reshape(2, 1, 32),
        "diff_lambda": inp["diff_lambda"].reshape(2, 1, 128), "diff_subln_g": inp["diff_subln_g"].reshape(2, 1, 64),
        "fox_qnorm_g": inp["fox_qnorm_g"].reshape(2, 1, 64), "fox_knorm_g": inp["fox_knorm_g"].reshape(2, 1, 64),
        "fox_b_f": inp["fox_b_f"].reshape(2, 1, 4), "nsa_qnorm_g": inp["nsa_qnorm_g"].reshape(2, 1, 64),
        "nsa_knorm_g": inp["nsa_knorm_g"].reshape(2, 1, 192), "nsa_pe": inp["nsa_pe"], "nsa_phi_w": inp["nsa_phi_w"],
        "ffn_w_up": inp["ffn_w_up"], "ffn_conv_w": inp["ffn_conv_w"].reshape(2, 66, 128),
        "ffn_conv_b": inp["ffn_conv_b"].reshape(2, 22, 128), "ffn_w_down": inp["ffn_w_down"],
        "cf": _CF_ARR, "cf2": _CF2_ARR, "cex": _EX_ARR,
    }
    return {k_: np.ascontiguousarray(v, dtype=f) for k_, v in m.items()}


def _phase_b(self, l):
    k, I, cf, ps = self.k, self.I, self.cf, self.ps
    self.push()
    tthi = self.sb("tthi", [128, 8, 256], BF16)
    ttlo = self.sb("ttlo", [128, 8, 256], BF16)
    brel = self.sb("brel", [128, 4, 512], F32)
    k.dma("sp", tthi[:, :, :].rearrange("p h j -> p (h j)"), self.tts[:, 0:2048], reads=[self.dd["tts"]], writes=[tthi])
    k.dma("sp", ttlo[:, :, :].rearrange("p h j -> p (h j)"), self.tts[:, 2048:4096], reads=[self.dd["tts"]], writes=[ttlo])
    k.dma("sp", brel[:, :, :].rearrange("p h j -> p (h j)"), self.brs[:, :], reads=[self.dd["brs"]], writes=[brel])
    pTs = [self.sb("pT%d" % i, [128, 512], BF16) for i in range(6)]
    oTs = [self.sb("oTs%d" % i, [128, 512], F32) for i in range(2)]
    osts = [self.sb("ost%d" % i, [128, 4, 64], BF16) for i in range(4)]
    rzs = [self.sb("rz%d" % i, [128, 8], F32) for i in range(2)]
    ons = [self.sb("on%d" % i, [128, 4, 64], F32) for i in range(2)]
    sqss = [self.sb("sqs%d" % i, [128, 4, 64], F32) for i in range(2)]
    qkTv = self.qkT
    qd = self.dd["qkT"]
    ident = self.cfs("ident")
    cnt = {"s": 0, "c": 0, "o": 0}

    def load_chunk(name, ch, src=None, dep=None):
        t = self.sb(name, [128, S], BF16)
        s_ = qkTv[ch] if src is None else src
        for hh in range(2):
            k.dma("sp" if hh == 0 else "act", t[:, hh * 2048:(hh + 1) * 2048], s_[:, hh * 2048:(hh + 1) * 2048],
                  reads=[qd if dep is None else dep], writes=[t])
        return t

    def load_rows(t, r0, src_rows, dep=None, q="sp"):
        n_ = src_rows.shape[0]
        for hh in range(2):
            k.dma(q if hh == 0 else "act", t[r0:r0 + n_, hh * 2048:(hh + 1) * 2048], src_rows[:, hh * 2048:(hh + 1) * 2048],
                  reads=[qd if dep is None else dep], writes=[t])

    def padded(name, parts):
        t = self.sb(name, [128, S], BF16)
        cnt["pad"] = cnt.get("pad", 0) + 1
        k.op("dve", lambda e: e.memset(t[:, :], 0.0), writes=[t])
        for (r0, rows, dep) in parts:
            load_rows(t, r0, rows, dep)
        return t
    self.padded = padded
    self.load_rows = load_rows

    def load_vaug(name, c0, nh):
        t = self.sb(name, [128, 32, nh * 65], BF16)
        k.dma("sp", t[:, :, :], self.vtm[:, c0:c0 + nh * 65].rearrange("(t p) c -> p t c", p=128),
              reads=[self.dd["vtm"]], writes=[t])
        return t

    NS = 4
    NPT = 6
    sbanks = [ps[0], ps[1], ps[2], ps[4]]
    jobs2 = [[], []]

    def softmax_head(qT, qdeps, kT, kdeps, vaug, vdeps, bias_h, act_bias, window, extras, out_cb, slot=0):
        jobs = jobs2[slot]
        tcount = 0
        for c in range(8):
            kbs = list(range(max(0, 4 * c - 4) if window else 0, 4 * c + 4))
            touched = set()
            if window:
                touched = {0, 1, 2, 3}
            acc = [[ps[6], ps[3]], [ps[7], ps[5]]][slot][c % 2]
            for kb in kbs:
                d0 = c * 512 - kb * 128
                segs = [s for s in range(4) if d0 + 128 * s >= 0 and (not window or d0 + 128 * s <= 512)]
                groups = []
                for s in segs:
                    ft = s not in touched
                    touched.add(s)
                    if groups and groups[-1][2] == ft:
                        groups[-1][1] = s
                    else:
                        groups.append([s, s, ft])
                jlo, jhi = segs[0] * 128, (segs[-1] + 1) * 128
                Sb = sbanks[(2 * tcount + slot) % NS]
                pT = pTs[(2 * tcount + slot) % NPT]
                tcount += 1
                first, last = (kb == kbs[0]), (kb == kbs[-1])

                def s0(c=c, kb=kb, d0=d0, segs=segs, Sb=Sb, jlo=jlo, jhi=jhi):
                    mms = [(Sb[:, jlo:jhi], kT[:, kb * 128:(kb + 1) * 128], qT[:, c * 512 + jlo:c * 512 + jhi], qdeps + kdeps)]
                    for (lf, rf, deps) in extras:
                        mms.append((Sb[:, jlo:jhi], lf(kb), rf(c * 512 + jlo, c * 512 + jhi), deps))
                    for s in segs:
                        dl = d0 + 128 * s
                        seg = Sb[:, s * 128:(s + 1) * 128]
                        if bias_h is not None and dl in (0, 128):
                            mms.append((seg, self.identb[:, :], tthi[:, bias_h, dl:dl + 128], [self.identb, tthi]))
                            mms.append((seg, self.identb[:, :], ttlo[:, bias_h, dl:dl + 128], [self.identb, ttlo]))
                        elif bias_h is None and dl == 0:
                            mms.append((seg, self.identb[:, :], self.trib[:, :], [self.identb, self.trib]))
                        if window and dl == 512:
                            mms.append((seg, self.identb[:, :], self.twb[:, :], [self.identb, self.twb]))
                    for i, (o_, a_, b_, deps) in enumerate(mms):
                        k.mm(Sb, o_, a_, b_, deps, start=(i == 0), stop=(i == len(mms) - 1))

                def s1(Sb=Sb, pT=pT, jlo=jlo, jhi=jhi):
                    if act_bias is None:
                        k.act(pT, pT[:, jlo:jhi], Sb[:, jlo:jhi], AF.Exp, [Sb])
                    else:
                        k.act(pT, pT[:, jlo:jhi], Sb[:, jlo:jhi], AF.Exp, [Sb, self.tab], bias=act_bias)

                def s2(kb=kb, segs=segs, pT=pT, acc=acc, first=first, last=last):
                    if first:
                        k.mm(acc, acc[:, 0:260], self.zerob[:, 0:128], self.zerob[:, 0:260], [self.zerob], start=True, stop=False)
                    for s in segs:
                        k.mm(acc, acc[:, s * 65:(s + 1) * 65], pT[:, s * 128:(s + 1) * 128], vaug(kb), [pT] + vdeps,
                             start=False, stop=last)
                stages = [s0, s1, s2]
                if last:
                    def s3(c=c, acc=acc):
                        out_cb(c, acc, acc[:, 0:260].rearrange("p (s e) -> p s e", e=65))
                    stages += [s3]
                jobs.append(stages)

    def run_jobs():
        ja, jb = jobs2
        merged = []
        for i in range(max(len(ja), len(jb))):
            sa = ja[i] if i < len(ja) else []
            sb_ = jb[i] if i < len(jb) else []
            st = []
            for si in range(max(len(sa), len(sb_))):
                fa = sa[si] if si < len(sa) else None
                fb = sb_[si] if si < len(sb_) else None
                st.append(lambda fa=fa, fb=fb: ((fa() if fa else None), (fb() if fb else None)))
            merged.append(st)
        run_staged(merged)
        del ja[:]
        del jb[:]
    self.run_jobs = run_jobs

    def normalize(bank, v3, dst_tt, dst_ap, slot=0):
        rz = rzs[slot]
        k.op("dve", lambda e: e.reciprocal(out=rz[:, 0:4], in_=v3[:, :, 64]), reads=[bank], writes=[rz])
        k.tt("dve", dst_tt, dst_ap, v3[:, :, 0:64], blast(rz[:, 0:4], 64), ALU.mult, [bank, rz])

    def store_o(c, col, src_tt, src_ap):
        ost = osts[cnt["o"] % 4]
        cnt["o"] += 1
        k.cp("act", ost, ost[:, :, :], src_ap, [src_tt])
        k.dma("pool", self.o_d[c * 512:(c + 1) * 512, col:col + 64].rearrange("(s p) e -> p s e", p=128), ost[:, :, :],
              reads=[ost], writes=[self.dd["o_d"]])

    if "A" in self.mixers:
        self.push()
        kA = [load_chunk("kA%d" % i, 4 + i) for i in range(2)]
        vA = load_vaug("vA", 0, 4)
        o1s = [self.sb("o1all%d" % i, [128, 32, 64], F32) for i in range(2)]
        nlam = self.sb("nlam", [128, 1], F32)
        k.ts("dve", nlam, nlam[:, :], self.lam[:, l:l + 1], -1.0, None, ALU.mult, None, [self.lam])
        qpad = {}
        for hp in range(2):
            for m in range(2):
                for slot in range(2):
                    pb = 64 * slot
                    qpad[(hp, m, slot)] = padded("qAp%d%d%d" % (hp, m, slot), [(pb, qkTv[2 * hp + m][pb:pb + 64, :], None)])
        for hp in range(2):
            for m in range(2):
                for slot in range(2):
                    h = 2 * hp + slot
                    pb = 64 * slot
                    qt = qpad[(hp, m, slot)]
                    kt = kA[hp]

                    def cb(c, bank, v3, h=h, m=m, slot=slot):
                        o1, on, sqs, rz = o1s[slot], ons[slot], sqss[slot], rzs[slot]
                        if m == 0:
                            normalize(bank, v3, o1, o1[:, 4 * c:4 * c + 4, :], slot)
                        else:
                            normalize(bank, v3, on, on[:, :, :], slot)
                            k.op("dve", lambda e: e.scalar_tensor_tensor(
                                out=on[:, :, :].rearrange("p s e -> p (s e)"), in0=on[:, :, :].rearrange("p s e -> p (s e)"),
                                scalar=nlam[:, 0:1], in1=o1[:, 4 * c:4 * c + 4, :].rearrange("p s e -> p (s e)"),
                                op0=ALU.mult, op1=ALU.add), reads=[on, o1, nlam], writes=[on])
                            k.act(sqs, sqs[:, :, :], on[:, :, :], AF.Square, [on])
                            k.op("dve", lambda e: e.tensor_reduce(out=rz[:, 4:8], in_=sqs[:, :, :], axis=AX.X, op=ALU.add),
                                 reads=[sqs], writes=[rz])
                            k.act(rz, rz[:, 4:8], rz[:, 4:8], AF.Ln, [rz], bias=self.epsc[:, 0:1], scale=1.0 / 64)
                            k.act(rz, rz[:, 4:8], rz[:, 4:8], AF.Exp, [rz], scale=-0.5)
                            k.tt("dve", on, on[:, :, :], on[:, :, :], blast(rz[:, 4:8], 64), ALU.mult, [rz])
                            k.tt("dve", on, on[:, :, :], on[:, :, :], bmid(self.subg[:, :], 4), ALU.mult, [self.subg])
                            store_o(c, h * 64, on, on[:, :, :])
                    softmax_head(qt[:, :], [qt], kt[:, :], [kt],
                                 lambda kb, h=h: vA[:, kb, h * 65:(h + 1) * 65], [vA], h, None,
                                 False, [], cb, slot)
        run_jobs()
        self.pop()

    if "B" in self.mixers:
        self.push()
        bd = self.dd["baug"]
        qBp, kBp = [], []
        for h in range(4):
            pr, pb = h // 2, 64 * (h % 2)
            qBp.append(padded("qBp%d" % h, [(0, qkTv[6 + pr][pb:pb + 64, :], None), (64, self.baug[pr][pb:pb + 6, :], bd)]))
            kBp.append(padded("kBp%d" % h, [(0, qkTv[8 + pr][pb:pb + 64, :], None), (64, self.baug[2 + pr][pb:pb + 6, :], bd)]))
        vB = load_vaug("vB", 260, 4)
        for pair in range(2):
            for slot in range(2):
                h = 2 * pair + slot
                pb = 64 * slot

                def cb(c, bank, v3, h=h, slot=slot):
                    on = ons[slot]
                    normalize(bank, v3, on, on[:, :, :], slot)
                    store_o(c, 256 + h * 64, on, on[:, :, :])
                softmax_head(qBp[h][:, :], [qBp[h]], kBp[h][:, :], [kBp[h]],
                             lambda kb, h=h: vB[:, kb, h * 65:(h + 1) * 65], [vB], None, None, False, [], cb, slot)
            run_jobs()
        self.pop()

    if "C" in self.mixers:
        self.mixer_c(l, load_chunk)

    if "D" in self.mixers:
        self.mixer_d(l, load_chunk, load_vaug, softmax_head, normalize, tthi, ttlo, brel)
    self.pop()


Prog.phase_b = _phase_b


def run_staged(jobs):
    if not jobs:
        return
    ns = max(len(j) for j in jobs)
    for step in range(len(jobs) + ns):
        for si in range(ns - 1, -1, -1):
            j = step - si
            if 0 <= j < len(jobs) and si < len(jobs[j]):
                jobs[j][si]()


def _mixer_c(self, l, load_chunk):
    k, ps, cf = self.k, self.ps, self.cf
    self.push()
    qC = [self.padded("qCp%d" % h, [(64 * (h % 2), self.qkT[10 + h // 2][64 * (h % 2):64 * (h % 2) + 64, :], None)])
          for h in range(4)]
    kC = [load_chunk("kC%d" % i, 12 + i) for i in range(2)]
    vC = self.sb("vC", [128, 32, 256], BF16)
    k.dma("sp", vC[:, :, :], self.cvr.rearrange("(t p) c -> p t c", p=128), reads=[self.dd["cvr"]], writes=[vC])
    NB = 9
    SPb = [self.sb("cSP%d" % i, [128, 512], F32) for i in range(NB)]
    PRb = [self.sb("cPR%d" % i, [128, 512], F32) for i in range(NB)]
    Eb = [self.sb("cE%d" % i, [128, 512], F32) for i in range(NB)]
    ab = [self.sb("ca%d" % i, [128, 512], BF16) for i in range(NB)]
    aTb = [self.sb("caT%d" % i, [128, 512], BF16) for i in range(NB)]
    oc = [self.sb("coc%d" % i, [128, 256], BF16) for i in range(2)]
    zeros = self.zerob
    zbanks = [ps[0], ps[1], ps[2]]
    tbanks = [ps[3], ps[4], ps[5]]
    jobs = []
    n = 0
    for i in range(32):
        ost = oc[i % 2]
        accb = ps[6 + i % 2]
        k0 = 128 * (31 - i)
        nblk = i + 1
        chunks = [(k0 + 512 * j, min(512, 128 * nblk - 512 * j)) for j in range((nblk + 3) // 4)]
        carries = [None] * 4
        nch = len(chunks)
        for ci, (ks_, w_) in enumerate(chunks):
            for h in range(4):
                pb = 64 * (h % 2)
                qt, kt = qC[h], kC[h // 2]
                zb, tb = zbanks[n % 3], tbanks[n % 3]
                SP, PR, E, a, at = SPb[n % NB], PRb[n % NB], Eb[n % NB], ab[n % NB], aTb[n % NB]
                n += 1
                car = carries[h]
                carries[h] = (PR, PR[:, w_ - 1:w_])
                nb_ = w_ // 128

                def s0(i=i, pb=pb, qt=qt, kt=kt, zb=zb, ks_=ks_, w_=w_):
                    k.mm(zb, zb[:, 0:w_], qt[pb:pb + 64, i * 128:(i + 1) * 128], kt[pb:pb + 64, ks_:ks_ + w_], [qt, kt])

                def s1(zb=zb, E=E, w_=w_, ci=ci):
                    k.act(E, E[:, 0:w_], zb[:, 0:w_], AF.Exp, [zb])
                    if ci == 0:
                        k.tt("pool", E, E[:, 0:128], E[:, 0:128], self.cfs("M01"), ALU.mult, [cf])

                def s2(E=E, SP=SP, w_=w_):
                    k.act(SP, SP[:, 0:w_], E[:, 0:w_], AF.Ln, [E, self.epsc], bias=self.epsc[:, 1:2])

                def s3(SP=SP, PR=PR, w_=w_, car=car):
                    if car is None:
                        k.op("dve", lambda e: e.tensor_tensor_scan(
                            out=PR[:, 0:w_], data0=SP[:, 0:w_], data1=zeros[:, 0:w_], initial=0.0, op0=ALU.add, op1=ALU.add),
                            reads=[SP, zeros], writes=[PR])
                    else:
                        ctt, cap = car
                        k.op("dve", lambda e: e.tensor_tensor_scan(
                            out=PR[:, 0:w_], data0=SP[:, 0:w_], data1=zeros[:, 0:w_], initial=cap, op0=ALU.add, op1=ALU.add),
                            reads=[SP, zeros, ctt], writes=[PR])

                def s4(SP=SP, PR=PR, w_=w_):
                    k.act(SP, SP[:, 0:w_], PR[:, 0:w_], AF.Exp, [PR], scale=-1.0)

                def s5(SP=SP, E=E, a=a, w_=w_):
                    k.tt("pool", a, a[:, 0:w_], E[:, 0:w_], SP[:, 0:w_], ALU.mult, [E, SP])

                def s6(a=a, tb=tb, nb_=nb_):
                    for b in range(nb_):
                        k.mm(tb, tb[:, b * 128:(b + 1) * 128], a[:, b * 128:(b + 1) * 128], self.identb[:, :], [a, self.identb])

                def s7(at=at, tb=tb, w_=w_):
                    k.cp("dve", at, at[:, 0:w_], tb[:, 0:w_], [tb])

                def s8(at=at, accb=accb, h=h, ci=ci, ks_=ks_, nb_=nb_, nch=nch):
                    if ci == 0 and h == 0:
                        k.mm(accb, accb[:, 0:256], self.zerob[:, 0:128], self.zerob[:, 0:256], [self.zerob], start=True, stop=False)
                    for b in range(nb_):
                        kblk = ks_ // 128 + b
                        k.mm(accb, accb[:, h * 64:(h + 1) * 64], at[:, b * 128:(b + 1) * 128], vC[:, kblk, h * 64:(h + 1) * 64],
                             [at, vC], start=False, stop=(ci == nch - 1 and b == nb_ - 1 and h == 3))

                stages = [s0, s1, s2, s3, s4, s5, s6, s7, s8]
                if ci == nch - 1 and h == 3:
                    def s9(accb=accb, ost=ost, i=i):
                        k.cp("act", ost, ost[:, :], accb[:, 0:256], [accb])

                    def s10(ost=ost, i=i):
                        k.dma("pool", self.o_d[i * 128:(i + 1) * 128, 512:768], ost[:, :], reads=[ost], writes=[self.dd["o_d"]])
                    stages += [s9, s10]
                jobs.append(stages)
    run_staged(jobs)
    self.pop()


def _mixer_d(self, l, load_chunk, load_vaug, softmax_head, normalize, tthi, ttlo, brel):
    k, ps, cf, I = self.k, self.ps, self.cf, self.I
    self.push()
    qD = [load_chunk("qD%d" % i, 14 + i) for i in range(2)]
    oD = self.sb("oDacc", [128, 32, 256], F32)
    nmT = self.sb("nmT", [128, S], BF16)
    kcT, vcs = self.kcT, self.vcs
    NBc = 6
    sC = [self.sb("dsC%d" % i, [128, 256], F32) for i in range(NBc)]
    eC = [self.sb("deC%d" % i, [128, 256], F32) for i in range(NBc)]
    pbf = [self.sb("dpb%d" % i, [128, 256], BF16) for i in range(NBc)]
    pTc = [self.sb("dpT%d" % i, [128, 256], BF16) for i in range(NBc)]
    psumCs = [self.sb("dpsum%d" % i, [128, 256], F32) for i in range(3)]
    zcs = [self.sb("dzc%d" % i, [128, 8], F32) for i in range(3)]
    imps = [self.sb("dimp%d" % i, [128, 64], F32) for i in range(3)]
    imp2s = [self.sb("dimp2%d" % i, [128, 64], F32) for i in range(3)]
    nmfs = [self.sb("dnmf%d" % i, [128, 64], F32) for i in range(3)]
    m8s = [self.sb("dm8%d" % i, [128, 16], F32) for i in range(3)]
    nms = [self.sb("dnm%d" % i, [128, 128], BF16) for i in range(3)]
    dons = [self.sb("don%d" % i, [128, 4, 64], F32) for i in range(2)]
    import os
    dstage = int(os.environ.get("DSTAGE", "9"))
    jobs = []
    n = 0
    zbanks = [ps[0], ps[1], ps[2]]
    for i in range(32):
        off = 255 - 8 * i
        nnb = 1 if i < 16 else 2
        psumC, zc, imp, imp2, nmf, m8, nm = (psumCs[i % 3], zcs[i % 3], imps[i % 3], imp2s[i % 3], nmfs[i % 3],
                                             m8s[i % 3], nms[i % 3])
        ocb = [ps[4], ps[6]][i % 2]
        nmb = [ps[5], ps[7]][i % 2]
        for h in range(4):
            pb = 64 * (h % 2)
            qt = qD[h // 2]
            zb = zbanks[n % 3]
            tb = ps[3]
            s_, e_, pb_, pt_ = sC[n % NBc], eC[n % NBc], pbf[n % NBc], pTc[n % NBc]
            n += 1

            def s0(i=i, pb=pb, qt=qt, zb=zb):
                k.mm(zb, zb[:, 0:256], qt[pb:pb + 64, i * 128:(i + 1) * 128], kcT[pb:pb + 64, :], [qt, kcT])

            def s1(zb=zb, s_=s_, h=h, off=off):
                k.tt("dve", s_, s_[:, :], zb[:, 0:256], brel[:, h, off:off + 256], ALU.add, [zb, brel])

            def s2(s_=s_, e_=e_, zc=zc, h=h):
                k.act(e_, e_[:, :], s_[:, :], AF.Exp, [s_], accum=zc[:, h:h + 1], extra_w=[zc])

            def s3(e_=e_, pb_=pb_, zc=zc, psumC=psumC, h=h):
                k.ts("dve", zc, zc[:, 4 + h:5 + h], zc[:, h:h + 1], 1e-30, None, ALU.add, None, [zc])
                k.op("dve", lambda e: e.reciprocal(out=zc[:, 4 + h:5 + h], in_=zc[:, 4 + h:5 + h]), reads=[zc], writes=[zc])
                k.ts("dve", pb_, pb_[:, :], e_[:, :], zc[:, 4 + h:5 + h], None, ALU.mult, None, [e_, zc])
                if h == 0:
                    k.ts("dve", psumC, psumC[:, :], e_[:, :], zc[:, 4:5], None, ALU.mult, None, [e_, zc])
                else:
                    k.op("dve", lambda e: e.scalar_tensor_tensor(
                        out=psumC[:, :], in0=e_[:, :], scalar=zc[:, 4 + h:5 + h], in1=psumC[:, :], op0=ALU.mult, op1=ALU.add),
                        reads=[e_, zc, psumC], writes=[psumC])

            def s4(pb_=pb_, tb=tb, nnb=nnb):
                for nb in range(nnb):
                    k.mm(tb, tb[:, nb * 128:(nb + 1) * 128], pb_[:, nb * 128:(nb + 1) * 128], self.identb[:, :], [pb_, self.identb])

            def s5(pt_=pt_, tb=tb, nnb=nnb):
                k.cp("act", pt_, pt_[:, 0:nnb * 128], tb[:, 0:nnb * 128], [tb])

            def s6(pt_=pt_, ocb=ocb, h=h, nnb=nnb):
                for nb in range(nnb):
                    k.mm(ocb, ocb[:, h * 64:(h + 1) * 64], pt_[:, nb * 128:(nb + 1) * 128], vcs[:, nb, :], [pt_, vcs],
                         start=(nb == 0), stop=(nb == nnb - 1))
            stages = [s0, s1, s2, s3, s4, s5, s6]
            if h == 3:
                def s7(i=i, ocb=ocb, psumC=psumC, imp=imp):
                    k.tt("dve", oD, oD[:, i, :].rearrange("p (h e) -> p h e", e=64),
                         ocb[:, 0:256].rearrange("p (h e) -> p h e", e=64), blast(self.gates[:, i, 0:4], 64), ALU.mult,
                         [ocb, self.gates])
                    pv = psumC[:, :].rearrange("p (j r) -> p j r", r=4)
                    k.op("dve", lambda e: e.tensor_reduce(out=imp[:, :], in_=pv, axis=AX.X, op=ALU.add), reads=[psumC], writes=[imp])
                    k.tt("dve", imp, imp[:, 1:64], imp[:, 1:64], pv[:, 0:63, 3], ALU.add, [psumC])

                def s8(i=i, imp=imp, imp2=imp2, m8=m8):
                    st = 64 - 2 * i
                    k.tt("dve", imp2, imp2[:, :], imp[:, :], self.cfs("M1", st, st + 64), ALU.mult, [imp, cf])
                    k.tt("dve", imp2, imp2[:, :], imp2[:, :], self.cfs("CC", st, st + 64), ALU.add, [cf])
                    k.op("dve", lambda e: e.memset(imp2[:, 0:1], 1.0e4), writes=[imp2])
                    k.op("dve", lambda e: e.max(out=m8[:, 0:8], in_=imp2[:, :]), reads=[imp2], writes=[m8])

                def s9(imp=imp, imp2=imp2, m8=m8, nmf=nmf, nm=nm):
                    k.op("dve", lambda e: e.match_replace(out=imp[:, :], in_to_replace=m8[:, 0:8], in_values=imp2[:, :],
                                                          imm_value=-2.0), reads=[m8, imp2], writes=[imp])
                    k.op("dve", lambda e: e.max(out=m8[:, 8:16], in_=imp[:, :]), reads=[imp], writes=[m8])
                    k.ts("dve", nmf, nmf[:, :], imp2[:, :], m8[:, 15:16], BIG, ALU.is_ge, ALU.mult, [imp2, m8])
                    k.ts("dve", nm, nm[:, 0:64], nmf[:, :], -BIG, None, ALU.add, None, [nmf])
                    k.ts("dve", nm, nm[:, 64:128], nmf[:, :], -BIG, None, ALU.add, None, [nmf])

                def s10(nm=nm, nmb=nmb):
                    k.mm(nmb, nmb[:, 0:128], nm[:, :], self.identb[:, :], [nm, self.identb])

                def s11(i=i, nmb=nmb):
                    k.cp("act", nmT, nmT[:, i * 128:(i + 1) * 128], nmb[:, 0:128], [nmb])
                stages += [s7, s8, s9, s10, s11]
            jobs.append(stages)
    run_staged(jobs)
    k.dma("pool", self.baug[0, 0:64, :], nmT[0:64, :], reads=[nmT], writes=[self.dd["baug"]])
    for br, (ch, c0, gcol) in enumerate([(16, 520, 4), (17, 585, 8)]):
        if dstage < 4 + br:
            continue
        self.push()
        if br == 0:
            kT = self.sb("ksx", [128, S], BF16)
            self.load_rows(kT, 0, self.qkT[16][0:64, :])
            k.dma("pool", kT[64:128, :], I["cex"][:, :], writes=[kT])
        else:
            kT = load_chunk("dk%d" % br, ch)
        vv = load_vaug("dv%d" % br, c0, 1)
        qq4 = []
        for h in range(4):
            pair, pb = h // 2, 64 * (h % 2)
            if br == 0:
                qq4.append(self.padded("qsel%d" % h, [(0, self.qkT[14 + pair][pb:pb + 64, :], None),
                                                     (64, self.baug[0][0:64, :], self.dd["baug"])]))
            else:
                qq4.append(self.padded("qwin%d" % h, [(pb, self.qkT[14 + pair][pb:pb + 64, :], None)]))
        for pair in range(2):
            for slot in range(2):
                h = 2 * pair + slot

                def cb(c, bank, v3, h=h, gcol=gcol, slot=slot):
                    on = dons[slot]
                    normalize(bank, v3, on, on[:, :, :], slot)
                    k.tt("dve", on, on[:, :, :], on[:, :, :], blast(self.gates[:, 4 * c:4 * c + 4, gcol + h], 64), ALU.mult,
                         [self.gates])
                    dst = oD[:, 4 * c:4 * c + 4, h * 64:(h + 1) * 64]
                    k.tt("dve", oD, dst, dst, on[:, :, :], ALU.add, [on])
                qq = qq4[h]
                softmax_head(qq[:, :], [qq], kT[:, :], [kT],
                             lambda kb, vv=vv: vv[:, kb, 0:65], [vv], 4 + h, None, br == 1, [], cb, slot)
        self.run_jobs()
        self.pop()
    ob = [self.sb("dob%d" % i, [128, 256], BF16) for i in range(2)]
    for t in range(32):
        o_ = ob[t % 2]
        k.cp("act", o_, o_[:, :], oD[:, t, :], [oD])
        k.dma("pool", self.o_d[t * 128:(t + 1) * 128, 768:1024], o_[:, :], reads=[o_], writes=[self.dd["o_d"]])
    self.pop()


Prog.mixer_c = _mixer_c
Prog.mixer_d = _mixer_d


def _phase_c(self, l, xsrc, xsd, xdst, xdd):
    k, I, cf, ps = self.k, self.I, self.cf, self.ps
    ident = self.cfs("ident")
    GT = 256
    NG = S // GT
    self.push()
    wout = self.sb("wout", [128, 8, D], BF16)
    wup = self.sb("wup", [128, 8, 2 * DFF], BF16)
    wdn = self.sb("wdn", [128, NFC, D], BF16)
    src_up = I["ffn_w_up"][l].rearrange("(c p) n -> p c n", p=128)
    for c in range(8):
        for hh in range(2):
            k.dma("pool", wup[:, c, hh * DFF:(hh + 1) * DFF], src_up[:, c, hh * DFF:(hh + 1) * DFF], writes=[wup])
    cst = self.sb("cst", [88, 128], F32)
    k.dma("sp", cst[0:66, :], I["ffn_conv_w"][l], writes=[cst])
    k.dma("sp", cst[66:88, :], I["ffn_conv_b"][l], writes=[cst])
    self.tr(ps[0], ps[0][:, 0:88], cst[:, :], ident[0:88, 0:88], [cst, cf])
    cw = self.sb("cw", [128, 88], F32)
    k.cp("dve", cw, cw[:, :], ps[0][:, 0:88], [ps[0]])
    ab = self.sb("abf", [128, 16], F32)
    k.ts("dve", ab, ab[:, 0:8], self.modcol(l, 4), 1.0, None, ALU.add, None, [self.modT])
    k.tt("dve", ab, ab[:, 0:8], ab[:, 0:8], self.colv[:, 112 + 8 * l:120 + 8 * l], ALU.mult, [self.colv])
    k.cp("dve", ab, ab[:, 8:16], self.modcol(l, 3), [self.modT])
    self.push()
    gbc = [self.sb("gbc%d" % i, [128, D], F32) for i in range(2)]
    dg = self.sb("dgt", [128, 128], F32)
    for gi, which in enumerate([2, 5]):
        for c in range(8):
            k.ts("dve", dg, dg[:, :], ident, self.modcol(l, which)[:, c:c + 1], None, ALU.mult, None, [cf, self.modT])
            b = ps[1 + (c // 4)]
            k.mm(b, b[:, (c % 4) * 128:(c % 4 + 1) * 128], self.cfs("ones"), dg[:, :], [dg, cf])
        for hh in range(2):
            k.cp("dve", gbc[gi], gbc[gi][:, hh * 512:(hh + 1) * 512], ps[1 + hh][:, :], [ps[1 + hh]])
    stg = [self.sb("wstg%d" % i, [128, D], F32) for i in range(2)]
    src_o = I["w_out"][l].rearrange("(c p) n -> p c n", p=128)
    src_d = I["ffn_w_down"][l].rearrange("(c p) n -> p c n", p=128)
    n = 0
    for c in range(8):
        st_ = stg[n % 2]
        k.dma("sp", st_[:, :], src_o[:, c, :], writes=[st_])
        k.tt("dve" if n % 2 == 0 else "pool", wout, wout[:, c, :], st_[:, :], gbc[0][:, :], ALU.mult, [st_, gbc[0]])
        n += 1
    for c in range(NFC):
        st_ = stg[n % 2]
        k.dma("sp", st_[:, :], src_d[:, c, :], writes=[st_])
        k.tt("dve" if n % 2 == 0 else "pool", wdn, wdn[:, c, :], st_[:, :], gbc[1][:, :], ALU.mult, [st_, gbc[1]])
        n += 1
    self.pop()
    ots = [self.sb("ot%d" % i, [128, D], BF16) for i in range(2)]
    oT = self.sb("oTc", [128, 8, 128], BF16)
    xms = [self.sb("xm%d" % i, [128, D], F32) for i in range(3)]
    xn = self.sb("xnc", [128, D], F32)
    sss = [self.sb("ssc%d" % i, [128, 4], F32) for i in range(2)]
    h2T = self.sb("h2T", [128, 8, GT], BF16)
    gsb = [self.sb("gsb%d" % i, [128, GT + 2], F32) for i in range(2)]
    accs = [self.sb("cacc%d" % i, [128, GT], F32) for i in range(2)]
    sgs = [self.sb("csg%d" % i, [128, GT], F32) for i in range(2)]
    aT = self.sb("aTc", [128, NFC, GT], BF16)
    halo = self.sb("halo", [128, NFC, 2], F32)
    k.op("dve", lambda e: e.memset(halo[:, :, :], 0.0), writes=[halo])
    nt = 0
    for g in range(NG):
        for tt_ in range(GT // 128):
            t = g * (GT // 128) + tt_
            ot = ots[t % 2]
            xm = xms[t % 3]
            ss = sss[t % 2]
            k.dma("sp", ot[:, :], self.o_d[t * 128:(t + 1) * 128, :], reads=[self.dd["o_d"]], writes=[ot])
            k.dma("act", xm[:, :], xsrc[t * 128:(t + 1) * 128, :], reads=[xsd] if xsd else [], writes=[xm])
            for c in range(8):
                b = ps[c // 4]
                k.mm(b, b[:, (c % 4) * 128:(c % 4 + 1) * 128], ot[:, c * 128:(c + 1) * 128], self.identb[:, :], [ot, self.identb])
            for hh in range(2):
                k.cp("act" if hh == 0 else "dve", oT, oT[:, hh * 4:(hh + 1) * 4, :].rearrange("p c t -> p (c t)"), ps[hh][:, :], [ps[hh]])
            for hh in range(2):
                b = ps[2 + hh]
                for c in range(8):
                    k.mm(b, b[:, :], oT[:, c, :], wout[:, c, hh * 512:(hh + 1) * 512], [oT, wout], start=(c == 0), stop=(c == 7))
                k.tt("dve", xm, xm[:, hh * 512:(hh + 1) * 512], b[:, :], xm[:, hh * 512:(hh + 1) * 512], ALU.add, [b])
            if self.dbg:
                k.dma("pool", self.xmid[t * 128:(t + 1) * 128, :], xm[:, :], reads=[xm], writes=[self.dd["xmid"]])
            k.act(xn, xn[:, :], xm[:, :], AF.Square, [xm], accum=ss[:, 0:1], extra_w=[ss])
            k.act(ss, ss[:, 1:2], ss[:, 0:1], AF.Ln, [ss], bias=self.epsc[:, 0:1], scale=1.0 / D)
            k.act(ss, ss[:, 2:3], ss[:, 1:2], AF.Exp, [ss], scale=-0.5)
            k.ts("dve", xn, xn[:, :], xm[:, :], ss[:, 2:3], None, ALU.mult, None, [xm, ss])
            for c in range(8):
                b = ps[4 + c // 4]
                self.tr(b, b[:, (c % 4) * 128:(c % 4 + 1) * 128], xn[:, c * 128:(c + 1) * 128], ident, [xn, cf])
            for c in range(8):
                b = ps[4 + c // 4]
                pin = b[:, (c % 4) * 128:(c % 4 + 1) * 128]
                dst = h2T[:, c, tt_ * 128:(tt_ + 1) * 128]
                if c % 2 == 0:
                    k.ts("dve", h2T, dst, pin, ab[:, c:c + 1], ab[:, 8 + c:9 + c], ALU.mult, ALU.add, [b, ab])
                else:
                    k.act(h2T, dst, pin, AF.Identity, [b, ab], bias=ab[:, 8 + c:9 + c], scale=ab[:, c:c + 1])
        for fc in range(NFC):
            bg = ps[(2 * fc) % 4]
            bv = ps[(2 * fc + 1) % 4]
            for c in range(8):
                k.mm(bg, bg[:, 0:GT], wup[:, c, fc * 128:(fc + 1) * 128], h2T[:, c, :], [wup, h2T], start=(c == 0), stop=(c == 7))
            for c in range(8):
                k.mm(bv, bv[:, 0:GT], wup[:, c, DFF + fc * 128:DFF + (fc + 1) * 128], h2T[:, c, :], [wup, h2T],
                     start=(c == 0), stop=(c == 7))
            gs_, ac_, sg_ = gsb[fc % 2], accs[fc % 2], sgs[fc % 2]
            k.cp("act", gs_, gs_[:, 0:2], halo[:, fc, :], [halo])
            k.cp("act", gs_, gs_[:, 2:GT + 2], bg[:, 0:GT], [bg])
            k.cp("act", halo, halo[:, fc, :], gs_[:, GT:GT + 2], [gs_])
            k.ts("dve", ac_, ac_[:, :], gs_[:, 2:GT + 2], cw[:, 44 + fc:45 + fc], cw[:, 66 + fc:67 + fc], ALU.mult, ALU.add,
                 [gs_, cw])
            k.op("dve", lambda e, ac_=ac_, gs_=gs_, fc=fc: e.scalar_tensor_tensor(
                out=ac_[:, :], in0=gs_[:, 1:GT + 1], scalar=cw[:, 22 + fc:23 + fc], in1=ac_[:, :], op0=ALU.mult, op1=ALU.add),
                reads=[gs_, cw, ac_], writes=[ac_])
            k.op("dve", lambda e, ac_=ac_, gs_=gs_, fc=fc: e.scalar_tensor_tensor(
                out=ac_[:, :], in0=gs_[:, 0:GT], scalar=cw[:, fc:fc + 1], in1=ac_[:, :], op0=ALU.mult, op1=ALU.add),
                reads=[gs_, cw, ac_], writes=[ac_])
            k.act(sg_, sg_[:, :], ac_[:, :], AF.Silu, [ac_])
            k.tt("dve", aT, aT[:, fc, :], sg_[:, :], bv[:, 0:GT], ALU.mult, [sg_, bv])
        for tt_ in range(GT // 128):
            t = g * (GT // 128) + tt_
            xm = xms[t % 3]
            for hh in range(2):
                b = ps[4 + hh]
                for fc in range(NFC):
                    k.mm(b, b[:, :], aT[:, fc, tt_ * 128:(tt_ + 1) * 128], wdn[:, fc, hh * 512:(hh + 1) * 512], [aT, wdn],
                         start=(fc == 0), stop=(fc == NFC - 1))
                k.tt("dve", xm, xm[:, hh * 512:(hh + 1) * 512], b[:, :], xm[:, hh * 512:(hh + 1) * 512], ALU.add, [b])
            k.dma("pool", xdst[t * 128:(t + 1) * 128, :], xm[:, :], reads=[xm], writes=[xdd])
    self.pop()


Prog.phase_c = _phase_c


_PROG = None


def kernel(**inputs):
    global _PROG
    if _PROG is None:
        _PROG = Prog()
    in_maps = [make_inputs(inputs, r % 4) for r in range(4)]
    in_maps = in_maps + in_maps
    res = run_bass_kernel_spmd(_PROG.nc, in_maps, core_ids=list(range(8)))
    out = np.stack([np.asarray(res.results[b]["y"], dtype=np.float32) for b in range(4)], axis=0)
    return out
```
